# Optimizing a Trainium2 kernel written in Bass

```python
import math
import jax, jax.numpy as jnp
from jax import lax
import numpy as np

D_MODEL = 1024
BATCH = 16
SEQ = 2048
DEPTH = 2

HEAD_DIM = 64
H_A = D_MODEL // HEAD_DIM
KV_A = 4
G_A = H_A // KV_A
WINDOW_A = 128
BLOCK_A = 128
H_B = D_MODEL // HEAD_DIM
DILATED_GROUPS = ((128, 1), (512, 4), (2048, 16))
N_GROUPS_B = len(DILATED_GROUPS)
BLOCK_B = 64
D_FF = -(-8 * D_MODEL // (3 * 256)) * 256
RMS_EPS = 1e-6
NEG = -1e30

kernel_name = "hybrid_window_gqa_dilated_attn_encoder"


def rmsnorm(x, g):
    x32 = x.astype(jnp.float32)
    y = x32 * lax.rsqrt(jnp.mean(x32 * x32, axis=-1, keepdims=True) + RMS_EPS)
    return (y * g.astype(jnp.float32)).astype(x.dtype)


def alibi_slopes(n):
    return 2.0 ** (-8.0 * jnp.arange(1, n + 1, dtype=jnp.float32) / n)


def banded_attention(q, k, v, slopes, dist_unit, window, block):
    n, L, hk, g, dh = q.shape
    c = window // block
    nb = -(-L // block)
    lp = nb * block
    kw = (2 * c + 1) * block
    qb = jnp.pad(q, ((0, 0), (0, lp - L), (0, 0), (0, 0), (0, 0))).reshape(n, nb, block, hk, g, dh)
    pad_k = ((0, 0), (window, window + lp - L), (0, 0), (0, 0))
    kb = jnp.pad(k, pad_k).reshape(n, nb + 2 * c, block, hk, dh)
    vb = jnp.pad(v, pad_k).reshape(n, nb + 2 * c, block, hk, dh)
    kwin = jnp.concatenate([kb[:, j:j + nb] for j in range(2 * c + 1)], axis=2)
    vwin = jnp.concatenate([vb[:, j:j + nb] for j in range(2 * c + 1)], axis=2)
    scores = jnp.einsum('nbqhgd,nbkhd->nbhgqk', qb, kwin,
                        preferred_element_type=jnp.float32) * (dh ** -0.5)
    rel = jnp.arange(kw)[None, :] - window - jnp.arange(block)[:, None]
    key_pos = jnp.arange(nb)[:, None] * block - window + jnp.arange(kw)[None, :]
    valid = (jnp.abs(rel)[None] <= window) & ((key_pos >= 0) & (key_pos < L))[:, None, :]
    dist = jnp.abs(rel).astype(jnp.float32) * dist_unit
    scores = scores - slopes.astype(jnp.float32)[:, :, None, None] * dist
    scores = jnp.where(valid[None, :, None, None], scores, NEG)
    m = jnp.max(scores, axis=-1)
    p = jnp.exp(scores - m[..., None])
    den = jnp.sum(p, axis=-1)
    lse = m + jnp.log(den)
    out = jnp.einsum('nbhgqk,nbkhd->nbqhgd', p.astype(v.dtype), vwin,
                     preferred_element_type=jnp.float32)
    out = out / jnp.moveaxis(den, -1, 2)[..., None]
    out = out.reshape(n, lp, hk, g, dh)[:, :L]
    lse = jnp.moveaxis(lse, -1, 2).reshape(n, lp, hk, g)[:, :L]
    return out.astype(q.dtype), lse


def window_gqa_sink(h, w_qkv, w_out, sink):
    b, s, _ = h.shape
    qkv = h @ w_qkv
    q = qkv[..., :H_A * HEAD_DIM].reshape(b, s, KV_A, G_A, HEAD_DIM)
    k = qkv[..., H_A * HEAD_DIM:(H_A + KV_A) * HEAD_DIM].reshape(b, s, KV_A, HEAD_DIM)
    v = qkv[..., (H_A + KV_A) * HEAD_DIM:].reshape(b, s, KV_A, HEAD_DIM)
    slopes = alibi_slopes(H_A).reshape(KV_A, G_A)
    o, lse = banded_attention(q, k, v, slopes, 1, WINDOW_A, BLOCK_A)
    o = o * jax.nn.sigmoid(lse - sink.astype(jnp.float32).reshape(KV_A, G_A))[..., None]
    return o.astype(h.dtype).reshape(b, s, H_A * HEAD_DIM) @ w_out


def to_residue(t, dil):
    b, s, hh, dh = t.shape
    return t.reshape(b, s // dil, dil, hh, dh).transpose(0, 2, 1, 3, 4).reshape(b * dil, s // dil, hh, dh)


def from_residue(t, b, dil):
    rest = t.shape[2:]
    L = t.shape[1]
    t = t.reshape((b, dil, L) + rest)
    t = jnp.swapaxes(t, 1, 2)
    return t.reshape((b, L * dil) + rest)


def dilated_mixture_attention(h, w_qkv, w_out):
    b, s, _ = h.shape
    qkv = (h @ w_qkv).reshape(b, s, N_GROUPS_B, 3, H_B, HEAD_DIM)
    slopes = alibi_slopes(H_B)[:, None]
    outs, lses = [], []
    for gi, (win, dil) in enumerate(DILATED_GROUPS):
        q = to_residue(qkv[:, :, gi, 0], dil)[:, :, :, None]
        k = to_residue(qkv[:, :, gi, 1], dil)
        v = to_residue(qkv[:, :, gi, 2], dil)
        o, lse = banded_attention(q, k, v, slopes, dil, win // (2 * dil), BLOCK_B)
        outs.append(from_residue(o[:, :, :, 0], b, dil))
        lses.append(from_residue(lse[:, :, :, 0], b, dil))
    wts = jax.nn.softmax(jnp.stack(lses), axis=0)
    o = jnp.einsum('gbsh,gbshd->bshd', wts, jnp.stack(outs).astype(jnp.float32))
    return o.astype(h.dtype).reshape(b, s, H_B * HEAD_DIM) @ w_out


def swiglu(h, w_gate, w_up, w_down):
    return (jax.nn.silu(h @ w_gate) * (h @ w_up)) @ w_down


def setup_inputs(seed: int = 0) -> dict:
    key = jax.random.key(seed)
    ks = jax.random.split(key, 14)
    n_a = (DEPTH + 1) // 2
    n_b = DEPTH // 2
    qkv_a = (H_A + 2 * KV_A) * HEAD_DIM
    qkv_b = N_GROUPS_B * 3 * H_B * HEAD_DIM
    f32 = jnp.float32

    def w(k, shape, fan_in):
        return jax.random.normal(k, shape, f32) * fan_in ** -0.5

    return {
        "x": jax.random.normal(ks[0], (BATCH, SEQ, D_MODEL), f32),
        "norm_mix": 1.0 + 0.02 * jax.random.normal(ks[1], (DEPTH, D_MODEL), f32),
        "norm_ffn": 1.0 + 0.02 * jax.random.normal(ks[2], (DEPTH, D_MODEL), f32),
        "w_qkv_a": w(ks[3], (n_a, D_MODEL, qkv_a), D_MODEL),
        "w_out_a": w(ks[4], (n_a, H_A * HEAD_DIM, D_MODEL), H_A * HEAD_DIM),
        "sink_a": 0.5 * jax.random.normal(ks[5], (n_a, H_A), f32),
        "w_qkv_b": w(ks[6], (n_b, D_MODEL, qkv_b), D_MODEL),
        "w_out_b": w(ks[7], (n_b, H_B * HEAD_DIM, D_MODEL), H_B * HEAD_DIM),
        "w_gate": w(ks[8], (DEPTH, D_MODEL, D_FF), D_MODEL),
        "w_up": w(ks[9], (DEPTH, D_MODEL, D_FF), D_MODEL),
        "w_down": w(ks[10], (DEPTH, D_FF, D_MODEL), D_FF),
        "norm_final": 1.0 + 0.02 * jax.random.normal(ks[11], (D_MODEL,), f32),
    }


def reference(x, norm_mix, norm_ffn, w_qkv_a, w_out_a, sink_a, w_qkv_b, w_out_b,
              w_gate, w_up, w_down, norm_final):
    for i in range(DEPTH):
        h = rmsnorm(x, norm_mix[i])
        j = i // 2
        if i % 2 == 0:
            x = x + window_gqa_sink(h, w_qkv_a[j], w_out_a[j], sink_a[j])
        else:
            x = x + dilated_mixture_attention(h, w_qkv_b[j], w_out_b[j])
        h = rmsnorm(x, norm_ffn[i])
        x = x + swiglu(h, w_gate[i], w_up[i], w_down[i])
    return rmsnorm(x, norm_final)
```

```python
import numpy as np
import concourse.bass as bass
import concourse.mybir as mybir
from concourse.bass_utils import run_bass_kernel_spmd

F32 = mybir.dt.float32
BF16 = mybir.dt.bfloat16
I32 = mybir.dt.int32
AF = mybir.ActivationFunctionType
ALU = mybir.AluOpType

T = 2048
KC = 8
NF = 22
NSLOT = 12
FSPL = [(0, 8), (8, 15), (15, 22)]
BIG = 30000.0
EPS = 1e-6
N_CORES = 8
SEQ_PER_CORE = 2
GROUPS = ((128, 1), (512, 4), (2048, 16))


def _unit_lists(layers=(0, 1)):
    ring = []
    if 0 in layers:
        for c in range(8):
            ring.append(("qa", c))
        for p in range(2):
            ring.append(("ka", p))
        for v in range(2):
            ring.append(("va", v))
        for kc in range(8):
            ring.append(("oa", kc))
        for (f0, f1) in FSPL:
            for f in range(f0, f1):
                ring.append(("g", 0, f))
                ring.append(("u", 0, f))
    if 1 in layers:
        items = [(j, g) for j in range(8) for g in range(3)]
        for s in range(3):
            ring.append(("qkvb", 0, 0, s))
        for i, (j, g) in enumerate(items):
            if g == 1 and j >= 2 and j % 2 == 0:
                ring.append(("ob", j - 2))
                ring.append(("ob", j - 1))
            if i + 1 < len(items):
                for s in range(3):
                    ring.append(("qkvb", items[i + 1][0], items[i + 1][1], s))
        ring.append(("ob", 6))
        ring.append(("ob", 7))
        for (f0, f1) in FSPL:
            for f in range(f0, f1):
                ring.append(("g", 1, f))
                ring.append(("u", 1, f))
    wd = []
    for l in range(2):
        for f in range(NF):
            wd.append(("d", l, f))
    return ring, wd


def _pack_weights(w_qkv_a, w_out_a, w_qkv_b, w_out_b, w_gate, w_up, w_down, layers=(0, 1)):
    ring, wd = _unit_lists(layers)
    keys = ring + wd
    out = np.empty((len(keys), 128, 1024), np.float32)

    def proj_unit(W, cols):
        return W[:, cols].reshape(8, 128, 128).transpose(1, 0, 2).reshape(128, 1024)

    qa_rows = {}
    for i, k in enumerate(keys):
        kind = k[0]
        if kind == "qa":
            c = k[1]
            p, a = c // 4, c % 4
            h0, h1 = 8 * p + a, 8 * p + 4 + a
            cols = np.concatenate([np.arange(64 * h0, 64 * h0 + 64), np.arange(64 * h1, 64 * h1 + 64)])
            out[i] = proj_unit(w_qkv_a[0], cols)
        elif kind == "ka":
            p = k[1]
            out[i] = proj_unit(w_qkv_a[0], np.arange(1024 + 128 * p, 1024 + 128 * p + 128))
        elif kind == "va":
            v = k[1]
            out[i] = proj_unit(w_qkv_a[0], np.arange(1280 + 128 * v, 1280 + 128 * v + 128))
        elif kind == "oa":
            c = k[1]
            p, a = c // 4, c % 4
            h0, h1 = 8 * p + a, 8 * p + 4 + a
            rows = np.concatenate([np.arange(64 * h0, 64 * h0 + 64), np.arange(64 * h1, 64 * h1 + 64)])
            out[i] = w_out_a[0][rows, :]
        elif kind == "g":
            _, l, f = k
            out[i] = proj_unit(w_gate[l], np.arange(128 * f, 128 * f + 128))
        elif kind == "u":
            _, l, f = k
            out[i] = proj_unit(w_up[l], np.arange(128 * f, 128 * f + 128))
        elif kind == "qkvb":
            _, j, g, s = k
            base = (3 * g + s) * 1024 + 128 * j
            out[i] = proj_unit(w_qkv_b[0], np.arange(base, base + 128))
        elif kind == "ob":
            kc = k[1]
            out[i] = w_out_b[0][128 * kc:128 * kc + 128, :]
        elif kind == "d":
            _, l, f = k
            out[i] = w_down[l][128 * f:128 * f + 128, :]
        else:
            raise AssertionError(kind)
    return out


class Sem:
    def __init__(self, h):
        self.h = h
        self.cnt = 0


class Tok:
    __slots__ = ("sem", "val")

    def __init__(self, sem, val):
        self.sem = sem
        self.val = val


class Res:
    __slots__ = ("w", "r")

    def __init__(self, fence=None):
        self.w = None
        self.r = list(fence) if fence else []


class Eng:
    def __init__(self, e, sem, is_pe=False):
        self.e = e
        self.sem = sem
        self.is_pe = is_pe
        self.seen = {}
        self.pend = None
        self.last = None

    def wait(self, tok):
        assert tok.val is not None, "unresolved token"
        if self.seen.get(tok.sem, 0) >= tok.val:
            return
        self.e.wait_ge(tok.sem.h, tok.val)
        self.seen[tok.sem] = tok.val


def _add_reader(res, tok):
    if res.r and res.r[-1] is tok:
        return
    if res.r:
        l = res.r[-1]
        if l.sem is tok.sem and l.val is not None and tok.val is not None and tok.val >= l.val:
            res.r[-1] = tok
            return
    res.r.append(tok)


def op(eng, fn, reads=(), writes=(), sig=True):
    deps = []
    for r in reads:
        if r.w is not None:
            deps.append(r.w)
    for w in writes:
        if w.w is not None:
            deps.append(w.w)
        deps.extend(w.r)
    for t in deps:
        if eng.is_pe and t.sem is eng.sem:
            continue
        eng.wait(t)
    ins = fn()
    if sig:
        ins.then_inc(eng.sem.h, 1)
        eng.sem.cnt += 1
        tok = Tok(eng.sem, eng.sem.cnt)
        if eng.pend is not None:
            eng.pend.val = tok.val
            eng.pend = None
    else:
        if eng.pend is None:
            eng.pend = Tok(eng.sem, None)
        tok = eng.pend
    eng.last = tok
    for r in reads:
        _add_reader(r, tok)
    for w in writes:
        w.w = tok
        w.r = []
    return tok


def dma(eng, dsem, out, in_, reads=(), writes=()):
    deps = []
    for r in reads:
        if r.w is not None:
            deps.append(r.w)
    for w in writes:
        if w.w is not None:
            deps.append(w.w)
        deps.extend(w.r)
    for t in deps:
        eng.wait(t)
    eng.e.dma_start(out=out, in_=in_).then_inc(dsem.h, 16)
    dsem.cnt += 16
    tok = Tok(dsem, dsem.cnt)
    for r in reads:
        _add_reader(r, tok)
    for w in writes:
        w.w = tok
        w.r = []
    return tok


def build_program(n_seq=SEQ_PER_CORE, layers=(0, 1), dbg=None):
    dbg = dbg or {}
    ring_keys, wd_keys = _unit_lists(layers)
    n_ring = len(ring_keys)
    n_units = n_ring + len(wd_keys)
    wd_index = {k: n_ring + i for i, k in enumerate(wd_keys)}

    nc = bass.Bass("TRN2", target_bir_lowering=False)
    xin = nc.dram_tensor("xin", [n_seq, 128, 8 * T], F32, kind="ExternalInput").ap()
    wts = nc.dram_tensor("wts", [n_units, 128, 1024], F32, kind="ExternalInput").ap()
    gains_d = nc.dram_tensor("gains", [128, 40], F32, kind="ExternalInput").ap()
    sink_d = nc.dram_tensor("sink", [128, 8], F32, kind="ExternalInput").ap()
    yout = nc.dram_tensor("yout", [n_seq, 128, 8 * T], F32, kind="ExternalOutput").ap()

    from contextlib import ExitStack
    with ExitStack() as es:
        def sb(name, shape, dt):
            return es.enter_context(nc.sbuf_tensor(name, shape, dt))

        def mksem(name):
            return Sem(es.enter_context(nc.semaphore(name)))

        xT = sb("xT", [128, 8, T], F32)
        A = sb("A", [128, 8, T], BF16)
        SCR = sb("SCR", [128, 30720], BF16)
        SWAP = sb("SWAP", [128, 128], F32)
        RD = sb("RD", [128, 512], F32)
        RDXY = sb("RDXY", [128, 2, 2, 256], BF16)
        RING = sb("RING", [128, NSLOT, 1024], BF16)
        IDT = sb("IDT", [128, 24, 128], BF16)
        ONES = sb("ONES", [128, 128], BF16)
        D1 = sb("D1", [128, 256], BF16)
        D0 = sb("D0", [128, 3, 128], BF16)
        GAINS = sb("GAINS", [128, 40], F32)
        ESINK = sb("ESINK", [128, 8], F32)
        EPST = sb("EPST", [128, 1], F32)
        SQ = sb("SQ", [128, 2, 512], BF16)
        RSTD = sb("RSTD", [128, 2, 512], F32)
        PT = sb("PT", [128, 3, 512], BF16)
        SG = sb("SG", [128, 2, 512], BF16)
        PS = [es.enter_context(nc.psum_tensor(f"ps{i}", [128, 512], F32)) for i in range(8)]

        PE = Eng(nc.tensor, mksem("s_pe"), is_pe=True)
        ACT = Eng(nc.scalar, mksem("s_act"))
        DVE = Eng(nc.vector, mksem("s_dve"))
        POOL = Eng(nc.gpsimd, mksem("s_pool"))
        SP = Eng(nc.sync, mksem("s_sp"))
        ENGS = [PE, ACT, DVE, POOL, SP]
        slot_sem = [mksem(f"s_slot{i}") for i in range(NSLOT)]
        wd_sem = [mksem(f"s_wd{i}") for i in range(8)]
        x_sem = [mksem(f"s_x{i}") for i in range(8)]
        y_sem = [mksem(f"s_y{i}") for i in range(2)]
        c_sem = mksem("s_const")

        ps_res = [Res() for _ in range(8)]
        slot_res = [Res() for _ in range(NSLOT)]
        xT_res = [[Res() for _ in range(4)] for _ in range(8)]
        A_res = [[Res() for _ in range(4)] for _ in range(8)]
        sq_res = [Res() for _ in range(2)]
        rstd_res = [Res() for _ in range(2)]
        pt_res = [Res() for _ in range(3)]
        sg_res = [Res() for _ in range(2)]
        rd_res = Res()
        rrh_res = rd_res
        cnt = {"sq": 0, "rstd": 0, "pt": 0, "sg": 0, "y": 0}

        def fence():
            toks = []
            for e in ENGS:
                if e.last is not None:
                    assert e.last.val is not None
                    toks.append(e.last)
            for s_ in wd_sem + y_sem:
                if s_.cnt > 0:
                    toks.append(Tok(s_, s_.cnt))
            return toks

        def scr_f32(off_bf, n_f32):
            return SCR[:, off_bf:off_bf + 2 * n_f32].bitcast(F32)

        cres = Res()
        tok_g = dma(SP, c_sem, GAINS[:], gains_d[:, :], writes=[cres])
        tok_s = dma(SP, c_sem, ESINK[:], sink_d[:, :], writes=[cres])
        tok_s = Tok(c_sem, c_sem.cnt)
        ACT.wait(tok_s)
        es_res = Res()
        op(ACT, lambda: nc.scalar.activation(out=ESINK[:], in_=ESINK[:], func=AF.Exp), writes=[es_res])
        k_res = Res()
        op(DVE, lambda: nc.vector.memset(ONES[:], 1.0), writes=[k_res])
        op(DVE, lambda: nc.vector.memset(EPST[:], EPS), writes=[k_res])
        io_id = SCR[:, 0:256].bitcast(I32)
        io_d1 = SCR[:, 256:768].bitcast(I32)
        io_d0 = SCR[:, 768:1536].bitcast(I32)
        f_a = scr_f32(1536, 384)
        f_b = scr_f32(2304, 384)
        f_c = scr_f32(3072, 384)
        io_res = Res()
        op(POOL, lambda: nc.gpsimd.iota(io_id, pattern=[[1, 128]], base=0, channel_multiplier=-1), writes=[io_res])
        op(POOL, lambda: nc.gpsimd.iota(io_d1, pattern=[[-1, 256]], base=64, channel_multiplier=1), writes=[io_res])
        op(POOL, lambda: nc.gpsimd.iota(io_d0.rearrange("p (a b) -> p a b", a=3), pattern=[[128, 3], [-1, 128]],
                                        base=-128, channel_multiplier=1), writes=[io_res])
        tmp_res = Res()
        op(DVE, lambda: nc.vector.tensor_scalar(out=f_a[:, 0:128], in0=io_id, scalar1=0.0, scalar2=None,
                                                op0=ALU.is_equal), reads=[io_res], writes=[tmp_res])
        for k in range(24):
            m = k - 7
            val = 8.0 * (2.0 ** (-m / 2.0))
            op(DVE, lambda k=k, val=val: nc.vector.tensor_scalar(out=IDT[:, k, :], in0=f_a[:, 0:128], scalar1=val,
                                                                 scalar2=None, op0=ALU.mult),
               reads=[tmp_res], writes=[k_res])

        op(DVE, lambda: nc.vector.tensor_scalar(out=f_b[:, 0:128], in0=io_id, scalar1=64.0, scalar2=None,
                                                op0=ALU.is_equal), reads=[io_res], writes=[tmp_res])
        op(DVE, lambda: nc.vector.tensor_scalar(out=f_c[:, 0:128], in0=io_id, scalar1=-64.0, scalar2=None,
                                                op0=ALU.is_equal), reads=[io_res], writes=[tmp_res])
        op(DVE, lambda: nc.vector.tensor_tensor(out=SWAP[:, :], in0=f_b[:, 0:128], in1=f_c[:, 0:128], op=ALU.add),
           reads=[tmp_res], writes=[k_res])

        def build_dist(io, n, thr, dst):
            op(DVE, lambda: nc.vector.tensor_scalar(out=f_a[:, 0:n], in0=io, scalar1=1.0, scalar2=None,
                                                    op0=ALU.mult), reads=[io_res, tmp_res], writes=[tmp_res])
            op(DVE, lambda: nc.vector.tensor_scalar(out=f_b[:, 0:n], in0=io, scalar1=-1.0, scalar2=None,
                                                    op0=ALU.mult), reads=[io_res, tmp_res], writes=[tmp_res])
            op(DVE, lambda: nc.vector.tensor_tensor(out=f_a[:, 0:n], in0=f_a[:, 0:n], in1=f_b[:, 0:n],
                                                    op=ALU.max), reads=[tmp_res], writes=[tmp_res])
            op(DVE, lambda: nc.vector.tensor_scalar(out=f_b[:, 0:n], in0=f_a[:, 0:n], scalar1=float(thr),
                                                    scalar2=None, op0=ALU.is_gt), reads=[tmp_res], writes=[tmp_res])
            op(DVE, lambda: nc.vector.tensor_scalar(out=f_c[:, 0:n], in0=f_a[:, 0:n], scalar1=-1.0, scalar2=None,
                                                    op0=ALU.mult), reads=[tmp_res], writes=[tmp_res])
            op(DVE, lambda: nc.vector.scalar_tensor_tensor(out=dst, in0=f_b[:, 0:n], scalar=-BIG, in1=f_c[:, 0:n],
                                                           op0=ALU.mult, op1=ALU.add),
               reads=[tmp_res], writes=[k_res])

        build_dist(io_d1, 256, 64, D1[:])
        build_dist(io_d0, 384, 128, D0[:].rearrange("p a b -> p (a b)"))
        for e in (PE, ACT, DVE):
            e.wait(DVE.last)
            e.wait(ACT.last)
            e.wait(Tok(c_sem, c_sem.cnt))

        ws = {"dma": 0, "use": 0, "total": n_seq * n_ring}

        def ws_pump():
            while ws["dma"] < ws["total"] and ws["dma"] < ws["use"] + NSLOT:
                n = ws["dma"]
                s = n % NSLOT
                dma(POOL, slot_sem[s], RING[:, s, :], wts[n % n_ring], writes=[slot_res[s]])
                ws["dma"] += 1

        def ws_take(key):
            n = ws["use"]
            assert ring_keys[n % n_ring] == key, (ring_keys[n % n_ring], key)
            assert ws["dma"] > n
            s = n % NSLOT
            return s

        def ws_release(count=1):
            ws["use"] += count
            ws_pump()

        ws_pump()

        def slot_proj(s, kc):
            return RING[:, s, kc * 128:(kc + 1) * 128]

        pj = {"i": 0, "banks": [6, 7, 0, 1, 2, 3, 4, 5]}

        def pj_bank():
            b = pj["banks"][pj["i"] % len(pj["banks"])]
            pj["i"] += 1
            return b

        def set_pj(banks):
            pj["banks"] = list(banks)
            pj["i"] = 0

        def rmsnorm(n_idx, tb, dst_fn, dst_res_fn, final=False):
            tsl = slice(tb * 512, (tb + 1) * 512)
            bank = pj_bank()
            for c in range(8):
                q = cnt["sq"] % 2
                cnt["sq"] += 1
                op(ACT, lambda c=c, q=q: nc.scalar.activation(out=SQ[:, q, :], in_=xT[:, c, tsl], func=AF.Square),
                   reads=[xT_res[c][tb]], writes=[sq_res[q]])
                op(PE, lambda c=c, q=q: nc.tensor.matmul(PS[bank][:, :], ONES[:, :], SQ[:, q, :], start=(c == 0),
                                                         stop=(c == 7)),
                   reads=[sq_res[q]], writes=[ps_res[bank]], sig=True)
            r = cnt["rstd"] % 2
            cnt["rstd"] += 1
            op(ACT, lambda: nc.scalar.activation(out=RSTD[:, r, :], in_=PS[bank][:, :], func=AF.Ln, bias=EPST[:, 0:1],
                                                 scale=1.0 / 1024.0),
               reads=[ps_res[bank]], writes=[rstd_res[r]])
            op(ACT, lambda: nc.scalar.activation(out=RSTD[:, r, :], in_=RSTD[:, r, :], func=AF.Exp, scale=-0.5),
               reads=[rstd_res[r]], writes=[rstd_res[r]])
            for c in range(8):
                op(DVE, lambda c=c: nc.vector.scalar_tensor_tensor(out=dst_fn(c), in0=xT[:, c, tsl],
                                                                   scalar=GAINS[:, n_idx * 8 + c:n_idx * 8 + c + 1],
                                                                   in1=RSTD[:, r, :], op0=ALU.mult, op1=ALU.mult),
                   reads=[xT_res[c][tb], rstd_res[r]], writes=[dst_res_fn(c)])

        def norm_to_A(n_idx, tb):
            tsl = slice(tb * 512, (tb + 1) * 512)
            rmsnorm(n_idx, tb, lambda c: A[:, c, tsl], lambda c: A_res[c][tb])

        def proj_fm(s, tb, dst, dst_res, last_use, evac_eng=None):
            bank = pj_bank()
            tsl = slice(tb * 512, (tb + 1) * 512)
            for kc in range(8):
                op(PE, lambda kc=kc: nc.tensor.matmul(PS[bank][:, :], slot_proj(s, kc), A[:, kc, tsl],
                                                      start=(kc == 0), stop=(kc == 7)),
                   reads=[slot_res[s], A_res[kc][tb]], writes=[ps_res[bank]], sig=(kc == 7))
            if evac_eng is not None:
                evac_eng(bank)
            else:
                op(DVE, lambda: nc.vector.tensor_copy(out=dst, in_=PS[bank][:, :]), reads=[ps_res[bank]],
                   writes=[dst_res])

        def resid_add(bank, oc, tb):
            tsl = slice(tb * 512, (tb + 1) * 512)
            op(DVE, lambda: nc.vector.tensor_tensor(out=xT[:, oc, tsl], in0=PS[bank][:, :], in1=xT[:, oc, tsl],
                                                    op=ALU.add),
               reads=[ps_res[bank], xT_res[oc][tb]], writes=[xT_res[oc][tb]])

        def ffn(l, scr_fence, pre_normed=False, next_norm=None):
            aT = SCR[:, 0:16384].rearrange("p (f t) -> p f t", f=8)
            WD = SCR[:, 16384:24576].rearrange("p (f c) -> p f c", f=8)
            aT_res = [[Res(scr_fence) for _ in range(4)] for _ in range(8)]
            wd_res = [Res(scr_fence) for _ in range(8)]
            set_pj([6, 7, 0, 1, 2, 3, 4, 5])
            if not pre_normed:
                for tb in range(4):
                    norm_to_A(2 + l, tb)
            for (f0, f1) in FSPL:
                nf = f1 - f0
                for fi in range(nf):
                    dma(POOL, wd_sem[fi], WD[:, fi, :], wts[wd_index[("d", l, f0 + fi)]], writes=[wd_res[fi]])
                for fi in range(nf):
                    f = f0 + fi
                    sg_ = ws_take(("g", l, f))
                    su_ = (ws["use"] + 1) % NSLOT
                    assert ring_keys[(ws["use"] + 1) % n_ring] == ("u", l, f)
                    for tb in range(4):
                        tsl = slice(tb * 512, (tb + 1) * 512)
                        bg = pj_bank()
                        bu = pj_bank()
                        for kc in range(8):
                            op(PE, lambda kc=kc: nc.tensor.matmul(PS[bg][:, :], slot_proj(sg_, kc), A[:, kc, tsl],
                                                                  start=(kc == 0), stop=(kc == 7)),
                               reads=[slot_res[sg_], A_res[kc][tb]], writes=[ps_res[bg]], sig=(kc == 7))
                        for kc in range(8):
                            op(PE, lambda kc=kc: nc.tensor.matmul(PS[bu][:, :], slot_proj(su_, kc), A[:, kc, tsl],
                                                                  start=(kc == 0), stop=(kc == 7)),
                               reads=[slot_res[su_], A_res[kc][tb]], writes=[ps_res[bu]], sig=(kc == 7))
                        q = cnt["sg"] % 2
                        cnt["sg"] += 1
                        op(ACT, lambda q=q: nc.scalar.activation(out=SG[:, q, :], in_=PS[bg][:, :], func=AF.Silu),
                           reads=[ps_res[bg]], writes=[sg_res[q]])
                        op(DVE, lambda q=q: nc.vector.tensor_tensor(out=aT[:, fi, tsl], in0=PS[bu][:, :],
                                                                    in1=SG[:, q, :], op=ALU.mult),
                           reads=[ps_res[bu], sg_res[q]], writes=[aT_res[fi][tb]])
                    ws_release(2)
                for tb in range(4):
                    tsl = slice(tb * 512, (tb + 1) * 512)
                    for oc in range(8):
                        bank = pj_bank()
                        for fi in range(nf):
                            op(PE, lambda fi=fi: nc.tensor.matmul(PS[bank][:, :], WD[:, fi, oc * 128:(oc + 1) * 128],
                                                                  aT[:, fi, tsl], start=(fi == 0), stop=(fi == nf - 1)),
                               reads=[wd_res[fi], aT_res[fi][tb]], writes=[ps_res[bank]], sig=(fi == nf - 1))
                        resid_add(bank, oc, tb)
                    if next_norm is not None and (f0, f1) == FSPL[-1] and tb >= 1:
                        next_norm(tb - 1)
                if next_norm is not None and (f0, f1) == FSPL[-1]:
                    next_norm(3)

        def layer0(scr_fence, next_norm=None):
            QK = SCR[:, 0:20480].rearrange("p (c t) -> p c t", c=10)
            V0 = SCR[:, 20480:24576].rearrange("p (n v) -> p n v", n=16)
            qk_res = [[Res(scr_fence) for _ in range(4)] for _ in range(10)]
            v_res = [Res(scr_fence) for _ in range(16)]
            set_pj([6, 7, 0, 1, 2, 3, 4, 5])
            slots = []
            for i in range(12):
                assert ring_keys[(ws["use"] + i) % n_ring] == (("qa", i) if i < 8 else (("ka", i - 8) if i < 10 else ("va", i - 10)))
                slots.append((ws["use"] + i) % NSLOT)
            for tb in range(4):
                norm_to_A(0, tb)
                tsl = slice(tb * 512, (tb + 1) * 512)
                for oc in range(10):
                    proj_fm(slots[oc], tb, QK[:, oc, tsl], qk_res[oc][tb], False)
                for tt in range(4):
                    n = tb * 4 + tt
                    bank = pj_bank()
                    first = True
                    for vu in range(2):
                        s = slots[10 + vu]
                        for kc in range(8):
                            op(PE, lambda kc=kc, s=s, vu=vu, first=first: nc.tensor.matmul(
                                PS[bank][:, vu * 128:(vu + 1) * 128], A[:, kc, n * 128:(n + 1) * 128], slot_proj(s, kc),
                                start=(kc == 0), stop=(kc == 7), skip_group_check=True),
                               reads=[slot_res[s], A_res[kc][tb]], writes=[ps_res[bank]], sig=(kc == 7))
                            first = False
                    op(DVE, lambda n=n: nc.vector.tensor_copy(out=V0[:, n, :], in_=PS[bank][:, 0:256]),
                       reads=[ps_res[bank]], writes=[v_res[n]])
            ws_release(12)
            set_pj([6, 7])
            s_banks = [0, 1]
            od_banks = [(2, 3), (4, 5)]
            items = []
            for b in range(16):
                for p in range(2):
                    tiles = []
                    for gl in range(2):
                        for c in (b - 1, b, b + 1):
                            if 0 <= c < 16:
                                tiles.append((gl, c))
                    for i, (gl, c) in enumerate(tiles):
                        items.append((b, p, gl, c, i == 0, i == len(tiles) - 1))
            LA = 1
            n_items = len(items)
            grp = {"k": 0}

            def emit_S(i):
                b, p, gl, c, _, _ = items[i]
                bank = s_banks[i % 2]
                rows = slice(gl * 64, gl * 64 + 64)
                typ = c - b + 1
                for a in range(4):
                    h = 8 * p + 4 * gl + a
                    k = (h + 1) + 7
                    op(PE, lambda a=a, k=k: nc.tensor.matmul(PS[bank][:, a * 128:(a + 1) * 128], IDT[:, k, :],
                                                             D0[:, typ, :], start=(a == 0), stop=False,
                                                             skip_group_check=True),
                       writes=[ps_res[bank]], sig=False)
                tb = b // 4
                tbk = c // 4
                op(PE, lambda: nc.tensor.matmul(PS[bank][:, :], QK[rows, 8 + p, c * 128:(c + 1) * 128],
                                                QK[rows, 4 * p:4 * p + 4, b * 128:(b + 1) * 128],
                                                start=False, stop=True, skip_group_check=True),
                   reads=[qk_res[8 + p][tbk]] + [qk_res[4 * p + a][tb] for a in range(4)], writes=[ps_res[bank]],
                   sig=True)

            def emit_rest(i):
                b, p, gl, c, first, last = items[i]
                bank = s_banks[i % 2]
                q = cnt["pt"] % 3
                cnt["pt"] += 1
                op(ACT, lambda: nc.scalar.activation(out=PT[:, q, :], in_=PS[bank][:, :], func=AF.Exp, scale=0.125),
                   reads=[ps_res[bank]], writes=[pt_res[q]])
                if first:
                    grp["k"] += 1
                ob, db = od_banks[grp["k"] % 2]
                rows = slice(gl * 64, gl * 64 + 64)
                cs = [cc for cc in (b - 1, b, b + 1) if 0 <= cc < 16]
                st = (c == cs[0])
                sp = (c == cs[-1])
                g = 2 * p + gl
                op(PE, lambda: nc.tensor.matmul(PS[ob][rows, :], V0[:, c, g * 64:(g + 1) * 64], PT[:, q, :],
                                                start=st, stop=sp, skip_group_check=True),
                   reads=[v_res[c], pt_res[q]], writes=[ps_res[ob]], sig=False)
                op(PE, lambda: nc.tensor.matmul(PS[db][rows, :], ONES[:, 0:64], PT[:, q, :],
                                                start=st, stop=sp, skip_group_check=True),
                   reads=[pt_res[q]], writes=[ps_res[db]], sig=True)
                if last:
                    tb = b // 4
                    for a in range(4):
                        op(DVE, lambda a=a: nc.vector.tensor_scalar(out=RD[:, a * 128:(a + 1) * 128],
                                                                    in0=PS[db][:, a * 128:(a + 1) * 128],
                                                                    scalar1=ESINK[:, 4 * p + a:4 * p + a + 1],
                                                                    scalar2=None, op0=ALU.add),
                           reads=[ps_res[db]], writes=[rd_res])
                    op(DVE, lambda: nc.vector.reciprocal(out=RD[:, :], in_=RD[:, :]), reads=[rd_res], writes=[rd_res])
                    op(DVE, lambda: nc.vector.tensor_tensor(
                        out=A[:, 4 * p:4 * p + 4, b * 128:(b + 1) * 128],
                        in0=PS[ob][:, :].rearrange("p (a q) -> p a q", a=4),
                        in1=RD[:, :].rearrange("p (a q) -> p a q", a=4), op=ALU.mult),
                       reads=[ps_res[ob], rd_res], writes=[A_res[4 * p + a][tb] for a in range(4)])

            for i in range(min(LA, n_items)):
                emit_S(i)
            for i in range(n_items):
                if i + LA < n_items:
                    emit_S(i + LA)
                emit_rest(i)
            set_pj([6, 7, 0, 1])
            oslots = []
            for kc in range(8):
                assert ring_keys[(ws["use"] + kc) % n_ring] == ("oa", kc)
                oslots.append((ws["use"] + kc) % NSLOT)
            for tb in range(4):
                tsl = slice(tb * 512, (tb + 1) * 512)
                for oc in range(8):
                    bank = pj_bank()
                    for kc in range(8):
                        s = oslots[kc]
                        op(PE, lambda kc=kc, s=s: nc.tensor.matmul(PS[bank][:, :], RING[:, s, oc * 128:(oc + 1) * 128],
                                                                   A[:, kc, tsl], start=(kc == 0), stop=(kc == 7)),
                           reads=[slot_res[s], A_res[kc][tb]], writes=[ps_res[bank]], sig=(kc == 7))
                    resid_add(bank, oc, tb)
                if next_norm is not None and tb >= 1:
                    next_norm(tb - 1)
            ws_release(8)
            if next_norm is not None:
                next_norm(3)

        def layer1(scr_fence, pre_normed=False, next_norm=None):
            Bh = SCR[:, 0:4096].rearrange("p (c t) -> p c t", c=2)
            QXY = SCR[:, 4096:12288].rearrange("p (s h t) -> p s h t", s=2, h=2)
            K1 = SCR[:, 12288:16384].rearrange("p (s t) -> p s t", s=2)
            V1 = SCR[:, 16384:22528].rearrange("p (s n v) -> p s n v", s=2, n=16)
            OACC = scr_f32(22528, 2048)
            DACC = scr_f32(26624, 2048)
            rdxy_res = [Res() for _ in range(2)]
            bh_res = [[Res(scr_fence) for _ in range(4)] for _ in range(2)]
            q_res = [Res(scr_fence) for _ in range(2)]
            k_res1 = [Res(scr_fence) for _ in range(2)]
            v_res = [[Res(scr_fence) for _ in range(4)] for _ in range(2)]
            acc_res = [Res(scr_fence) for _ in range(4)]
            acc_all = Res(scr_fence)
            set_pj([6, 7, 0, 1, 2, 3, 4, 5])
            for sl_ in range(2):
                op(DVE, lambda sl_=sl_: nc.vector.memset(V1[:, sl_, :, 64:128], 1.0),
                   writes=[v_res[sl_][n4_] for n4_ in range(4)])
                op(DVE, lambda sl_=sl_: nc.vector.memset(QXY[64:128, sl_, 0, :], 0.0), writes=[q_res[sl_]])
                op(DVE, lambda sl_=sl_: nc.vector.memset(QXY[0:64, sl_, 1, :], 0.0), writes=[q_res[sl_]])
            if not pre_normed:
                for tb in range(4):
                    norm_to_A(1, tb)
            set_pj([6, 7])
            s_banks = [0, 1]
            od_banks = [(2, 3), (4, 5)]
            st = {"s": 0, "od": 0}
            pgi = 0

            def projections(j, g, sl):
                dil = GROUPS[g][1]
                L = T // dil
                sq_ = ws_take(("qkvb", j, g, 0))
                sk_ = (ws["use"] + 1) % NSLOT
                sv_ = (ws["use"] + 2) % NSLOT
                for h_ in range(2):
                    val = 2.0 ** (-((2 * j + h_ + 1) - 4 * g) / 2.0)
                    op(ACT, lambda h_=h_, val=val: nc.scalar.activation(out=RDXY[:, sl, h_, :], in_=D1[:, :],
                                                                        func=AF.Exp, scale=val),
                       writes=[rdxy_res[sl]])
                for tb in range(4):
                    tsl = slice(tb * 512, (tb + 1) * 512)

                    def q_evac(bank, tsl=tsl):
                        op(DVE, lambda: nc.vector.tensor_copy(out=QXY[0:64, sl, 0, tsl], in_=PS[bank][0:64, :]),
                           reads=[ps_res[bank]], writes=[q_res[sl]])
                        op(DVE, lambda: nc.vector.tensor_copy(out=QXY[64:128, sl, 1, tsl], in_=PS[bank][64:128, :]),
                           reads=[ps_res[bank]], writes=[q_res[sl]])

                    proj_fm(sq_, tb, None, None, False, evac_eng=q_evac)
                    yield
                    proj_fm(sk_, tb, K1[:, sl, tsl], k_res1[sl], False)
                    yield
                for n4 in range(4):
                    bank = pj_bank()
                    for tt in range(4):
                        n = n4 * 4 + tt
                        pos = n * 128
                        r = pos // L
                        u0 = pos % L
                        t0 = r + dil * u0
                        tsel = slice(t0, t0 + dil * 127 + 1, dil)
                        for kc in range(8):
                            tbs = sorted({(t0 + dil * i) // 512 for i in (0, 127)})
                            rd = [slot_res[sv_]] + [A_res[kc][x] for x in range(tbs[0], tbs[-1] + 1)]
                            op(PE, lambda kc=kc, tt=tt, tsel=tsel: nc.tensor.matmul(
                                PS[bank][:, tt * 128:(tt + 1) * 128], A[:, kc, tsel], slot_proj(sv_, kc),
                                start=(kc == 0), stop=(kc == 7), skip_group_check=True),
                               reads=rd, writes=[ps_res[bank]], sig=(kc == 7))
                        if tt == 3:
                            op(DVE, lambda n4=n4: nc.vector.tensor_copy(
                                out=V1[:, sl, n4 * 4:(n4 + 1) * 4, :].rearrange("p n (h d) -> p n h d", h=3)[:, :, 0:3:2, :],
                                in_=PS[bank][:, :].rearrange("p (n h d) -> p n h d", n=4, h=2)),
                               reads=[ps_res[bank]], writes=[v_res[sl][n4]])
                        yield
                ws_release(3)

            def drain(gen):
                if gen is not None:
                    for _ in gen:
                        pass

            def attention(j, g, sl, filler=None, fill_n=2, prework=None, n_ch=0):
                win, dil = GROUPS[g]
                L = T // dil
                tot_tiles = 22 if dil == 1 else 16
                tctr = {"k": 0}
                hX, hY = 2 * j, 2 * j + 1
                kX = (hX + 1 - 4 * g) + 7
                kY = (hY + 1 - 4 * g) + 7
                for B in range(4):
                    tiles = []
                    p0 = 512 * B
                    for r in range(dil):
                        lo = max(r * L, p0) - r * L
                        hi = min((r + 1) * L, p0 + 512) - r * L
                        if hi <= lo:
                            continue
                        for m in range(L // 128):
                            ulo = max(128 * m - 64, lo, 0)
                            uhi = min(128 * m + 192, hi, L)
                            if uhi > ulo:
                                tiles.append((r, m, ulo, uhi))
                    ob, db = od_banks[st["od"] % 2]
                    st["od"] += 1
                    nt = len(tiles)
                    written = set()

                    def emit_S(i):
                        r, m, ulo, uhi = tiles[i]
                        N = uhi - ulo
                        bank = s_banks[(st["s"] + i) % 2]
                        jlo = ulo - 128 * m + 64
                        k0 = r + dil * 128 * m
                        ksel = slice(k0, k0 + dil * 127 + 1, dil)
                        q0 = r + dil * ulo
                        qsel = slice(q0, q0 + dil * (N - 1) + 1, dil)
                        psv_ = PS[bank][:, :].rearrange("p (h n) -> p h n", h=2)
                        op(PE, lambda: nc.tensor.matmul(psv_[:, :, 0:N], K1[:, sl, ksel], QXY[:, sl, :, qsel],
                                                        start=True, stop=True, skip_group_check=True),
                           reads=[k_res1[sl], q_res[sl]], writes=[ps_res[bank]], sig=True)

                    def emit_rest(i):
                        r, m, ulo, uhi = tiles[i]
                        N = uhi - ulo
                        bank = s_banks[(st["s"] + i) % 2]
                        q = cnt["pt"] % 3
                        cnt["pt"] += 1
                        ptv = PT[:, q, :].rearrange("p (h n) -> p h n", h=2)
                        psv = PS[bank][:, :].rearrange("p (h n) -> p h n", h=2)
                        op(ACT, lambda: nc.scalar.activation(out=ptv[:, :, 0:N], in_=psv[:, :, 0:N], func=AF.Exp,
                                                             scale=0.125),
                           reads=[ps_res[bank]], writes=[pt_res[q]])
                        jlo_ = ulo - 128 * m + 64
                        op(POOL, lambda: nc.gpsimd.tensor_tensor(out=ptv[:, :, 0:N], in0=ptv[:, :, 0:N],
                                                                 in1=RDXY[:, sl, :, jlo_:jlo_ + N], op=ALU.mult),
                           reads=[rdxy_res[sl]], writes=[pt_res[q]])
                        n = (r * L + 128 * m) // 128
                        vr = v_res[sl][n // 4]
                        if L == 128:
                            pieces = [(ulo, uhi)]
                        else:
                            b1 = 128 * m + 64
                            pieces = [(a_, e_) for (a_, e_) in ((ulo, min(uhi, b1)), (max(ulo, b1), uhi)) if e_ > a_]
                        for pi, (a_, e_) in enumerate(pieces):
                            c0 = r * L + a_ - p0
                            n_ = e_ - a_
                            o_ = a_ - ulo
                            fw = c0 not in written
                            written.add(c0)
                            lastp = (pi == len(pieces) - 1)
                            op(PE, lambda: nc.tensor.matmul(PS[ob][:, c0:c0 + n_], V1[:, sl, n, 0:128],
                                                            ptv[:, 0, o_:o_ + n_], start=fw, stop=True,
                                                            skip_group_check=True),
                               reads=[vr, pt_res[q]], writes=[ps_res[ob]], sig=False)
                            op(PE, lambda: nc.tensor.matmul(PS[db][:, c0:c0 + n_], V1[:, sl, n, 64:192],
                                                            ptv[:, 1, o_:o_ + n_], start=fw, stop=True,
                                                            skip_group_check=True),
                               reads=[vr, pt_res[q]], writes=[ps_res[db]], sig=lastp)

                    emit_S(0)
                    for i in range(nt):
                        if i + 1 < nt:
                            emit_S(i + 1)
                        if prework is not None:
                            next(prework, None)
                            next(prework, None)
                        if filler is not None:
                            k_ = tctr["k"]
                            tctr["k"] += 1
                            npull = (n_ch * (k_ + 1)) // tot_tiles - (n_ch * k_) // tot_tiles
                            for _ in range(npull):
                                next(filler, None)
                        emit_rest(i)
                    st["s"] += nt
                    if prework is not None:
                        drain(prework)
                        prework = None
                    for (acc, bank) in ((OACC, ob), (DACC, db)):
                        if dil == 1:
                            dst = acc[:, 512 * B:512 * B + 512]
                            src = PS[bank][:, :]
                        elif dil == 4:
                            dst = acc.rearrange("p (u w) -> p w u", w=4)[:, B, :]
                            src = PS[bank][:, :]
                        else:
                            dst = acc.rearrange("p (u w) -> p w u", w=16)[:, 4 * B:4 * B + 4, :]
                            src = PS[bank][:, :].rearrange("p (w u) -> p w u", w=4)
                        if g == 0:
                            op(DVE, lambda dst=dst, src=src: nc.vector.tensor_copy(out=dst, in_=src),
                               reads=[ps_res[bank]], writes=[acc_all])
                        else:
                            op(DVE, lambda dst=dst, src=src: nc.vector.tensor_tensor(out=dst, in0=src, in1=dst,
                                                                                     op=ALU.add),
                               reads=[ps_res[bank], acc_all], writes=[acc_all])

            def finalize(j):
                jj = j % 2
                for tb in range(4):
                    tsl = slice(tb * 512, (tb + 1) * 512)
                    r = cnt["rstd"] % 2
                    cnt["rstd"] += 1
                    op(ACT, lambda: nc.scalar.activation(out=RSTD[0:64, r, :], in_=DACC[0:64, tsl], func=AF.Ln),
                       reads=[acc_all], writes=[rstd_res[r]])
                    op(ACT, lambda: nc.scalar.activation(out=RSTD[64:128, r, :], in_=OACC[64:128, tsl], func=AF.Ln),
                       reads=[acc_all], writes=[rstd_res[r]])
                    op(ACT, lambda: nc.scalar.activation(out=RSTD[:, r, :], in_=RSTD[:, r, :], func=AF.Exp, scale=-1.0),
                       reads=[rstd_res[r]], writes=[rstd_res[r]])
                    yield
                    bank = pj_bank()
                    op(PE, lambda: nc.tensor.matmul(PS[bank][:, :], SWAP[:, :], RSTD[:, r, :], start=True, stop=True),
                       reads=[rstd_res[r]], writes=[ps_res[bank]], sig=True)
                    op(DVE, lambda: nc.vector.tensor_tensor(out=Bh[0:64, jj, tsl], in0=PS[bank][0:64, :],
                                                            in1=OACC[0:64, tsl], op=ALU.mult),
                       reads=[acc_all, ps_res[bank]], writes=[bh_res[jj][tb]])
                    op(DVE, lambda: nc.vector.tensor_tensor(out=Bh[64:128, jj, tsl], in0=PS[bank][64:128, :],
                                                            in1=DACC[64:128, tsl], op=ALU.mult),
                       reads=[acc_all, ps_res[bank]], writes=[bh_res[jj][tb]])
                    yield

            def outproj(quarter):
                oslots = []
                for kc in range(2):
                    assert ring_keys[(ws["use"] + kc) % n_ring] == ("ob", 2 * quarter + kc), (ring_keys[(ws["use"] + kc) % n_ring], quarter)
                    oslots.append((ws["use"] + kc) % NSLOT)
                ws["use"] += 2
                for tb in range(4):
                    tsl = slice(tb * 512, (tb + 1) * 512)
                    for oc in range(8):
                        bank = pj_bank()
                        for kc in range(2):
                            s = oslots[kc]
                            op(PE, lambda kc=kc, s=s: nc.tensor.matmul(PS[bank][:, :],
                                                                       RING[:, s, oc * 128:(oc + 1) * 128],
                                                                       Bh[:, kc, tsl], start=(kc == 0), stop=(kc == 1)),
                               reads=[slot_res[s], bh_res[kc][tb]], writes=[ps_res[bank]], sig=(kc == 1))
                        resid_add(bank, oc, tb)
                        yield
                ws_pump()

            seq = [(j, g) for j in range(8) for g in range(3)]
            def chain(*gens):
                for g_ in gens:
                    for _ in g_:
                        yield

            drain(projections(seq[0][0], seq[0][1], 0))
            pend_fin = None
            pend_out = None
            for i, (j, g) in enumerate(seq):
                sl = i % 2
                fillers = []
                n_ch = 0
                if pend_out is not None:
                    fillers.append(outproj(pend_out))
                    pend_out = None
                    n_ch += 32
                if i + 1 < len(seq):
                    fillers.append(projections(seq[i + 1][0], seq[i + 1][1], (i + 1) % 2))
                    n_ch += 24
                gen = chain(*fillers)
                pre = None
                if pend_fin is not None:
                    pre = finalize(pend_fin)
                    if pend_fin % 2 == 1:
                        pend_out = pend_fin // 2
                    pend_fin = None
                attention(j, g, sl, gen, prework=pre, n_ch=n_ch)
                drain(gen)
                if g == 2:
                    pend_fin = j
            drain(finalize(7))
            gen = outproj(3)
            for tb in range(4):
                for _ in range(8):
                    next(gen)
                if next_norm is not None and tb >= 1:
                    next_norm(tb - 1)
            drain(gen)
            if next_norm is not None:
                next_norm(3)

        def make_final_norm(seq_i, scr_fence):
            YS = scr_f32(24576, 1024).rearrange("p (s t) -> p s t", s=2)
            ys_res = [Res(scr_fence) for _ in range(2)]
            yv = yout[seq_i].rearrange("p (c t) -> p c t", c=8)

            def final_norm_tb(tb):
                tsl = slice(tb * 512, (tb + 1) * 512)
                bank = pj_bank()
                for c in range(8):
                    q = cnt["sq"] % 2
                    cnt["sq"] += 1
                    op(ACT, lambda c=c, q=q: nc.scalar.activation(out=SQ[:, q, :], in_=xT[:, c, tsl], func=AF.Square),
                       reads=[xT_res[c][tb]], writes=[sq_res[q]])
                    op(PE, lambda c=c, q=q: nc.tensor.matmul(PS[bank][:, :], ONES[:, :], SQ[:, q, :], start=(c == 0),
                                                             stop=(c == 7)),
                       reads=[sq_res[q]], writes=[ps_res[bank]], sig=True)
                r = cnt["rstd"] % 2
                cnt["rstd"] += 1
                op(ACT, lambda: nc.scalar.activation(out=RSTD[:, r, :], in_=PS[bank][:, :], func=AF.Ln,
                                                     bias=EPST[:, 0:1], scale=1.0 / 1024.0),
                   reads=[ps_res[bank]], writes=[rstd_res[r]])
                op(ACT, lambda: nc.scalar.activation(out=RSTD[:, r, :], in_=RSTD[:, r, :], func=AF.Exp, scale=-0.5),
                   reads=[rstd_res[r]], writes=[rstd_res[r]])
                for c in range(8):
                    y = cnt["y"] % 2
                    cnt["y"] += 1
                    op(DVE, lambda c=c, y=y: nc.vector.scalar_tensor_tensor(
                        out=YS[:, y, :], in0=xT[:, c, tsl], scalar=GAINS[:, 32 + c:33 + c], in1=RSTD[:, r, :],
                        op0=ALU.mult, op1=ALU.mult),
                       reads=[xT_res[c][tb], rstd_res[r]], writes=[ys_res[y]])
                    dma(SP, y_sem[y], yv[:, c, tsl], YS[:, y, :], reads=[ys_res[y]])

            return final_norm_tb

        for si in range(n_seq):
            xv = xin[si].rearrange("p (c t) -> p c t", c=8)
            for tb in range(4):
                tsl = slice(tb * 512, (tb + 1) * 512)
                dma(SP, x_sem[tb], xT[:, :, tsl], xv[:, :, tsl], writes=[xT_res[c][tb] for c in range(8)])
            phases = []
            if 0 in layers:
                phases += ["L0", "F0"]
            if 1 in layers:
                phases += ["L1", "F1"]
            nidx = {"L0": 0, "F0": 2, "L1": 1, "F1": 3}
            pre = False
            for pi, ph in enumerate(phases):
                f_ = fence()
                if pi + 1 < len(phases):
                    nn = (lambda tb, n=nidx[phases[pi + 1]]: norm_to_A(n, tb))
                else:
                    nn = make_final_norm(si, f_)
                if ph == "L0":
                    layer0(f_, next_norm=nn)
                elif ph == "L1":
                    layer1(f_, pre_normed=pre, next_norm=nn)
                else:
                    ffn(int(ph[1]), f_, pre_normed=pre, next_norm=nn)
                pre = True
        for y in range(2):
            SP.wait(Tok(y_sem[y], y_sem[y].cnt))
        for e in (PE, ACT, DVE, POOL):
            pass
    return nc


_PROG_CACHE = {}


def kernel(x, norm_mix, norm_ffn, w_qkv_a, w_out_a, sink_a, w_qkv_b, w_out_b, w_gate, w_up, w_down, norm_final):
    x = np.asarray(x, np.float32)
    B, S, Dm = x.shape
    wts = _pack_weights(*(np.asarray(w, np.float32) for w in (w_qkv_a, w_out_a, w_qkv_b, w_out_b, w_gate, w_up, w_down)))
    gl = [np.asarray(norm_mix, np.float32)[0], np.asarray(norm_mix, np.float32)[1],
          np.asarray(norm_ffn, np.float32)[0], np.asarray(norm_ffn, np.float32)[1],
          np.asarray(norm_final, np.float32)]
    gains = np.stack([g.reshape(8, 128).T for g in gl], axis=1).reshape(128, 40).copy()
    sk = np.asarray(sink_a, np.float32)[0]
    sink = np.empty((128, 8), np.float32)
    for p in range(2):
        for a in range(4):
            sink[0:64, 4 * p + a] = sk[8 * p + a]
            sink[64:128, 4 * p + a] = sk[8 * p + 4 + a]
    xt = np.ascontiguousarray(x.reshape(B, S, 8, 128).transpose(0, 3, 2, 1)).reshape(B, 128, 8 * S)
    if "nc" not in _PROG_CACHE:
        _PROG_CACHE["nc"] = build_program()
    nc = _PROG_CACHE["nc"]
    in_maps = []
    for i in range(N_CORES):
        in_maps.append({"xin": xt[SEQ_PER_CORE * i:SEQ_PER_CORE * (i + 1)], "wts": wts, "gains": gains, "sink": sink})
    res = run_bass_kernel_spmd(nc, in_maps, core_ids=list(range(N_CORES)))
    ys = np.concatenate([np.asarray(r["yout"]) for r in res.results], axis=0)
    out = ys.reshape(B, 128, 8, S).transpose(0, 3, 2, 1).reshape(B, S, Dm)
    return np.ascontiguousarray(out.astype(np.float32))
```

```python
import numpy as np
import concourse.bass as bass
import concourse.mybir as mybir
from concourse.bass_utils import run_bass_kernel_spmd

F32 = mybir.dt.float32
BF16 = mybir.dt.bfloat16
I32 = mybir.dt.int32
AF = mybir.ActivationFunctionType
ALU = mybir.AluOpType

T = 2048
KC = 8
NF = 22
NSLOT = 12
FSPL = [(0, 8), (8, 15), (15, 22)]
BIG = 30000.0
EPS = 1e-6
N_CORES = 8
SEQ_PER_CORE = 2
GROUPS = ((128, 1), (512, 4), (2048, 16))


def _unit_lists(layers=(0, 1)):
    ring = []
    if 0 in layers:
        for c in range(8):
            ring.append(("qa", c))
        for p in range(2):
            ring.append(("ka", p))
        for v in range(2):
            ring.append(("va", v))
        for kc in range(8):
            ring.append(("oa", kc))
        for (f0, f1) in FSPL:
            for f in range(f0, f1):
                ring.append(("g", 0, f))
                ring.append(("u", 0, f))
    if 1 in layers:
        items = [(j, g) for j in range(8) for g in range(3)]
        for s in range(3):
            ring.append(("qkvb", 0, 0, s))
        for i, (j, g) in enumerate(items):
            if g == 1 and j >= 2 and j % 2 == 0:
                ring.append(("ob", j - 2))
                ring.append(("ob", j - 1))
            if i + 1 < len(items):
                for s in range(3):
                    ring.append(("qkvb", items[i + 1][0], items[i + 1][1], s))
        ring.append(("ob", 6))
        ring.append(("ob", 7))
        for (f0, f1) in FSPL:
            for f in range(f0, f1):
                ring.append(("g", 1, f))
                ring.append(("u", 1, f))
    wd = []
    for l in range(2):
        for f in range(NF):
            wd.append(("d", l, f))
    return ring, wd


def _pack_weights(w_qkv_a, w_out_a, w_qkv_b, w_out_b, w_gate, w_up, w_down, layers=(0, 1)):
    ring, wd = _unit_lists(layers)
    keys = ring + wd
    out = np.empty((len(keys), 128, 1024), np.float32)

    def proj_unit(W, cols):
        return W[:, cols].reshape(8, 128, 128).transpose(1, 0, 2).reshape(128, 1024)

    qa_rows = {}
    for i, k in enumerate(keys):
        kind = k[0]
        if kind == "qa":
            c = k[1]
            p, a = c // 4, c % 4
            h0, h1 = 8 * p + a, 8 * p + 4 + a
            cols = np.concatenate([np.arange(64 * h0, 64 * h0 + 64), np.arange(64 * h1, 64 * h1 + 64)])
            out[i] = proj_unit(w_qkv_a[0], cols)
        elif kind == "ka":
            p = k[1]
            out[i] = proj_unit(w_qkv_a[0], np.arange(1024 + 128 * p, 1024 + 128 * p + 128))
        elif kind == "va":
            v = k[1]
            out[i] = proj_unit(w_qkv_a[0], np.arange(1280 + 128 * v, 1280 + 128 * v + 128))
        elif kind == "oa":
            c = k[1]
            p, a = c // 4, c % 4
            h0, h1 = 8 * p + a, 8 * p + 4 + a
            rows = np.concatenate([np.arange(64 * h0, 64 * h0 + 64), np.arange(64 * h1, 64 * h1 + 64)])
            out[i] = w_out_a[0][rows, :]
        elif kind == "g":
            _, l, f = k
            out[i] = proj_unit(w_gate[l], np.arange(128 * f, 128 * f + 128))
        elif kind == "u":
            _, l, f = k
            out[i] = proj_unit(w_up[l], np.arange(128 * f, 128 * f + 128))
        elif kind == "qkvb":
            _, j, g, s = k
            base = (3 * g + s) * 1024 + 128 * j
            out[i] = proj_unit(w_qkv_b[0], np.arange(base, base + 128))
        elif kind == "ob":
            kc = k[1]
            out[i] = w_out_b[0][128 * kc:128 * kc + 128, :]
        elif kind == "d":
            _, l, f = k
            out[i] = w_down[l][128 * f:128 * f + 128, :]
        else:
            raise AssertionError(kind)
    return out


class Sem:
    def __init__(self, h):
        self.h = h
        self.cnt = 0


class Tok:
    __slots__ = ("sem", "val")

    def __init__(self, sem, val):
        self.sem = sem
        self.val = val


class Res:
    __slots__ = ("w", "r")

    def __init__(self, fence=None):
        self.w = None
        self.r = list(fence) if fence else []


class Eng:
    def __init__(self, e, sem, is_pe=False):
        self.e = e
        self.sem = sem
        self.is_pe = is_pe
        self.seen = {}
        self.pend = None
        self.last = None

    def wait(self, tok):
        assert tok.val is not None, "unresolved token"
        if self.seen.get(tok.sem, 0) >= tok.val:
            return
        self.e.wait_ge(tok.sem.h, tok.val)
        self.seen[tok.sem] = tok.val


def _add_reader(res, tok):
    if res.r and res.r[-1] is tok:
        return
    if res.r:
        l = res.r[-1]
        if l.sem is tok.sem and l.val is not None and tok.val is not None and tok.val >= l.val:
            res.r[-1] = tok
            return
    res.r.append(tok)


def op(eng, fn, reads=(), writes=(), sig=True):
    deps = []
    for r in reads:
        if r.w is not None:
            deps.append(r.w)
    for w in writes:
        if w.w is not None:
            deps.append(w.w)
        deps.extend(w.r)
    for t in deps:
        if eng.is_pe and t.sem is eng.sem:
            continue
        eng.wait(t)
    ins = fn()
    if sig:
        ins.then_inc(eng.sem.h, 1)
        eng.sem.cnt += 1
        tok = Tok(eng.sem, eng.sem.cnt)
        if eng.pend is not None:
            eng.pend.val = tok.val
            eng.pend = None
    else:
        if eng.pend is None:
            eng.pend = Tok(eng.sem, None)
        tok = eng.pend
    eng.last = tok
    for r in reads:
        _add_reader(r, tok)
    for w in writes:
        w.w = tok
        w.r = []
    return tok


def dma(eng, dsem, out, in_, reads=(), writes=()):
    deps = []
    for r in reads:
        if r.w is not None:
            deps.append(r.w)
    for w in writes:
        if w.w is not None:
            deps.append(w.w)
        deps.extend(w.r)
    for t in deps:
        eng.wait(t)
    eng.e.dma_start(out=out, in_=in_).then_inc(dsem.h, 16)
    dsem.cnt += 16
    tok = Tok(dsem, dsem.cnt)
    for r in reads:
        _add_reader(r, tok)
    for w in writes:
        w.w = tok
        w.r = []
    return tok


def build_program(n_seq=SEQ_PER_CORE, layers=(0, 1), dbg=None):
    dbg = dbg or {}
    ring_keys, wd_keys = _unit_lists(layers)
    n_ring = len(ring_keys)
    n_units = n_ring + len(wd_keys)
    wd_index = {k: n_ring + i for i, k in enumerate(wd_keys)}

    nc = bass.Bass("TRN2", target_bir_lowering=False)
    xin = nc.dram_tensor("xin", [n_seq, 128, 8 * T], F32, kind="ExternalInput").ap()
    wts = nc.dram_tensor("wts", [n_units, 128, 1024], F32, kind="ExternalInput").ap()
    gains_d = nc.dram_tensor("gains", [128, 40], F32, kind="ExternalInput").ap()
    sink_d = nc.dram_tensor("sink", [128, 8], F32, kind="ExternalInput").ap()
    yout = nc.dram_tensor("yout", [n_seq, 128, 8 * T], F32, kind="ExternalOutput").ap()

    from contextlib import ExitStack
    with ExitStack() as es:
        def sb(name, shape, dt):
            return es.enter_context(nc.sbuf_tensor(name, shape, dt))

        def mksem(name):
            return Sem(es.enter_context(nc.semaphore(name)))

        xT = sb("xT", [128, 8, T], F32)
        A = sb("A", [128, 8, T], BF16)
        SCR = sb("SCR", [128, 30720], BF16)
        SWAP = sb("SWAP", [128, 128], F32)
        RD = sb("RD", [128, 512], F32)
        RDXY = sb("RDXY", [128, 2, 2, 256], BF16)
        RING = sb("RING", [128, NSLOT, 1024], BF16)
        IDT = sb("IDT", [128, 24, 128], BF16)
        ONES = sb("ONES", [128, 128], BF16)
        D1 = sb("D1", [128, 256], BF16)
        D0 = sb("D0", [128, 3, 128], BF16)
        GAINS = sb("GAINS", [128, 40], F32)
        ESINK = sb("ESINK", [128, 8], F32)
        EPST = sb("EPST", [128, 1], F32)
        SQ = sb("SQ", [128, 2, 512], BF16)
        RSTD = sb("RSTD", [128, 2, 512], F32)
        PT = sb("PT", [128, 3, 512], BF16)
        SG = sb("SG", [128, 2, 512], BF16)
        PS = [es.enter_context(nc.psum_tensor(f"ps{i}", [128, 512], F32)) for i in range(8)]

        PE = Eng(nc.tensor, mksem("s_pe"), is_pe=True)
        ACT = Eng(nc.scalar, mksem("s_act"))
        DVE = Eng(nc.vector, mksem("s_dve"))
        POOL = Eng(nc.gpsimd, mksem("s_pool"))
        SP = Eng(nc.sync, mksem("s_sp"))
        ENGS = [PE, ACT, DVE, POOL, SP]
        slot_sem = [mksem(f"s_slot{i}") for i in range(NSLOT)]
        wd_sem = [mksem(f"s_wd{i}") for i in range(8)]
        x_sem = [mksem(f"s_x{i}") for i in range(8)]
        y_sem = [mksem(f"s_y{i}") for i in range(2)]
        c_sem = mksem("s_const")

        ps_res = [Res() for _ in range(8)]
        slot_res = [Res() for _ in range(NSLOT)]
        xT_res = [[Res() for _ in range(4)] for _ in range(8)]
        A_res = [[Res() for _ in range(4)] for _ in range(8)]
        sq_res = [Res() for _ in range(2)]
        rstd_res = [Res() for _ in range(2)]
        pt_res = [Res() for _ in range(3)]
        sg_res = [Res() for _ in range(2)]
        rd_res = Res()
        rrh_res = rd_res
        cnt = {"sq": 0, "rstd": 0, "pt": 0, "sg": 0, "y": 0}

        def fence():
            toks = []
            for e in ENGS:
                if e.last is not None:
                    assert e.last.val is not None
                    toks.append(e.last)
            for s_ in wd_sem + y_sem:
                if s_.cnt > 0:
                    toks.append(Tok(s_, s_.cnt))
            return toks

        def scr_f32(off_bf, n_f32):
            return SCR[:, off_bf:off_bf + 2 * n_f32].bitcast(F32)

        cres = Res()
        tok_g = dma(SP, c_sem, GAINS[:], gains_d[:, :], writes=[cres])
        tok_s = dma(SP, c_sem, ESINK[:], sink_d[:, :], writes=[cres])
        tok_s = Tok(c_sem, c_sem.cnt)
        ACT.wait(tok_s)
        es_res = Res()
        op(ACT, lambda: nc.scalar.activation(out=ESINK[:], in_=ESINK[:], func=AF.Exp), writes=[es_res])
        k_res = Res()
        op(DVE, lambda: nc.vector.memset(ONES[:], 1.0), writes=[k_res])
        op(DVE, lambda: nc.vector.memset(EPST[:], EPS), writes=[k_res])
        io_id = SCR[:, 0:256].bitcast(I32)
        io_d1 = SCR[:, 256:768].bitcast(I32)
        io_d0 = SCR[:, 768:1536].bitcast(I32)
        f_a = scr_f32(1536, 384)
        f_b = scr_f32(2304, 384)
        f_c = scr_f32(3072, 384)
        io_res = Res()
        op(POOL, lambda: nc.gpsimd.iota(io_id, pattern=[[1, 128]], base=0, channel_multiplier=-1), writes=[io_res])
        op(POOL, lambda: nc.gpsimd.iota(io_d1, pattern=[[-1, 256]], base=64, channel_multiplier=1), writes=[io_res])
        op(POOL, lambda: nc.gpsimd.iota(io_d0.rearrange("p (a b) -> p a b", a=3), pattern=[[128, 3], [-1, 128]],
                                        base=-128, channel_multiplier=1), writes=[io_res])
        tmp_res = Res()
        op(DVE, lambda: nc.vector.tensor_scalar(out=f_a[:, 0:128], in0=io_id, scalar1=0.0, scalar2=None,
                                                op0=ALU.is_equal), reads=[io_res], writes=[tmp_res])
        for k in range(24):
            m = k - 7
            val = 8.0 * (2.0 ** (-m / 2.0))
            op(DVE, lambda k=k, val=val: nc.vector.tensor_scalar(out=IDT[:, k, :], in0=f_a[:, 0:128], scalar1=val,
                                                                 scalar2=None, op0=ALU.mult),
               reads=[tmp_res], writes=[k_res])

        op(DVE, lambda: nc.vector.tensor_scalar(out=f_b[:, 0:128], in0=io_id, scalar1=64.0, scalar2=None,
                                                op0=ALU.is_equal), reads=[io_res], writes=[tmp_res])
        op(DVE, lambda: nc.vector.tensor_scalar(out=f_c[:, 0:128], in0=io_id, scalar1=-64.0, scalar2=None,
                                                op0=ALU.is_equal), reads=[io_res], writes=[tmp_res])
        op(DVE, lambda: nc.vector.tensor_tensor(out=SWAP[:, :], in0=f_b[:, 0:128], in1=f_c[:, 0:128], op=ALU.add),
           reads=[tmp_res], writes=[k_res])

        def build_dist(io, n, thr, dst):
            op(DVE, lambda: nc.vector.tensor_scalar(out=f_a[:, 0:n], in0=io, scalar1=1.0, scalar2=None,
                                                    op0=ALU.mult), reads=[io_res, tmp_res], writes=[tmp_res])
            op(DVE, lambda: nc.vector.tensor_scalar(out=f_b[:, 0:n], in0=io, scalar1=-1.0, scalar2=None,
                                                    op0=ALU.mult), reads=[io_res, tmp_res], writes=[tmp_res])
            op(DVE, lambda: nc.vector.tensor_tensor(out=f_a[:, 0:n], in0=f_a[:, 0:n], in1=f_b[:, 0:n],
                                                    op=ALU.max), reads=[tmp_res], writes=[tmp_res])
            op(DVE, lambda: nc.vector.tensor_scalar(out=f_b[:, 0:n], in0=f_a[:, 0:n], scalar1=float(thr),
                                                    scalar2=None, op0=ALU.is_gt), reads=[tmp_res], writes=[tmp_res])
            op(DVE, lambda: nc.vector.tensor_scalar(out=f_c[:, 0:n], in0=f_a[:, 0:n], scalar1=-1.0, scalar2=None,
                                                    op0=ALU.mult), reads=[tmp_res], writes=[tmp_res])
            op(DVE, lambda: nc.vector.scalar_tensor_tensor(out=dst, in0=f_b[:, 0:n], scalar=-BIG, in1=f_c[:, 0:n],
                                                           op0=ALU.mult, op1=ALU.add),
               reads=[tmp_res], writes=[k_res])

        build_dist(io_d1, 256, 64, D1[:])
        build_dist(io_d0, 384, 128, D0[:].rearrange("p a b -> p (a b)"))
        for e in (PE, ACT, DVE):
            e.wait(DVE.last)
            e.wait(ACT.last)
            e.wait(Tok(c_sem, c_sem.cnt))

        ws = {"dma": 0, "use": 0, "total": n_seq * n_ring}

        def ws_pump():
            while ws["dma"] < ws["total"] and ws["dma"] < ws["use"] + NSLOT:
                n = ws["dma"]
                s = n % NSLOT
                dma(POOL, slot_sem[s], RING[:, s, :], wts[n % n_ring], writes=[slot_res[s]])
                ws["dma"] += 1

        def ws_take(key):
            n = ws["use"]
            assert ring_keys[n % n_ring] == key, (ring_keys[n % n_ring], key)
            assert ws["dma"] > n
            s = n % NSLOT
            return s

        def ws_release(count=1):
            ws["use"] += count
            ws_pump()

        ws_pump()

        def slot_proj(s, kc):
            return RING[:, s, kc * 128:(kc + 1) * 128]

        pj = {"i": 0, "banks": [6, 7, 0, 1, 2, 3, 4, 5]}

        def pj_bank():
            b = pj["banks"][pj["i"] % len(pj["banks"])]
            pj["i"] += 1
            return b

        def set_pj(banks):
            pj["banks"] = list(banks)
            pj["i"] = 0

        def rmsnorm(n_idx, tb, dst_fn, dst_res_fn, final=False):
            tsl = slice(tb * 512, (tb + 1) * 512)
            bank = pj_bank()
            for c in range(8):
                q = cnt["sq"] % 2
                cnt["sq"] += 1
                op(ACT, lambda c=c, q=q: nc.scalar.activation(out=SQ[:, q, :], in_=xT[:, c, tsl], func=AF.Square),
                   reads=[xT_res[c][tb]], writes=[sq_res[q]])
                op(PE, lambda c=c, q=q: nc.tensor.matmul(PS[bank][:, :], ONES[:, :], SQ[:, q, :], start=(c == 0),
                                                         stop=(c == 7)),
                   reads=[sq_res[q]], writes=[ps_res[bank]], sig=True)
            r = cnt["rstd"] % 2
            cnt["rstd"] += 1
            op(ACT, lambda: nc.scalar.activation(out=RSTD[:, r, :], in_=PS[bank][:, :], func=AF.Ln, bias=EPST[:, 0:1],
                                                 scale=1.0 / 1024.0),
               reads=[ps_res[bank]], writes=[rstd_res[r]])
            op(ACT, lambda: nc.scalar.activation(out=RSTD[:, r, :], in_=RSTD[:, r, :], func=AF.Exp, scale=-0.5),
               reads=[rstd_res[r]], writes=[rstd_res[r]])
            for c in range(8):
                op(DVE, lambda c=c: nc.vector.scalar_tensor_tensor(out=dst_fn(c), in0=xT[:, c, tsl],
                                                                   scalar=GAINS[:, n_idx * 8 + c:n_idx * 8 + c + 1],
                                                                   in1=RSTD[:, r, :], op0=ALU.mult, op1=ALU.mult),
                   reads=[xT_res[c][tb], rstd_res[r]], writes=[dst_res_fn(c)])

        def norm_to_A(n_idx, tb):
            tsl = slice(tb * 512, (tb + 1) * 512)
            rmsnorm(n_idx, tb, lambda c: A[:, c, tsl], lambda c: A_res[c][tb])

        def proj_fm(s, tb, dst, dst_res, last_use, evac_eng=None):
            bank = pj_bank()
            tsl = slice(tb * 512, (tb + 1) * 512)
            for kc in range(8):
                op(PE, lambda kc=kc: nc.tensor.matmul(PS[bank][:, :], slot_proj(s, kc), A[:, kc, tsl],
                                                      start=(kc == 0), stop=(kc == 7)),
                   reads=[slot_res[s], A_res[kc][tb]], writes=[ps_res[bank]], sig=(kc == 7))
            if evac_eng is not None:
                evac_eng(bank)
            else:
                op(DVE, lambda: nc.vector.tensor_copy(out=dst, in_=PS[bank][:, :]), reads=[ps_res[bank]],
                   writes=[dst_res])

        def resid_add(bank, oc, tb):
            tsl = slice(tb * 512, (tb + 1) * 512)
            op(DVE, lambda: nc.vector.tensor_tensor(out=xT[:, oc, tsl], in0=PS[bank][:, :], in1=xT[:, oc, tsl],
                                                    op=ALU.add),
               reads=[ps_res[bank], xT_res[oc][tb]], writes=[xT_res[oc][tb]])

        def ffn(l, scr_fence, pre_normed=False, next_norm=None):
            aT = SCR[:, 0:16384].rearrange("p (f t) -> p f t", f=8)
            WD = SCR[:, 16384:24576].rearrange("p (f c) -> p f c", f=8)
            aT_res = [[Res(scr_fence) for _ in range(4)] for _ in range(8)]
            wd_res = [Res(scr_fence) for _ in range(8)]
            set_pj([6, 7, 0, 1, 2, 3, 4, 5])
            if not pre_normed:
                for tb in range(4):
                    norm_to_A(2 + l, tb)
            for (f0, f1) in FSPL:
                nf = f1 - f0
                for fi in range(nf):
                    dma(POOL, wd_sem[fi], WD[:, fi, :], wts[wd_index[("d", l, f0 + fi)]], writes=[wd_res[fi]])
                for fi in range(nf):
                    f = f0 + fi
                    sg_ = ws_take(("g", l, f))
                    su_ = (ws["use"] + 1) % NSLOT
                    assert ring_keys[(ws["use"] + 1) % n_ring] == ("u", l, f)
                    for tb in range(4):
                        tsl = slice(tb * 512, (tb + 1) * 512)
                        bg = pj_bank()
                        bu = pj_bank()
                        for kc in range(8):
                            op(PE, lambda kc=kc: nc.tensor.matmul(PS[bg][:, :], slot_proj(sg_, kc), A[:, kc, tsl],
                                                                  start=(kc == 0), stop=(kc == 7)),
                               reads=[slot_res[sg_], A_res[kc][tb]], writes=[ps_res[bg]], sig=(kc == 7))
                        for kc in range(8):
                            op(PE, lambda kc=kc: nc.tensor.matmul(PS[bu][:, :], slot_proj(su_, kc), A[:, kc, tsl],
                                                                  start=(kc == 0), stop=(kc == 7)),
                               reads=[slot_res[su_], A_res[kc][tb]], writes=[ps_res[bu]], sig=(kc == 7))
                        q = cnt["sg"] % 2
                        cnt["sg"] += 1
                        op(ACT, lambda q=q: nc.scalar.activation(out=SG[:, q, :], in_=PS[bg][:, :], func=AF.Silu),
                           reads=[ps_res[bg]], writes=[sg_res[q]])
                        op(DVE, lambda q=q: nc.vector.tensor_tensor(out=aT[:, fi, tsl], in0=PS[bu][:, :],
                                                                    in1=SG[:, q, :], op=ALU.mult),
                           reads=[ps_res[bu], sg_res[q]], writes=[aT_res[fi][tb]])
                    ws_release(2)
                for tb in range(4):
                    tsl = slice(tb * 512, (tb + 1) * 512)
                    for oc in range(8):
                        bank = pj_bank()
                        for fi in range(nf):
                            op(PE, lambda fi=fi: nc.tensor.matmul(PS[bank][:, :], WD[:, fi, oc * 128:(oc + 1) * 128],
                                                                  aT[:, fi, tsl], start=(fi == 0), stop=(fi == nf - 1)),
                               reads=[wd_res[fi], aT_res[fi][tb]], writes=[ps_res[bank]], sig=(fi == nf - 1))
                        resid_add(bank, oc, tb)
                    if next_norm is not None and (f0, f1) == FSPL[-1] and tb >= 1:
                        next_norm(tb - 1)
                if next_norm is not None and (f0, f1) == FSPL[-1]:
                    next_norm(3)

        def layer0(scr_fence, next_norm=None):
            QK = SCR[:, 0:20480].rearrange("p (c t) -> p c t", c=10)
            V0 = SCR[:, 20480:24576].rearrange("p (n v) -> p n v", n=16)
            qk_res = [[Res(scr_fence) for _ in range(4)] for _ in range(10)]
            v_res = [Res(scr_fence) for _ in range(16)]
            set_pj([6, 7, 0, 1, 2, 3, 4, 5])
            slots = []
            for i in range(12):
                assert ring_keys[(ws["use"] + i) % n_ring] == (("qa", i) if i < 8 else (("ka", i - 8) if i < 10 else ("va", i - 10)))
                slots.append((ws["use"] + i) % NSLOT)
            for tb in range(4):
                norm_to_A(0, tb)
                tsl = slice(tb * 512, (tb + 1) * 512)
                for oc in range(10):
                    proj_fm(slots[oc], tb, QK[:, oc, tsl], qk_res[oc][tb], False)
                for tt in range(4):
                    n = tb * 4 + tt
                    bank = pj_bank()
                    first = True
                    for vu in range(2):
                        s = slots[10 + vu]
                        for kc in range(8):
                            op(PE, lambda kc=kc, s=s, vu=vu, first=first: nc.tensor.matmul(
                                PS[bank][:, vu * 128:(vu + 1) * 128], A[:, kc, n * 128:(n + 1) * 128], slot_proj(s, kc),
                                start=(kc == 0), stop=(kc == 7), skip_group_check=True),
                               reads=[slot_res[s], A_res[kc][tb]], writes=[ps_res[bank]], sig=(kc == 7))
                            first = False
                    op(DVE, lambda n=n: nc.vector.tensor_copy(out=V0[:, n, :], in_=PS[bank][:, 0:256]),
                       reads=[ps_res[bank]], writes=[v_res[n]])
            ws_release(12)
            set_pj([7])
            s_banks = [0, 1, 6]
            od_banks = [(2, 3), (4, 5)]
            items = []
            for b in range(16):
                for p in range(2):
                    tiles = []
                    for gl in range(2):
                        for c in (b - 1, b, b + 1):
                            if 0 <= c < 16:
                                tiles.append((gl, c))
                    for i, (gl, c) in enumerate(tiles):
                        items.append((b, p, gl, c, i == 0, i == len(tiles) - 1))
            LA = 2
            n_items = len(items)
            grp = {"k": 0}

            def emit_S(i):
                b, p, gl, c, _, _ = items[i]
                bank = s_banks[i % 3]
                rows = slice(gl * 64, gl * 64 + 64)
                typ = c - b + 1
                for a in range(4):
                    h = 8 * p + 4 * gl + a
                    k = (h + 1) + 7
                    op(PE, lambda a=a, k=k: nc.tensor.matmul(PS[bank][:, a * 128:(a + 1) * 128], IDT[:, k, :],
                                                             D0[:, typ, :], start=(a == 0), stop=False,
                                                             skip_group_check=True),
                       writes=[ps_res[bank]], sig=False)
                tb = b // 4
                tbk = c // 4
                op(PE, lambda: nc.tensor.matmul(PS[bank][:, :], QK[rows, 8 + p, c * 128:(c + 1) * 128],
                                                QK[rows, 4 * p:4 * p + 4, b * 128:(b + 1) * 128],
                                                start=False, stop=True, skip_group_check=True),
                   reads=[qk_res[8 + p][tbk]] + [qk_res[4 * p + a][tb] for a in range(4)], writes=[ps_res[bank]],
                   sig=True)

            def emit_rest(i):
                b, p, gl, c, first, last = items[i]
                bank = s_banks[i % 3]
                q = cnt["pt"] % 3
                cnt["pt"] += 1
                op(ACT, lambda: nc.scalar.activation(out=PT[:, q, :], in_=PS[bank][:, :], func=AF.Exp, scale=0.125),
                   reads=[ps_res[bank]], writes=[pt_res[q]])
                if first:
                    grp["k"] += 1
                ob, db = od_banks[grp["k"] % 2]
                rows = slice(gl * 64, gl * 64 + 64)
                cs = [cc for cc in (b - 1, b, b + 1) if 0 <= cc < 16]
                st = (c == cs[0])
                sp = (c == cs[-1])
                g = 2 * p + gl
                op(PE, lambda: nc.tensor.matmul(PS[ob][rows, :], V0[:, c, g * 64:(g + 1) * 64], PT[:, q, :],
                                                start=st, stop=sp, skip_group_check=True),
                   reads=[v_res[c], pt_res[q]], writes=[ps_res[ob]], sig=False)
                op(PE, lambda: nc.tensor.matmul(PS[db][rows, :], ONES[:, 0:64], PT[:, q, :],
                                                start=st, stop=sp, skip_group_check=True),
                   reads=[pt_res[q]], writes=[ps_res[db]], sig=True)
                if last:
                    tb = b // 4
                    for a in range(4):
                        op(DVE, lambda a=a: nc.vector.tensor_scalar(out=RD[:, a * 128:(a + 1) * 128],
                                                                    in0=PS[db][:, a * 128:(a + 1) * 128],
                                                                    scalar1=ESINK[:, 4 * p + a:4 * p + a + 1],
                                                                    scalar2=None, op0=ALU.add),
                           reads=[ps_res[db]], writes=[rd_res])
                    op(DVE, lambda: nc.vector.reciprocal(out=RD[:, :], in_=RD[:, :]), reads=[rd_res], writes=[rd_res])
                    op(DVE, lambda: nc.vector.tensor_tensor(
                        out=A[:, 4 * p:4 * p + 4, b * 128:(b + 1) * 128],
                        in0=PS[ob][:, :].rearrange("p (a q) -> p a q", a=4),
                        in1=RD[:, :].rearrange("p (a q) -> p a q", a=4), op=ALU.mult),
                       reads=[ps_res[ob], rd_res], writes=[A_res[4 * p + a][tb] for a in range(4)])

            for i in range(min(LA, n_items)):
                emit_S(i)
            for i in range(n_items):
                if i + LA < n_items:
                    emit_S(i + LA)
                emit_rest(i)
            set_pj([6, 7, 0, 1])
            oslots = []
            for kc in range(8):
                assert ring_keys[(ws["use"] + kc) % n_ring] == ("oa", kc)
                oslots.append((ws["use"] + kc) % NSLOT)
            for tb in range(4):
                tsl = slice(tb * 512, (tb + 1) * 512)
                for oc in range(8):
                    bank = pj_bank()
                    for kc in range(8):
                        s = oslots[kc]
                        op(PE, lambda kc=kc, s=s: nc.tensor.matmul(PS[bank][:, :], RING[:, s, oc * 128:(oc + 1) * 128],
                                                                   A[:, kc, tsl], start=(kc == 0), stop=(kc == 7)),
                           reads=[slot_res[s], A_res[kc][tb]], writes=[ps_res[bank]], sig=(kc == 7))
                    resid_add(bank, oc, tb)
                if next_norm is not None and tb >= 1:
                    next_norm(tb - 1)
            ws_release(8)
            if next_norm is not None:
                next_norm(3)

        def layer1(scr_fence, pre_normed=False, next_norm=None):
            Bh = SCR[:, 0:4096].rearrange("p (c t) -> p c t", c=2)
            QXY = SCR[:, 4096:12288].rearrange("p (s h t) -> p s h t", s=2, h=2)
            K1 = SCR[:, 12288:16384].rearrange("p (s t) -> p s t", s=2)
            V1 = SCR[:, 16384:22528].rearrange("p (s n v) -> p s n v", s=2, n=16)
            OACC = scr_f32(22528, 2048)
            DACC = scr_f32(26624, 2048)
            rdxy_res = [Res() for _ in range(2)]
            bh_res = [[Res(scr_fence) for _ in range(4)] for _ in range(2)]
            q_res = [Res(scr_fence) for _ in range(2)]
            k_res1 = [Res(scr_fence) for _ in range(2)]
            v_res = [[Res(scr_fence) for _ in range(4)] for _ in range(2)]
            acc_res = [Res(scr_fence) for _ in range(4)]
            acc_all = Res(scr_fence)
            set_pj([6, 7, 0, 1, 2, 3, 4, 5])
            for sl_ in range(2):
                op(DVE, lambda sl_=sl_: nc.vector.memset(V1[:, sl_, :, 64:128], 1.0),
                   writes=[v_res[sl_][n4_] for n4_ in range(4)])
                op(DVE, lambda sl_=sl_: nc.vector.memset(QXY[64:128, sl_, 0, :], 0.0), writes=[q_res[sl_]])
                op(DVE, lambda sl_=sl_: nc.vector.memset(QXY[0:64, sl_, 1, :], 0.0), writes=[q_res[sl_]])
            if not pre_normed:
                for tb in range(4):
                    norm_to_A(1, tb)
            set_pj([6, 7])
            s_banks = [0, 1]
            od_banks = [(2, 3), (4, 5)]
            st = {"s": 0, "od": 0}
            pgi = 0

            def projections(j, g, sl):
                dil = GROUPS[g][1]
                L = T // dil
                sq_ = ws_take(("qkvb", j, g, 0))
                sk_ = (ws["use"] + 1) % NSLOT
                sv_ = (ws["use"] + 2) % NSLOT
                for h_ in range(2):
                    val = 8.0 * (2.0 ** (-((2 * j + h_ + 1) - 4 * g) / 2.0))
                    op(DVE, lambda h_=h_, val=val: nc.vector.tensor_scalar(out=RDXY[:, sl, h_, :], in0=D1[:, :],
                                                                         scalar1=val, scalar2=None, op0=ALU.mult),
                       writes=[rdxy_res[sl]])
                for tb in range(4):
                    tsl = slice(tb * 512, (tb + 1) * 512)

                    def q_evac(bank, tsl=tsl):
                        op(DVE, lambda: nc.vector.tensor_copy(out=QXY[0:64, sl, 0, tsl], in_=PS[bank][0:64, :]),
                           reads=[ps_res[bank]], writes=[q_res[sl]])
                        op(DVE, lambda: nc.vector.tensor_copy(out=QXY[64:128, sl, 1, tsl], in_=PS[bank][64:128, :]),
                           reads=[ps_res[bank]], writes=[q_res[sl]])

                    proj_fm(sq_, tb, None, None, False, evac_eng=q_evac)
                    yield
                    proj_fm(sk_, tb, K1[:, sl, tsl], k_res1[sl], False)
                    yield
                for n4 in range(4):
                    bank = pj_bank()
                    for tt in range(4):
                        n = n4 * 4 + tt
                        pos = n * 128
                        r = pos // L
                        u0 = pos % L
                        t0 = r + dil * u0
                        tsel = slice(t0, t0 + dil * 127 + 1, dil)
                        for kc in range(8):
                            tbs = sorted({(t0 + dil * i) // 512 for i in (0, 127)})
                            rd = [slot_res[sv_]] + [A_res[kc][x] for x in range(tbs[0], tbs[-1] + 1)]
                            op(PE, lambda kc=kc, tt=tt, tsel=tsel: nc.tensor.matmul(
                                PS[bank][:, tt * 128:(tt + 1) * 128], A[:, kc, tsel], slot_proj(sv_, kc),
                                start=(kc == 0), stop=(kc == 7), skip_group_check=True),
                               reads=rd, writes=[ps_res[bank]], sig=(kc == 7))
                        if tt == 3:
                            op(DVE, lambda n4=n4: nc.vector.tensor_copy(
                                out=V1[:, sl, n4 * 4:(n4 + 1) * 4, :].rearrange("p n (h d) -> p n h d", h=3)[:, :, 0:3:2, :],
                                in_=PS[bank][:, :].rearrange("p (n h d) -> p n h d", n=4, h=2)),
                               reads=[ps_res[bank]], writes=[v_res[sl][n4]])
                        yield
                ws_release(3)

            def drain(gen):
                if gen is not None:
                    for _ in gen:
                        pass

            def attention(j, g, sl, filler=None, fill_n=2, prework=None, n_ch=0):
                win, dil = GROUPS[g]
                L = T // dil
                tot_tiles = 22 if dil == 1 else 16
                tctr = {"k": 0}
                hX, hY = 2 * j, 2 * j + 1
                kX = (hX + 1 - 4 * g) + 7
                kY = (hY + 1 - 4 * g) + 7
                for B in range(4):
                    tiles = []
                    p0 = 512 * B
                    for r in range(dil):
                        lo = max(r * L, p0) - r * L
                        hi = min((r + 1) * L, p0 + 512) - r * L
                        if hi <= lo:
                            continue
                        for m in range(L // 128):
                            ulo = max(128 * m - 64, lo, 0)
                            uhi = min(128 * m + 192, hi, L)
                            if uhi > ulo:
                                tiles.append((r, m, ulo, uhi))
                    ob, db = od_banks[st["od"] % 2]
                    st["od"] += 1
                    nt = len(tiles)
                    written = set()

                    def emit_S(i):
                        r, m, ulo, uhi = tiles[i]
                        N = uhi - ulo
                        bank = s_banks[(st["s"] + i) % 2]
                        jlo = ulo - 128 * m + 64
                        k0 = r + dil * 128 * m
                        ksel = slice(k0, k0 + dil * 127 + 1, dil)
                        q0 = r + dil * ulo
                        qsel = slice(q0, q0 + dil * (N - 1) + 1, dil)
                        psv_ = PS[bank][:, :].rearrange("p (h n) -> p h n", h=2)
                        op(PE, lambda: nc.tensor.matmul(psv_[:, :, 0:N], IDT[:, 13, :], RDXY[:, sl, :, jlo:jlo + N],
                                                        start=True, stop=False, skip_group_check=True),
                           reads=[rdxy_res[sl]], writes=[ps_res[bank]], sig=False)
                        op(PE, lambda: nc.tensor.matmul(psv_[:, :, 0:N], K1[:, sl, ksel], QXY[:, sl, :, qsel],
                                                        start=False, stop=True, skip_group_check=True),
                           reads=[k_res1[sl], q_res[sl]], writes=[ps_res[bank]], sig=True)

                    def emit_rest(i):
                        r, m, ulo, uhi = tiles[i]
                        N = uhi - ulo
                        bank = s_banks[(st["s"] + i) % 2]
                        q = cnt["pt"] % 3
                        cnt["pt"] += 1
                        ptv = PT[:, q, :].rearrange("p (h n) -> p h n", h=2)
                        psv = PS[bank][:, :].rearrange("p (h n) -> p h n", h=2)
                        op(ACT, lambda: nc.scalar.activation(out=ptv[:, :, 0:N], in_=psv[:, :, 0:N], func=AF.Exp,
                                                             scale=0.125),
                           reads=[ps_res[bank]], writes=[pt_res[q]])
                        n = (r * L + 128 * m) // 128
                        vr = v_res[sl][n // 4]
                        if L == 128:
                            pieces = [(ulo, uhi)]
                        else:
                            b1 = 128 * m + 64
                            pieces = [(a_, e_) for (a_, e_) in ((ulo, min(uhi, b1)), (max(ulo, b1), uhi)) if e_ > a_]
                        for pi, (a_, e_) in enumerate(pieces):
                            c0 = r * L + a_ - p0
                            n_ = e_ - a_
                            o_ = a_ - ulo
                            fw = c0 not in written
                            written.add(c0)
                            lastp = (pi == len(pieces) - 1)
                            op(PE, lambda: nc.tensor.matmul(PS[ob][:, c0:c0 + n_], V1[:, sl, n, 0:128],
                                                            ptv[:, 0, o_:o_ + n_], start=fw, stop=True,
                                                            skip_group_check=True),
                               reads=[vr, pt_res[q]], writes=[ps_res[ob]], sig=False)
                            op(PE, lambda: nc.tensor.matmul(PS[db][:, c0:c0 + n_], V1[:, sl, n, 64:192],
                                                            ptv[:, 1, o_:o_ + n_], start=fw, stop=True,
                                                            skip_group_check=True),
                               reads=[vr, pt_res[q]], writes=[ps_res[db]], sig=lastp)

                    emit_S(0)
                    for i in range(nt):
                        if i + 1 < nt:
                            emit_S(i + 1)
                        if prework is not None:
                            next(prework, None)
                            next(prework, None)
                        if filler is not None:
                            k_ = tctr["k"]
                            tctr["k"] += 1
                            npull = (n_ch * (k_ + 1)) // tot_tiles - (n_ch * k_) // tot_tiles
                            for _ in range(npull):
                                next(filler, None)
                        emit_rest(i)
                    st["s"] += nt
                    if prework is not None:
                        drain(prework)
                        prework = None
                    for (acc, bank) in ((OACC, ob), (DACC, db)):
                        if dil == 1:
                            dst = acc[:, 512 * B:512 * B + 512]
                            src = PS[bank][:, :]
                        elif dil == 4:
                            dst = acc.rearrange("p (u w) -> p w u", w=4)[:, B, :]
                            src = PS[bank][:, :]
                        else:
                            dst = acc.rearrange("p (u w) -> p w u", w=16)[:, 4 * B:4 * B + 4, :]
                            src = PS[bank][:, :].rearrange("p (w u) -> p w u", w=4)
                        if g == 0:
                            op(DVE, lambda dst=dst, src=src: nc.vector.tensor_copy(out=dst, in_=src),
                               reads=[ps_res[bank]], writes=[acc_all])
                        else:
                            op(DVE, lambda dst=dst, src=src: nc.vector.tensor_tensor(out=dst, in0=src, in1=dst,
                                                                                     op=ALU.add),
                               reads=[ps_res[bank], acc_all], writes=[acc_all])

            def finalize(j):
                jj = j % 2
                for tb in range(4):
                    tsl = slice(tb * 512, (tb + 1) * 512)
                    r = cnt["rstd"] % 2
                    cnt["rstd"] += 1
                    op(ACT, lambda: nc.scalar.activation(out=RSTD[0:64, r, :], in_=DACC[0:64, tsl], func=AF.Ln),
                       reads=[acc_all], writes=[rstd_res[r]])
                    op(ACT, lambda: nc.scalar.activation(out=RSTD[64:128, r, :], in_=OACC[64:128, tsl], func=AF.Ln),
                       reads=[acc_all], writes=[rstd_res[r]])
                    op(ACT, lambda: nc.scalar.activation(out=RSTD[:, r, :], in_=RSTD[:, r, :], func=AF.Exp, scale=-1.0),
                       reads=[rstd_res[r]], writes=[rstd_res[r]])
                    yield
                    bank = pj_bank()
                    op(PE, lambda: nc.tensor.matmul(PS[bank][:, :], SWAP[:, :], RSTD[:, r, :], start=True, stop=True),
                       reads=[rstd_res[r]], writes=[ps_res[bank]], sig=True)
                    op(DVE, lambda: nc.vector.tensor_tensor(out=Bh[0:64, jj, tsl], in0=PS[bank][0:64, :],
                                                            in1=OACC[0:64, tsl], op=ALU.mult),
                       reads=[acc_all, ps_res[bank]], writes=[bh_res[jj][tb]])
                    op(DVE, lambda: nc.vector.tensor_tensor(out=Bh[64:128, jj, tsl], in0=PS[bank][64:128, :],
                                                            in1=DACC[64:128, tsl], op=ALU.mult),
                       reads=[acc_all, ps_res[bank]], writes=[bh_res[jj][tb]])
                    yield

            def outproj(quarter):
                oslots = []
                for kc in range(2):
                    assert ring_keys[(ws["use"] + kc) % n_ring] == ("ob", 2 * quarter + kc), (ring_keys[(ws["use"] + kc) % n_ring], quarter)
                    oslots.append((ws["use"] + kc) % NSLOT)
                ws["use"] += 2
                for tb in range(4):
                    tsl = slice(tb * 512, (tb + 1) * 512)
                    for oc in range(8):
                        bank = pj_bank()
                        for kc in range(2):
                            s = oslots[kc]
                            op(PE, lambda kc=kc, s=s: nc.tensor.matmul(PS[bank][:, :],
                                                                       RING[:, s, oc * 128:(oc + 1) * 128],
                                                                       Bh[:, kc, tsl], start=(kc == 0), stop=(kc == 1)),
                               reads=[slot_res[s], bh_res[kc][tb]], writes=[ps_res[bank]], sig=(kc == 1))
                        resid_add(bank, oc, tb)
                        yield
                ws_pump()

            seq = [(j, g) for j in range(8) for g in range(3)]
            def chain(*gens):
                for g_ in gens:
                    for _ in g_:
                        yield

            drain(projections(seq[0][0], seq[0][1], 0))
            pend_fin = None
            pend_out = None
            for i, (j, g) in enumerate(seq):
                sl = i % 2
                fillers = []
                n_ch = 0
                if pend_out is not None:
                    fillers.append(outproj(pend_out))
                    pend_out = None
                    n_ch += 32
                if i + 1 < len(seq):
                    fillers.append(projections(seq[i + 1][0], seq[i + 1][1], (i + 1) % 2))
                    n_ch += 24
                gen = chain(*fillers)
                pre = None
                if pend_fin is not None:
                    pre = finalize(pend_fin)
                    if pend_fin % 2 == 1:
                        pend_out = pend_fin // 2
                    pend_fin = None
                attention(j, g, sl, gen, prework=pre, n_ch=n_ch)
                drain(gen)
                if g == 2:
                    pend_fin = j
            drain(finalize(7))
            gen = outproj(3)
            for tb in range(4):
                for _ in range(8):
                    next(gen)
                if next_norm is not None and tb >= 1:
                    next_norm(tb - 1)
            drain(gen)
            if next_norm is not None:
                next_norm(3)

        def make_final_norm(seq_i, scr_fence):
            YS = scr_f32(24576, 1024).rearrange("p (s t) -> p s t", s=2)
            ys_res = [Res(scr_fence) for _ in range(2)]
            yv = yout[seq_i].rearrange("p (c t) -> p c t", c=8)

            def final_norm_tb(tb):
                tsl = slice(tb * 512, (tb + 1) * 512)
                bank = pj_bank()
                for c in range(8):
                    q = cnt["sq"] % 2
                    cnt["sq"] += 1
                    op(ACT, lambda c=c, q=q: nc.scalar.activation(out=SQ[:, q, :], in_=xT[:, c, tsl], func=AF.Square),
                       reads=[xT_res[c][tb]], writes=[sq_res[q]])
                    op(PE, lambda c=c, q=q: nc.tensor.matmul(PS[bank][:, :], ONES[:, :], SQ[:, q, :], start=(c == 0),
                                                             stop=(c == 7)),
                       reads=[sq_res[q]], writes=[ps_res[bank]], sig=True)
                r = cnt["rstd"] % 2
                cnt["rstd"] += 1
                op(ACT, lambda: nc.scalar.activation(out=RSTD[:, r, :], in_=PS[bank][:, :], func=AF.Ln,
                                                     bias=EPST[:, 0:1], scale=1.0 / 1024.0),
                   reads=[ps_res[bank]], writes=[rstd_res[r]])
                op(ACT, lambda: nc.scalar.activation(out=RSTD[:, r, :], in_=RSTD[:, r, :], func=AF.Exp, scale=-0.5),
                   reads=[rstd_res[r]], writes=[rstd_res[r]])
                for c in range(8):
                    y = cnt["y"] % 2
                    cnt["y"] += 1
                    op(DVE, lambda c=c, y=y: nc.vector.scalar_tensor_tensor(
                        out=YS[:, y, :], in0=xT[:, c, tsl], scalar=GAINS[:, 32 + c:33 + c], in1=RSTD[:, r, :],
                        op0=ALU.mult, op1=ALU.mult),
                       reads=[xT_res[c][tb], rstd_res[r]], writes=[ys_res[y]])
                    dma(SP, y_sem[y], yv[:, c, tsl], YS[:, y, :], reads=[ys_res[y]])

            return final_norm_tb

        for si in range(n_seq):
            xv = xin[si].rearrange("p (c t) -> p c t", c=8)
            for tb in range(4):
                tsl = slice(tb * 512, (tb + 1) * 512)
                dma(SP, x_sem[tb], xT[:, :, tsl], xv[:, :, tsl], writes=[xT_res[c][tb] for c in range(8)])
            phases = []
            if 0 in layers:
                phases += ["L0", "F0"]
            if 1 in layers:
                phases += ["L1", "F1"]
            nidx = {"L0": 0, "F0": 2, "L1": 1, "F1": 3}
            pre = False
            for pi, ph in enumerate(phases):
                f_ = fence()
                if pi + 1 < len(phases):
                    nn = (lambda tb, n=nidx[phases[pi + 1]]: norm_to_A(n, tb))
                else:
                    nn = make_final_norm(si, f_)
                if ph == "L0":
                    layer0(f_, next_norm=nn)
                elif ph == "L1":
                    layer1(f_, pre_normed=pre, next_norm=nn)
                else:
                    ffn(int(ph[1]), f_, pre_normed=pre, next_norm=nn)
                pre = True
        for y in range(2):
            SP.wait(Tok(y_sem[y], y_sem[y].cnt))
        for e in (PE, ACT, DVE, POOL):
            pass
    return nc


_PROG_CACHE = {}


def kernel(x, norm_mix, norm_ffn, w_qkv_a, w_out_a, sink_a, w_qkv_b, w_out_b, w_gate, w_up, w_down, norm_final):
    x = np.asarray(x, np.float32)
    B, S, Dm = x.shape
    wts = _pack_weights(*(np.asarray(w, np.float32) for w in (w_qkv_a, w_out_a, w_qkv_b, w_out_b, w_gate, w_up, w_down)))
    gl = [np.asarray(norm_mix, np.float32)[0], np.asarray(norm_mix, np.float32)[1],
          np.asarray(norm_ffn, np.float32)[0], np.asarray(norm_ffn, np.float32)[1],
          np.asarray(norm_final, np.float32)]
    gains = np.stack([g.reshape(8, 128).T for g in gl], axis=1).reshape(128, 40).copy()
    sk = np.asarray(sink_a, np.float32)[0]
    sink = np.empty((128, 8), np.float32)
    for p in range(2):
        for a in range(4):
            sink[0:64, 4 * p + a] = sk[8 * p + a]
            sink[64:128, 4 * p + a] = sk[8 * p + 4 + a]
    xt = np.ascontiguousarray(x.reshape(B, S, 8, 128).transpose(0, 3, 2, 1)).reshape(B, 128, 8 * S)
    if "nc" not in _PROG_CACHE:
        _PROG_CACHE["nc"] = build_program()
    nc = _PROG_CACHE["nc"]
    in_maps = []
    for i in range(N_CORES):
        in_maps.append({"xin": xt[SEQ_PER_CORE * i:SEQ_PER_CORE * (i + 1)], "wts": wts, "gains": gains, "sink": sink})
    res = run_bass_kernel_spmd(nc, in_maps, core_ids=list(range(N_CORES)))
    ys = np.concatenate([np.asarray(r["yout"]) for r in res.results], axis=0)
    out = ys.reshape(B, 128, 8, S).transpose(0, 3, 2, 1).reshape(B, S, Dm)
    return np.ascontiguousarray(out.astype(np.float32))
```

```python
import numpy as np
import concourse.bass as bass
import concourse.mybir as mybir
from concourse.bass_utils import run_bass_kernel_spmd

F32 = mybir.dt.float32
BF16 = mybir.dt.bfloat16
I32 = mybir.dt.int32
AF = mybir.ActivationFunctionType
ALU = mybir.AluOpType

T = 2048
KC = 8
NF = 22
NSLOT = 12
FSPL = [(0, 8), (8, 15), (15, 22)]
BIG = 30000.0
EPS = 1e-6
N_CORES = 8
SEQ_PER_CORE = 2
GROUPS = ((128, 1), (512, 4), (2048, 16))


def _unit_lists(layers=(0, 1)):
    ring = []
    if 0 in layers:
        for c in range(8):
            ring.append(("qa", c))
        for p in range(2):
            ring.append(("ka", p))
        for v in range(2):
            ring.append(("va", v))
        for kc in range(8):
            ring.append(("oa", kc))
        for (f0, f1) in FSPL:
            for f in range(f0, f1):
                ring.append(("g", 0, f))
                ring.append(("u", 0, f))
    if 1 in layers:
        items = [(j, g) for j in range(8) for g in range(3)]
        for s in range(3):
            ring.append(("qkvb", 0, 0, s))
        for i, (j, g) in enumerate(items):
            if g == 1 and j >= 2 and j % 2 == 0:
                ring.append(("ob", j - 2))
                ring.append(("ob", j - 1))
            if i + 1 < len(items):
                for s in range(3):
                    ring.append(("qkvb", items[i + 1][0], items[i + 1][1], s))
        ring.append(("ob", 6))
        ring.append(("ob", 7))
        for (f0, f1) in FSPL:
            for f in range(f0, f1):
                ring.append(("g", 1, f))
                ring.append(("u", 1, f))
    wd = []
    for l in range(2):
        for f in range(NF):
            wd.append(("d", l, f))
    return ring, wd


def _pack_weights(w_qkv_a, w_out_a, w_qkv_b, w_out_b, w_gate, w_up, w_down, layers=(0, 1)):
    ring, wd = _unit_lists(layers)
    keys = ring + wd
    out = np.empty((len(keys), 128, 1024), np.float32)

    def proj_unit(W, cols):
        return W[:, cols].reshape(8, 128, 128).transpose(1, 0, 2).reshape(128, 1024)

    qa_rows = {}
    for i, k in enumerate(keys):
        kind = k[0]
        if kind == "qa":
            c = k[1]
            p, a = c // 4, c % 4
            h0, h1 = 8 * p + a, 8 * p + 4 + a
            cols = np.concatenate([np.arange(64 * h0, 64 * h0 + 64), np.arange(64 * h1, 64 * h1 + 64)])
            out[i] = proj_unit(w_qkv_a[0], cols)
        elif kind == "ka":
            p = k[1]
            out[i] = proj_unit(w_qkv_a[0], np.arange(1024 + 128 * p, 1024 + 128 * p + 128))
        elif kind == "va":
            v = k[1]
            out[i] = proj_unit(w_qkv_a[0], np.arange(1280 + 128 * v, 1280 + 128 * v + 128))
        elif kind == "oa":
            c = k[1]
            p, a = c // 4, c % 4
            h0, h1 = 8 * p + a, 8 * p + 4 + a
            rows = np.concatenate([np.arange(64 * h0, 64 * h0 + 64), np.arange(64 * h1, 64 * h1 + 64)])
            out[i] = w_out_a[0][rows, :]
        elif kind == "g":
            _, l, f = k
            out[i] = proj_unit(w_gate[l], np.arange(128 * f, 128 * f + 128))
        elif kind == "u":
            _, l, f = k
            out[i] = proj_unit(w_up[l], np.arange(128 * f, 128 * f + 128))
        elif kind == "qkvb":
            _, j, g, s = k
            base = (3 * g + s) * 1024 + 128 * j
            out[i] = proj_unit(w_qkv_b[0], np.arange(base, base + 128))
        elif kind == "ob":
            kc = k[1]
            out[i] = w_out_b[0][128 * kc:128 * kc + 128, :]
        elif kind == "d":
            _, l, f = k
            out[i] = w_down[l][128 * f:128 * f + 128, :]
        else:
            raise AssertionError(kind)
    return out


class Sem:
    def __init__(self, h):
        self.h = h
        self.cnt = 0


class Tok:
    __slots__ = ("sem", "val")

    def __init__(self, sem, val):
        self.sem = sem
        self.val = val


class Res:
    __slots__ = ("w", "r")

    def __init__(self, fence=None):
        self.w = None
        self.r = list(fence) if fence else []


class Eng:
    def __init__(self, e, sem, is_pe=False):
        self.e = e
        self.sem = sem
        self.is_pe = is_pe
        self.seen = {}
        self.pend = None
        self.last = None

    def wait(self, tok):
        assert tok.val is not None, "unresolved token"
        if self.seen.get(tok.sem, 0) >= tok.val:
            return
        self.e.wait_ge(tok.sem.h, tok.val)
        self.seen[tok.sem] = tok.val


def _add_reader(res, tok):
    if res.r and res.r[-1] is tok:
        return
    if res.r:
        l = res.r[-1]
        if l.sem is tok.sem and l.val is not None and tok.val is not None and tok.val >= l.val:
            res.r[-1] = tok
            return
    res.r.append(tok)


def op(eng, fn, reads=(), writes=(), sig=True):
    deps = []
    for r in reads:
        if r.w is not None:
            deps.append(r.w)
    for w in writes:
        if w.w is not None:
            deps.append(w.w)
        deps.extend(w.r)
    for t in deps:
        if eng.is_pe and t.sem is eng.sem:
            continue
        eng.wait(t)
    ins = fn()
    if sig:
        ins.then_inc(eng.sem.h, 1)
        eng.sem.cnt += 1
        tok = Tok(eng.sem, eng.sem.cnt)
        if eng.pend is not None:
            eng.pend.val = tok.val
            eng.pend = None
    else:
        if eng.pend is None:
            eng.pend = Tok(eng.sem, None)
        tok = eng.pend
    eng.last = tok
    for r in reads:
        _add_reader(r, tok)
    for w in writes:
        w.w = tok
        w.r = []
    return tok


def dma(eng, dsem, out, in_, reads=(), writes=()):
    deps = []
    for r in reads:
        if r.w is not None:
            deps.append(r.w)
    for w in writes:
        if w.w is not None:
            deps.append(w.w)
        deps.extend(w.r)
    for t in deps:
        eng.wait(t)
    eng.e.dma_start(out=out, in_=in_).then_inc(dsem.h, 16)
    dsem.cnt += 16
    tok = Tok(dsem, dsem.cnt)
    for r in reads:
        _add_reader(r, tok)
    for w in writes:
        w.w = tok
        w.r = []
    return tok


def build_program(n_seq=SEQ_PER_CORE, layers=(0, 1), dbg=None):
    dbg = dbg or {}
    ring_keys, wd_keys = _unit_lists(layers)
    n_ring = len(ring_keys)
    n_units = n_ring + len(wd_keys)
    wd_index = {k: n_ring + i for i, k in enumerate(wd_keys)}

    nc = bass.Bass("TRN2", target_bir_lowering=False)
    xin = nc.dram_tensor("xin", [n_seq, 128, 8 * T], F32, kind="ExternalInput").ap()
    wts = nc.dram_tensor("wts", [n_units, 128, 1024], F32, kind="ExternalInput").ap()
    gains_d = nc.dram_tensor("gains", [128, 40], F32, kind="ExternalInput").ap()
    sink_d = nc.dram_tensor("sink", [128, 8], F32, kind="ExternalInput").ap()
    yout = nc.dram_tensor("yout", [n_seq, 128, 8 * T], F32, kind="ExternalOutput").ap()

    from contextlib import ExitStack
    with ExitStack() as es:
        def sb(name, shape, dt):
            return es.enter_context(nc.sbuf_tensor(name, shape, dt))

        def mksem(name):
            return Sem(es.enter_context(nc.semaphore(name)))

        xT = sb("xT", [128, 8, T], F32)
        A = sb("A", [128, 8, T], BF16)
        SCR = sb("SCR", [128, 30720], BF16)
        SWAP = sb("SWAP", [128, 128], F32)
        RD = sb("RD", [128, 512], F32)
        RDXY = sb("RDXY", [128, 2, 2, 256], BF16)
        RING = sb("RING", [128, NSLOT, 1024], BF16)
        IDT = sb("IDT", [128, 24, 128], BF16)
        ONES = sb("ONES", [128, 128], BF16)
        D1 = sb("D1", [128, 256], BF16)
        D0 = sb("D0", [128, 3, 128], BF16)
        GAINS = sb("GAINS", [128, 40], F32)
        ESINK = sb("ESINK", [128, 8], F32)
        EPST = sb("EPST", [128, 1], F32)
        SQ = sb("SQ", [128, 2, 512], BF16)
        RSTD = sb("RSTD", [128, 2, 512], F32)
        PT = sb("PT", [128, 3, 512], BF16)
        SG = sb("SG", [128, 2, 512], BF16)
        PS = [es.enter_context(nc.psum_tensor(f"ps{i}", [128, 512], F32)) for i in range(8)]

        PE = Eng(nc.tensor, mksem("s_pe"), is_pe=True)
        ACT = Eng(nc.scalar, mksem("s_act"))
        DVE = Eng(nc.vector, mksem("s_dve"))
        POOL = Eng(nc.gpsimd, mksem("s_pool"))
        SP = Eng(nc.sync, mksem("s_sp"))
        ENGS = [PE, ACT, DVE, POOL, SP]
        slot_sem = [mksem(f"s_slot{i}") for i in range(NSLOT)]
        wd_sem = [mksem(f"s_wd{i}") for i in range(8)]
        x_sem = [mksem(f"s_x{i}") for i in range(8)]
        y_sem = [mksem(f"s_y{i}") for i in range(2)]
        c_sem = mksem("s_const")

        ps_res = [Res() for _ in range(8)]
        slot_res = [Res() for _ in range(NSLOT)]
        xT_res = [[Res() for _ in range(4)] for _ in range(8)]
        A_res = [[Res() for _ in range(4)] for _ in range(8)]
        sq_res = [Res() for _ in range(2)]
        rstd_res = [Res() for _ in range(2)]
        pt_res = [Res() for _ in range(3)]
        sg_res = [Res() for _ in range(2)]
        rd_res = Res()
        rrh_res = rd_res
        cnt = {"sq": 0, "rstd": 0, "pt": 0, "sg": 0, "y": 0}

        def fence():
            toks = []
            for e in ENGS:
                if e.last is not None:
                    assert e.last.val is not None
                    toks.append(e.last)
            for s_ in wd_sem + y_sem:
                if s_.cnt > 0:
                    toks.append(Tok(s_, s_.cnt))
            return toks

        def scr_f32(off_bf, n_f32):
            return SCR[:, off_bf:off_bf + 2 * n_f32].bitcast(F32)

        cres = Res()
        tok_g = dma(SP, c_sem, GAINS[:], gains_d[:, :], writes=[cres])
        tok_s = dma(SP, c_sem, ESINK[:], sink_d[:, :], writes=[cres])
        tok_s = Tok(c_sem, c_sem.cnt)
        ACT.wait(tok_s)
        es_res = Res()
        op(ACT, lambda: nc.scalar.activation(out=ESINK[:], in_=ESINK[:], func=AF.Exp), writes=[es_res])
        k_res = Res()
        op(DVE, lambda: nc.vector.memset(ONES[:], 1.0), writes=[k_res])
        op(DVE, lambda: nc.vector.memset(EPST[:], EPS), writes=[k_res])
        io_id = SCR[:, 0:256].bitcast(I32)
        io_d1 = SCR[:, 256:768].bitcast(I32)
        io_d0 = SCR[:, 768:1536].bitcast(I32)
        f_a = scr_f32(1536, 384)
        f_b = scr_f32(2304, 384)
        f_c = scr_f32(3072, 384)
        io_res = Res()
        op(POOL, lambda: nc.gpsimd.iota(io_id, pattern=[[1, 128]], base=0, channel_multiplier=-1), writes=[io_res])
        op(POOL, lambda: nc.gpsimd.iota(io_d1, pattern=[[-1, 256]], base=64, channel_multiplier=1), writes=[io_res])
        op(POOL, lambda: nc.gpsimd.iota(io_d0.rearrange("p (a b) -> p a b", a=3), pattern=[[128, 3], [-1, 128]],
                                        base=-128, channel_multiplier=1), writes=[io_res])
        tmp_res = Res()
        op(DVE, lambda: nc.vector.tensor_scalar(out=f_a[:, 0:128], in0=io_id, scalar1=0.0, scalar2=None,
                                                op0=ALU.is_equal), reads=[io_res], writes=[tmp_res])
        for k in range(24):
            m = k - 7
            val = 8.0 * (2.0 ** (-m / 2.0))
            op(DVE, lambda k=k, val=val: nc.vector.tensor_scalar(out=IDT[:, k, :], in0=f_a[:, 0:128], scalar1=val,
                                                                 scalar2=None, op0=ALU.mult),
               reads=[tmp_res], writes=[k_res])

        op(DVE, lambda: nc.vector.tensor_scalar(out=f_b[:, 0:128], in0=io_id, scalar1=64.0, scalar2=None,
                                                op0=ALU.is_equal), reads=[io_res], writes=[tmp_res])
        op(DVE, lambda: nc.vector.tensor_scalar(out=f_c[:, 0:128], in0=io_id, scalar1=-64.0, scalar2=None,
                                                op0=ALU.is_equal), reads=[io_res], writes=[tmp_res])
        op(DVE, lambda: nc.vector.tensor_tensor(out=SWAP[:, :], in0=f_b[:, 0:128], in1=f_c[:, 0:128], op=ALU.add),
           reads=[tmp_res], writes=[k_res])

        def build_dist(io, n, thr, dst):
            op(DVE, lambda: nc.vector.tensor_scalar(out=f_a[:, 0:n], in0=io, scalar1=1.0, scalar2=None,
                                                    op0=ALU.mult), reads=[io_res, tmp_res], writes=[tmp_res])
            op(DVE, lambda: nc.vector.tensor_scalar(out=f_b[:, 0:n], in0=io, scalar1=-1.0, scalar2=None,
                                                    op0=ALU.mult), reads=[io_res, tmp_res], writes=[tmp_res])
            op(DVE, lambda: nc.vector.tensor_tensor(out=f_a[:, 0:n], in0=f_a[:, 0:n], in1=f_b[:, 0:n],
                                                    op=ALU.max), reads=[tmp_res], writes=[tmp_res])
            op(DVE, lambda: nc.vector.tensor_scalar(out=f_b[:, 0:n], in0=f_a[:, 0:n], scalar1=float(thr),
                                                    scalar2=None, op0=ALU.is_gt), reads=[tmp_res], writes=[tmp_res])
            op(DVE, lambda: nc.vector.tensor_scalar(out=f_c[:, 0:n], in0=f_a[:, 0:n], scalar1=-1.0, scalar2=None,
                                                    op0=ALU.mult), reads=[tmp_res], writes=[tmp_res])
            op(DVE, lambda: nc.vector.scalar_tensor_tensor(out=dst, in0=f_b[:, 0:n], scalar=-BIG, in1=f_c[:, 0:n],
                                                           op0=ALU.mult, op1=ALU.add),
               reads=[tmp_res], writes=[k_res])

        build_dist(io_d1, 256, 64, D1[:])
        build_dist(io_d0, 384, 128, D0[:].rearrange("p a b -> p (a b)"))
        for e in (PE, ACT, DVE):
            e.wait(DVE.last)
            e.wait(ACT.last)
            e.wait(Tok(c_sem, c_sem.cnt))

        ws = {"dma": 0, "use": 0, "total": n_seq * n_ring}

        def ws_pump():
            while ws["dma"] < ws["total"] and ws["dma"] < ws["use"] + NSLOT:
                n = ws["dma"]
                s = n % NSLOT
                dma(POOL, slot_sem[s], RING[:, s, :], wts[n % n_ring], writes=[slot_res[s]])
                ws["dma"] += 1

        def ws_take(key):
            n = ws["use"]
            assert ring_keys[n % n_ring] == key, (ring_keys[n % n_ring], key)
            assert ws["dma"] > n
            s = n % NSLOT
            return s

        def ws_release(count=1):
            ws["use"] += count
            ws_pump()

        ws_pump()

        def slot_proj(s, kc):
            return RING[:, s, kc * 128:(kc + 1) * 128]

        pj = {"i": 0, "banks": [6, 7, 0, 1, 2, 3, 4, 5]}

        def pj_bank():
            b = pj["banks"][pj["i"] % len(pj["banks"])]
            pj["i"] += 1
            return b

        def set_pj(banks):
            pj["banks"] = list(banks)
            pj["i"] = 0

        def rmsnorm(n_idx, tb, dst_fn, dst_res_fn, final=False):
            tsl = slice(tb * 512, (tb + 1) * 512)
            bank = pj_bank()
            for c in range(8):
                q = cnt["sq"] % 2
                cnt["sq"] += 1
                op(ACT, lambda c=c, q=q: nc.scalar.activation(out=SQ[:, q, :], in_=xT[:, c, tsl], func=AF.Square),
                   reads=[xT_res[c][tb]], writes=[sq_res[q]])
                op(PE, lambda c=c, q=q: nc.tensor.matmul(PS[bank][:, :], ONES[:, :], SQ[:, q, :], start=(c == 0),
                                                         stop=(c == 7)),
                   reads=[sq_res[q]], writes=[ps_res[bank]], sig=True)
            r = cnt["rstd"] % 2
            cnt["rstd"] += 1
            op(ACT, lambda: nc.scalar.activation(out=RSTD[:, r, :], in_=PS[bank][:, :], func=AF.Ln, bias=EPST[:, 0:1],
                                                 scale=1.0 / 1024.0),
               reads=[ps_res[bank]], writes=[rstd_res[r]])
            op(ACT, lambda: nc.scalar.activation(out=RSTD[:, r, :], in_=RSTD[:, r, :], func=AF.Exp, scale=-0.5),
               reads=[rstd_res[r]], writes=[rstd_res[r]])
            for c in range(8):
                op(DVE, lambda c=c: nc.vector.scalar_tensor_tensor(out=dst_fn(c), in0=xT[:, c, tsl],
                                                                   scalar=GAINS[:, n_idx * 8 + c:n_idx * 8 + c + 1],
                                                                   in1=RSTD[:, r, :], op0=ALU.mult, op1=ALU.mult),
                   reads=[xT_res[c][tb], rstd_res[r]], writes=[dst_res_fn(c)])

        def norm_to_A(n_idx, tb):
            tsl = slice(tb * 512, (tb + 1) * 512)
            rmsnorm(n_idx, tb, lambda c: A[:, c, tsl], lambda c: A_res[c][tb])

        def proj_fm(s, tb, dst, dst_res, last_use, evac_eng=None):
            bank = pj_bank()
            tsl = slice(tb * 512, (tb + 1) * 512)
            for kc in range(8):
                op(PE, lambda kc=kc: nc.tensor.matmul(PS[bank][:, :], slot_proj(s, kc), A[:, kc, tsl],
                                                      start=(kc == 0), stop=(kc == 7)),
                   reads=[slot_res[s], A_res[kc][tb]], writes=[ps_res[bank]], sig=(kc == 7))
            if evac_eng is not None:
                evac_eng(bank)
            else:
                op(DVE, lambda: nc.vector.tensor_copy(out=dst, in_=PS[bank][:, :]), reads=[ps_res[bank]],
                   writes=[dst_res])

        def resid_add(bank, oc, tb):
            tsl = slice(tb * 512, (tb + 1) * 512)
            op(DVE, lambda: nc.vector.tensor_tensor(out=xT[:, oc, tsl], in0=PS[bank][:, :], in1=xT[:, oc, tsl],
                                                    op=ALU.add),
               reads=[ps_res[bank], xT_res[oc][tb]], writes=[xT_res[oc][tb]])

        def ffn(l, scr_fence, pre_normed=False, next_norm=None):
            aT = SCR[:, 0:16384].rearrange("p (f t) -> p f t", f=8)
            WD = SCR[:, 16384:24576].rearrange("p (f c) -> p f c", f=8)
            aT_res = [[Res(scr_fence) for _ in range(4)] for _ in range(8)]
            wd_res = [Res(scr_fence) for _ in range(8)]
            set_pj([6, 7, 0, 1, 2, 3, 4, 5])
            if not pre_normed:
                for tb in range(4):
                    norm_to_A(2 + l, tb)
            for (f0, f1) in FSPL:
                nf = f1 - f0
                for fi in range(nf):
                    dma(POOL, wd_sem[fi], WD[:, fi, :], wts[wd_index[("d", l, f0 + fi)]], writes=[wd_res[fi]])
                for fi in range(nf):
                    f = f0 + fi
                    sg_ = ws_take(("g", l, f))
                    su_ = (ws["use"] + 1) % NSLOT
                    assert ring_keys[(ws["use"] + 1) % n_ring] == ("u", l, f)
                    for tb in range(4):
                        tsl = slice(tb * 512, (tb + 1) * 512)
                        bg = pj_bank()
                        bu = pj_bank()
                        for kc in range(8):
                            op(PE, lambda kc=kc: nc.tensor.matmul(PS[bg][:, :], slot_proj(sg_, kc), A[:, kc, tsl],
                                                                  start=(kc == 0), stop=(kc == 7)),
                               reads=[slot_res[sg_], A_res[kc][tb]], writes=[ps_res[bg]], sig=(kc == 7))
                        for kc in range(8):
                            op(PE, lambda kc=kc: nc.tensor.matmul(PS[bu][:, :], slot_proj(su_, kc), A[:, kc, tsl],
                                                                  start=(kc == 0), stop=(kc == 7)),
                               reads=[slot_res[su_], A_res[kc][tb]], writes=[ps_res[bu]], sig=(kc == 7))
                        q = cnt["sg"] % 2
                        cnt["sg"] += 1
                        op(ACT, lambda q=q: nc.scalar.activation(out=SG[:, q, :], in_=PS[bg][:, :], func=AF.Silu),
                           reads=[ps_res[bg]], writes=[sg_res[q]])
                        op(DVE, lambda q=q: nc.vector.tensor_tensor(out=aT[:, fi, tsl], in0=PS[bu][:, :],
                                                                    in1=SG[:, q, :], op=ALU.mult),
                           reads=[ps_res[bu], sg_res[q]], writes=[aT_res[fi][tb]])
                    ws_release(2)
                for tb in range(4):
                    tsl = slice(tb * 512, (tb + 1) * 512)
                    for oc in range(8):
                        bank = pj_bank()
                        for fi in range(nf):
                            op(PE, lambda fi=fi: nc.tensor.matmul(PS[bank][:, :], WD[:, fi, oc * 128:(oc + 1) * 128],
                                                                  aT[:, fi, tsl], start=(fi == 0), stop=(fi == nf - 1)),
                               reads=[wd_res[fi], aT_res[fi][tb]], writes=[ps_res[bank]], sig=(fi == nf - 1))
                        resid_add(bank, oc, tb)
                    if next_norm is not None and (f0, f1) == FSPL[-1] and tb >= 1:
                        next_norm(tb - 1)
                if next_norm is not None and (f0, f1) == FSPL[-1]:
                    next_norm(3)

        def layer0(scr_fence, next_norm=None):
            QK = SCR[:, 0:20480].rearrange("p (c t) -> p c t", c=10)
            V0 = SCR[:, 20480:24576].rearrange("p (n v) -> p n v", n=16)
            qk_res = [[Res(scr_fence) for _ in range(4)] for _ in range(10)]
            v_res = [Res(scr_fence) for _ in range(16)]
            set_pj([6, 7, 0, 1, 2, 3, 4, 5])
            slots = []
            for i in range(12):
                assert ring_keys[(ws["use"] + i) % n_ring] == (("qa", i) if i < 8 else (("ka", i - 8) if i < 10 else ("va", i - 10)))
                slots.append((ws["use"] + i) % NSLOT)
            for tb in range(4):
                norm_to_A(0, tb)
                tsl = slice(tb * 512, (tb + 1) * 512)
                for oc in range(10):
                    proj_fm(slots[oc], tb, QK[:, oc, tsl], qk_res[oc][tb], False)
                for tt in range(4):
                    n = tb * 4 + tt
                    bank = pj_bank()
                    first = True
                    for vu in range(2):
                        s = slots[10 + vu]
                        for kc in range(8):
                            op(PE, lambda kc=kc, s=s, vu=vu, first=first: nc.tensor.matmul(
                                PS[bank][:, vu * 128:(vu + 1) * 128], A[:, kc, n * 128:(n + 1) * 128], slot_proj(s, kc),
                                start=(kc == 0), stop=(kc == 7), skip_group_check=True),
                               reads=[slot_res[s], A_res[kc][tb]], writes=[ps_res[bank]], sig=(kc == 7))
                            first = False
                    op(DVE, lambda n=n: nc.vector.tensor_copy(out=V0[:, n, :], in_=PS[bank][:, 0:256]),
                       reads=[ps_res[bank]], writes=[v_res[n]])
            ws_release(12)
            set_pj([6, 7])
            s_banks = [0, 1]
            od_banks = [(2, 3), (4, 5)]
            items = []
            for b in range(16):
                for p in range(2):
                    tiles = []
                    for gl in range(2):
                        for c in (b - 1, b, b + 1):
                            if 0 <= c < 16:
                                tiles.append((gl, c))
                    for i, (gl, c) in enumerate(tiles):
                        items.append((b, p, gl, c, i == 0, i == len(tiles) - 1))
            LA = 1
            n_items = len(items)
            grp = {"k": 0}

            def emit_S(i):
                b, p, gl, c, _, _ = items[i]
                bank = s_banks[i % 2]
                rows = slice(gl * 64, gl * 64 + 64)
                typ = c - b + 1
                for a in range(4):
                    h = 8 * p + 4 * gl + a
                    k = (h + 1) + 7
                    op(PE, lambda a=a, k=k: nc.tensor.matmul(PS[bank][:, a * 128:(a + 1) * 128], IDT[:, k, :],
                                                             D0[:, typ, :], start=(a == 0), stop=False,
                                                             skip_group_check=True),
                       writes=[ps_res[bank]], sig=False)
                tb = b // 4
                tbk = c // 4
                op(PE, lambda: nc.tensor.matmul(PS[bank][:, :], QK[rows, 8 + p, c * 128:(c + 1) * 128],
                                                QK[rows, 4 * p:4 * p + 4, b * 128:(b + 1) * 128],
                                                start=False, stop=True, skip_group_check=True),
                   reads=[qk_res[8 + p][tbk]] + [qk_res[4 * p + a][tb] for a in range(4)], writes=[ps_res[bank]],
                   sig=True)

            def emit_rest(i):
                b, p, gl, c, first, last = items[i]
                bank = s_banks[i % 2]
                q = cnt["pt"] % 3
                cnt["pt"] += 1
                op(ACT, lambda: nc.scalar.activation(out=PT[:, q, :], in_=PS[bank][:, :], func=AF.Exp, scale=0.125),
                   reads=[ps_res[bank]], writes=[pt_res[q]])
                if first:
                    grp["k"] += 1
                ob, db = od_banks[grp["k"] % 2]
                rows = slice(gl * 64, gl * 64 + 64)
                cs = [cc for cc in (b - 1, b, b + 1) if 0 <= cc < 16]
                st = (c == cs[0])
                sp = (c == cs[-1])
                g = 2 * p + gl
                op(PE, lambda: nc.tensor.matmul(PS[ob][rows, :], V0[:, c, g * 64:(g + 1) * 64], PT[:, q, :],
                                                start=st, stop=sp, skip_group_check=True),
                   reads=[v_res[c], pt_res[q]], writes=[ps_res[ob]], sig=False)
                op(PE, lambda: nc.tensor.matmul(PS[db][rows, :], ONES[:, 0:64], PT[:, q, :],
                                                start=st, stop=sp, skip_group_check=True),
                   reads=[pt_res[q]], writes=[ps_res[db]], sig=True)
                if last:
                    tb = b // 4
                    for a in range(4):
                        op(DVE, lambda a=a: nc.vector.tensor_scalar(out=RD[:, a * 128:(a + 1) * 128],
                                                                    in0=PS[db][:, a * 128:(a + 1) * 128],
                                                                    scalar1=ESINK[:, 4 * p + a:4 * p + a + 1],
                                                                    scalar2=None, op0=ALU.add),
                           reads=[ps_res[db]], writes=[rd_res])
                    op(DVE, lambda: nc.vector.reciprocal(out=RD[:, :], in_=RD[:, :]), reads=[rd_res], writes=[rd_res])
                    op(DVE, lambda: nc.vector.tensor_tensor(
                        out=A[:, 4 * p:4 * p + 4, b * 128:(b + 1) * 128],
                        in0=PS[ob][:, :].rearrange("p (a q) -> p a q", a=4),
                        in1=RD[:, :].rearrange("p (a q) -> p a q", a=4), op=ALU.mult),
                       reads=[ps_res[ob], rd_res], writes=[A_res[4 * p + a][tb] for a in range(4)])

            for i in range(min(LA, n_items)):
                emit_S(i)
            for i in range(n_items):
                if i + LA < n_items:
                    emit_S(i + LA)
                emit_rest(i)
            set_pj([6, 7, 0, 1])
            oslots = []
            for kc in range(8):
                assert ring_keys[(ws["use"] + kc) % n_ring] == ("oa", kc)
                oslots.append((ws["use"] + kc) % NSLOT)
            for tb in range(4):
                tsl = slice(tb * 512, (tb + 1) * 512)
                for oc in range(8):
                    bank = pj_bank()
                    for kc in range(8):
                        s = oslots[kc]
                        op(PE, lambda kc=kc, s=s: nc.tensor.matmul(PS[bank][:, :], RING[:, s, oc * 128:(oc + 1) * 128],
                                                                   A[:, kc, tsl], start=(kc == 0), stop=(kc == 7)),
                           reads=[slot_res[s], A_res[kc][tb]], writes=[ps_res[bank]], sig=(kc == 7))
                    resid_add(bank, oc, tb)
                if next_norm is not None and tb >= 1:
                    next_norm(tb - 1)
            ws_release(8)
            if next_norm is not None:
                next_norm(3)

        def layer1(scr_fence, pre_normed=False, next_norm=None):
            Bh = SCR[:, 0:4096].rearrange("p (c t) -> p c t", c=2)
            QXY = SCR[:, 4096:12288].rearrange("p (s h t) -> p s h t", s=2, h=2)
            K1 = SCR[:, 12288:16384].rearrange("p (s t) -> p s t", s=2)
            V1 = SCR[:, 16384:22528].rearrange("p (s n v) -> p s n v", s=2, n=16)
            OACC = scr_f32(22528, 2048)
            DACC = scr_f32(26624, 2048)
            rdxy_res = [Res() for _ in range(2)]
            bh_res = [[Res(scr_fence) for _ in range(4)] for _ in range(2)]
            q_res = [Res(scr_fence) for _ in range(2)]
            k_res1 = [Res(scr_fence) for _ in range(2)]
            v_res = [[Res(scr_fence) for _ in range(4)] for _ in range(2)]
            acc_res = [Res(scr_fence) for _ in range(4)]
            acc_all = Res(scr_fence)
            set_pj([6, 7, 0, 1, 2, 3, 4, 5])
            for sl_ in range(2):
                op(DVE, lambda sl_=sl_: nc.vector.memset(V1[:, sl_, :, 64:128], 1.0),
                   writes=[v_res[sl_][n4_] for n4_ in range(4)])
                op(DVE, lambda sl_=sl_: nc.vector.memset(QXY[64:128, sl_, 0, :], 0.0), writes=[q_res[sl_]])
                op(DVE, lambda sl_=sl_: nc.vector.memset(QXY[0:64, sl_, 1, :], 0.0), writes=[q_res[sl_]])
            if not pre_normed:
                for tb in range(4):
                    norm_to_A(1, tb)
            set_pj([6, 7])
            s_banks = [0, 1]
            od_banks = [(2, 3), (4, 5)]
            st = {"s": 0, "od": 0}
            pgi = 0

            def projections(j, g, sl):
                dil = GROUPS[g][1]
                L = T // dil
                sq_ = ws_take(("qkvb", j, g, 0))
                sk_ = (ws["use"] + 1) % NSLOT
                sv_ = (ws["use"] + 2) % NSLOT
                for h_ in range(2):
                    val = 8.0 * (2.0 ** (-((2 * j + h_ + 1) - 4 * g) / 2.0))
                    op(DVE, lambda h_=h_, val=val: nc.vector.tensor_scalar(out=RDXY[:, sl, h_, :], in0=D1[:, :],
                                                                         scalar1=val, scalar2=None, op0=ALU.mult),
                       writes=[rdxy_res[sl]])
                for tb in range(4):
                    tsl = slice(tb * 512, (tb + 1) * 512)

                    def q_evac(bank, tsl=tsl):
                        op(DVE, lambda: nc.vector.tensor_copy(out=QXY[0:64, sl, 0, tsl], in_=PS[bank][0:64, :]),
                           reads=[ps_res[bank]], writes=[q_res[sl]])
                        op(DVE, lambda: nc.vector.tensor_copy(out=QXY[64:128, sl, 1, tsl], in_=PS[bank][64:128, :]),
                           reads=[ps_res[bank]], writes=[q_res[sl]])

                    proj_fm(sq_, tb, None, None, False, evac_eng=q_evac)
                    yield
                    proj_fm(sk_, tb, K1[:, sl, tsl], k_res1[sl], False)
                    yield
                for n4 in range(4):
                    bank = pj_bank()
                    for tt in range(4):
                        n = n4 * 4 + tt
                        pos = n * 128
                        r = pos // L
                        u0 = pos % L
                        t0 = r + dil * u0
                        tsel = slice(t0, t0 + dil * 127 + 1, dil)
                        for kc in range(8):
                            tbs = sorted({(t0 + dil * i) // 512 for i in (0, 127)})
                            rd = [slot_res[sv_]] + [A_res[kc][x] for x in range(tbs[0], tbs[-1] + 1)]
                            op(PE, lambda kc=kc, tt=tt, tsel=tsel: nc.tensor.matmul(
                                PS[bank][:, tt * 128:(tt + 1) * 128], A[:, kc, tsel], slot_proj(sv_, kc),
                                start=(kc == 0), stop=(kc == 7), skip_group_check=True),
                               reads=rd, writes=[ps_res[bank]], sig=(kc == 7))
                        if tt == 3:
                            op(DVE, lambda n4=n4: nc.vector.tensor_copy(
                                out=V1[:, sl, n4 * 4:(n4 + 1) * 4, :].rearrange("p n (h d) -> p n h d", h=3)[:, :, 0:3:2, :],
                                in_=PS[bank][:, :].rearrange("p (n h d) -> p n h d", n=4, h=2)),
                               reads=[ps_res[bank]], writes=[v_res[sl][n4]])
                        yield
                ws_release(3)

            def drain(gen):
                if gen is not None:
                    for _ in gen:
                        pass

            def attention(j, g, sl, filler=None, fill_n=2, prework=None, n_ch=0):
                win, dil = GROUPS[g]
                L = T // dil
                tot_tiles = 22 if dil == 1 else 16
                tctr = {"k": 0}
                pacc = {"fn": None}
                hX, hY = 2 * j, 2 * j + 1
                kX = (hX + 1 - 4 * g) + 7
                kY = (hY + 1 - 4 * g) + 7
                for B in range(4):
                    tiles = []
                    p0 = 512 * B
                    for r in range(dil):
                        lo = max(r * L, p0) - r * L
                        hi = min((r + 1) * L, p0 + 512) - r * L
                        if hi <= lo:
                            continue
                        for m in range(L // 128):
                            ulo = max(128 * m - 64, lo, 0)
                            uhi = min(128 * m + 192, hi, L)
                            if uhi > ulo:
                                tiles.append((r, m, ulo, uhi))
                    ob, db = od_banks[st["od"] % 2]
                    st["od"] += 1
                    nt = len(tiles)
                    written = set()

                    def emit_S(i):
                        r, m, ulo, uhi = tiles[i]
                        N = uhi - ulo
                        bank = s_banks[(st["s"] + i) % 2]
                        jlo = ulo - 128 * m + 64
                        k0 = r + dil * 128 * m
                        ksel = slice(k0, k0 + dil * 127 + 1, dil)
                        q0 = r + dil * ulo
                        qsel = slice(q0, q0 + dil * (N - 1) + 1, dil)
                        psv_ = PS[bank][:, :].rearrange("p (h n) -> p h n", h=2)
                        op(PE, lambda: nc.tensor.matmul(psv_[:, :, 0:N], IDT[:, 13, :], RDXY[:, sl, :, jlo:jlo + N],
                                                        start=True, stop=False, skip_group_check=True),
                           reads=[rdxy_res[sl]], writes=[ps_res[bank]], sig=False)
                        op(PE, lambda: nc.tensor.matmul(psv_[:, :, 0:N], K1[:, sl, ksel], QXY[:, sl, :, qsel],
                                                        start=False, stop=True, skip_group_check=True),
                           reads=[k_res1[sl], q_res[sl]], writes=[ps_res[bank]], sig=True)

                    def emit_rest(i):
                        r, m, ulo, uhi = tiles[i]
                        N = uhi - ulo
                        bank = s_banks[(st["s"] + i) % 2]
                        q = cnt["pt"] % 3
                        cnt["pt"] += 1
                        ptv = PT[:, q, :].rearrange("p (h n) -> p h n", h=2)
                        psv = PS[bank][:, :].rearrange("p (h n) -> p h n", h=2)
                        op(ACT, lambda: nc.scalar.activation(out=ptv[:, :, 0:N], in_=psv[:, :, 0:N], func=AF.Exp,
                                                             scale=0.125),
                           reads=[ps_res[bank]], writes=[pt_res[q]])
                        n = (r * L + 128 * m) // 128
                        vr = v_res[sl][n // 4]
                        if L == 128:
                            pieces = [(ulo, uhi)]
                        else:
                            b1 = 128 * m + 64
                            pieces = [(a_, e_) for (a_, e_) in ((ulo, min(uhi, b1)), (max(ulo, b1), uhi)) if e_ > a_]
                        for pi, (a_, e_) in enumerate(pieces):
                            c0 = r * L + a_ - p0
                            n_ = e_ - a_
                            o_ = a_ - ulo
                            fw = c0 not in written
                            written.add(c0)
                            lastp = (pi == len(pieces) - 1)
                            op(PE, lambda: nc.tensor.matmul(PS[ob][:, c0:c0 + n_], V1[:, sl, n, 0:128],
                                                            ptv[:, 0, o_:o_ + n_], start=fw, stop=True,
                                                            skip_group_check=True),
                               reads=[vr, pt_res[q]], writes=[ps_res[ob]], sig=False)
                            op(PE, lambda: nc.tensor.matmul(PS[db][:, c0:c0 + n_], V1[:, sl, n, 64:192],
                                                            ptv[:, 1, o_:o_ + n_], start=fw, stop=True,
                                                            skip_group_check=True),
                               reads=[vr, pt_res[q]], writes=[ps_res[db]], sig=lastp)

                    emit_S(0)
                    for i in range(nt):
                        if i + 1 < nt:
                            emit_S(i + 1)
                        if prework is not None:
                            next(prework, None)
                            next(prework, None)
                        if filler is not None:
                            k_ = tctr["k"]
                            tctr["k"] += 1
                            npull = (n_ch * (k_ + 1)) // tot_tiles - (n_ch * k_) // tot_tiles
                            for _ in range(npull):
                                next(filler, None)
                        emit_rest(i)
                        if i == 0 and pacc["fn"] is not None:
                            pacc["fn"]()
                            pacc["fn"] = None
                    st["s"] += nt
                    if prework is not None:
                        drain(prework)
                        prework = None
                    def acc_fn(B=B, ob=ob, db=db):
                        for (acc, bank) in ((OACC, ob), (DACC, db)):
                            if dil == 1:
                                dst = acc[:, 512 * B:512 * B + 512]
                                src = PS[bank][:, :]
                            elif dil == 4:
                                dst = acc.rearrange("p (u w) -> p w u", w=4)[:, B, :]
                                src = PS[bank][:, :]
                            else:
                                dst = acc.rearrange("p (u w) -> p w u", w=16)[:, 4 * B:4 * B + 4, :]
                                src = PS[bank][:, :].rearrange("p (w u) -> p w u", w=4)
                            if g == 0:
                                op(DVE, lambda dst=dst, src=src: nc.vector.tensor_copy(out=dst, in_=src),
                                   reads=[ps_res[bank]], writes=[acc_all])
                            else:
                                op(DVE, lambda dst=dst, src=src: nc.vector.tensor_tensor(out=dst, in0=src, in1=dst,
                                                                                         op=ALU.add),
                                   reads=[ps_res[bank], acc_all], writes=[acc_all])
                    pacc["fn"] = acc_fn
                if pacc["fn"] is not None:
                    pacc["fn"]()
                    pacc["fn"] = None

            def finalize(j):
                jj = j % 2
                for tb in range(4):
                    tsl = slice(tb * 512, (tb + 1) * 512)
                    r = cnt["rstd"] % 2
                    cnt["rstd"] += 1
                    op(ACT, lambda: nc.scalar.activation(out=RSTD[0:64, r, :], in_=DACC[0:64, tsl], func=AF.Ln),
                       reads=[acc_all], writes=[rstd_res[r]])
                    op(ACT, lambda: nc.scalar.activation(out=RSTD[64:128, r, :], in_=OACC[64:128, tsl], func=AF.Ln),
                       reads=[acc_all], writes=[rstd_res[r]])
                    op(ACT, lambda: nc.scalar.activation(out=RSTD[:, r, :], in_=RSTD[:, r, :], func=AF.Exp, scale=-1.0),
                       reads=[rstd_res[r]], writes=[rstd_res[r]])
                    yield
                    bank = pj_bank()
                    op(PE, lambda: nc.tensor.matmul(PS[bank][:, :], SWAP[:, :], RSTD[:, r, :], start=True, stop=True),
                       reads=[rstd_res[r]], writes=[ps_res[bank]], sig=True)
                    op(DVE, lambda: nc.vector.tensor_tensor(out=Bh[0:64, jj, tsl], in0=PS[bank][0:64, :],
                                                            in1=OACC[0:64, tsl], op=ALU.mult),
                       reads=[acc_all, ps_res[bank]], writes=[bh_res[jj][tb]])
                    op(DVE, lambda: nc.vector.tensor_tensor(out=Bh[64:128, jj, tsl], in0=PS[bank][64:128, :],
                                                            in1=DACC[64:128, tsl], op=ALU.mult),
                       reads=[acc_all, ps_res[bank]], writes=[bh_res[jj][tb]])
                    yield

            def outproj(quarter):
                oslots = []
                for kc in range(2):
                    assert ring_keys[(ws["use"] + kc) % n_ring] == ("ob", 2 * quarter + kc), (ring_keys[(ws["use"] + kc) % n_ring], quarter)
                    oslots.append((ws["use"] + kc) % NSLOT)
                ws["use"] += 2
                for tb in range(4):
                    tsl = slice(tb * 512, (tb + 1) * 512)
                    for oc in range(8):
                        bank = pj_bank()
                        for kc in range(2):
                            s = oslots[kc]
                            op(PE, lambda kc=kc, s=s: nc.tensor.matmul(PS[bank][:, :],
                                                                       RING[:, s, oc * 128:(oc + 1) * 128],
                                                                       Bh[:, kc, tsl], start=(kc == 0), stop=(kc == 1)),
                               reads=[slot_res[s], bh_res[kc][tb]], writes=[ps_res[bank]], sig=(kc == 1))
                        resid_add(bank, oc, tb)
                        yield
                ws_pump()

            seq = [(j, g) for j in range(8) for g in range(3)]
            def chain(*gens):
                for g_ in gens:
                    for _ in g_:
                        yield

            drain(projections(seq[0][0], seq[0][1], 0))
            pend_fin = None
            pend_out = None
            for i, (j, g) in enumerate(seq):
                sl = i % 2
                fillers = []
                n_ch = 0
                if pend_out is not None:
                    fillers.append(outproj(pend_out))
                    pend_out = None
                    n_ch += 32
                if i + 1 < len(seq):
                    fillers.append(projections(seq[i + 1][0], seq[i + 1][1], (i + 1) % 2))
                    n_ch += 24
                gen = chain(*fillers)
                pre = None
                if pend_fin is not None:
                    pre = finalize(pend_fin)
                    if pend_fin % 2 == 1:
                        pend_out = pend_fin // 2
                    pend_fin = None
                attention(j, g, sl, gen, prework=pre, n_ch=n_ch)
                drain(gen)
                if g == 2:
                    pend_fin = j
            drain(finalize(7))
            gen = outproj(3)
            for tb in range(4):
                for _ in range(8):
                    next(gen)
                if next_norm is not None and tb >= 1:
                    next_norm(tb - 1)
            drain(gen)
            if next_norm is not None:
                next_norm(3)

        def make_final_norm(seq_i, scr_fence):
            YS = scr_f32(24576, 1024).rearrange("p (s t) -> p s t", s=2)
            ys_res = [Res(scr_fence) for _ in range(2)]
            yv = yout[seq_i].rearrange("p (c t) -> p c t", c=8)

            def final_norm_tb(tb):
                tsl = slice(tb * 512, (tb + 1) * 512)
                bank = pj_bank()
                for c in range(8):
                    q = cnt["sq"] % 2
                    cnt["sq"] += 1
                    op(ACT, lambda c=c, q=q: nc.scalar.activation(out=SQ[:, q, :], in_=xT[:, c, tsl], func=AF.Square),
                       reads=[xT_res[c][tb]], writes=[sq_res[q]])
                    op(PE, lambda c=c, q=q: nc.tensor.matmul(PS[bank][:, :], ONES[:, :], SQ[:, q, :], start=(c == 0),
                                                             stop=(c == 7)),
                       reads=[sq_res[q]], writes=[ps_res[bank]], sig=True)
                r = cnt["rstd"] % 2
                cnt["rstd"] += 1
                op(ACT, lambda: nc.scalar.activation(out=RSTD[:, r, :], in_=PS[bank][:, :], func=AF.Ln,
                                                     bias=EPST[:, 0:1], scale=1.0 / 1024.0),
                   reads=[ps_res[bank]], writes=[rstd_res[r]])
                op(ACT, lambda: nc.scalar.activation(out=RSTD[:, r, :], in_=RSTD[:, r, :], func=AF.Exp, scale=-0.5),
                   reads=[rstd_res[r]], writes=[rstd_res[r]])
                for c in range(8):
                    y = cnt["y"] % 2
                    cnt["y"] += 1
                    op(DVE, lambda c=c, y=y: nc.vector.scalar_tensor_tensor(
                        out=YS[:, y, :], in0=xT[:, c, tsl], scalar=GAINS[:, 32 + c:33 + c], in1=RSTD[:, r, :],
                        op0=ALU.mult, op1=ALU.mult),
                       reads=[xT_res[c][tb], rstd_res[r]], writes=[ys_res[y]])
                    dma(SP, y_sem[y], yv[:, c, tsl], YS[:, y, :], reads=[ys_res[y]])

            return final_norm_tb

        for si in range(n_seq):
            xv = xin[si].rearrange("p (c t) -> p c t", c=8)
            for tb in range(4):
                tsl = slice(tb * 512, (tb + 1) * 512)
                dma(SP, x_sem[tb], xT[:, :, tsl], xv[:, :, tsl], writes=[xT_res[c][tb] for c in range(8)])
            phases = []
            if 0 in layers:
                phases += ["L0", "F0"]
            if 1 in layers:
                phases += ["L1", "F1"]
            nidx = {"L0": 0, "F0": 2, "L1": 1, "F1": 3}
            pre = False
            for pi, ph in enumerate(phases):
                f_ = fence()
                if pi + 1 < len(phases):
                    nn = (lambda tb, n=nidx[phases[pi + 1]]: norm_to_A(n, tb))
                else:
                    nn = make_final_norm(si, f_)
                if ph == "L0":
                    layer0(f_, next_norm=nn)
                elif ph == "L1":
                    layer1(f_, pre_normed=pre, next_norm=nn)
                else:
                    ffn(int(ph[1]), f_, pre_normed=pre, next_norm=nn)
                pre = True
        for y in range(2):
            SP.wait(Tok(y_sem[y], y_sem[y].cnt))
        for e in (PE, ACT, DVE, POOL):
            pass
    return nc


_PROG_CACHE = {}


def kernel(x, norm_mix, norm_ffn, w_qkv_a, w_out_a, sink_a, w_qkv_b, w_out_b, w_gate, w_up, w_down, norm_final):
    x = np.asarray(x, np.float32)
    B, S, Dm = x.shape
    wts = _pack_weights(*(np.asarray(w, np.float32) for w in (w_qkv_a, w_out_a, w_qkv_b, w_out_b, w_gate, w_up, w_down)))
    gl = [np.asarray(norm_mix, np.float32)[0], np.asarray(norm_mix, np.float32)[1],
          np.asarray(norm_ffn, np.float32)[0], np.asarray(norm_ffn, np.float32)[1],
          np.asarray(norm_final, np.float32)]
    gains = np.stack([g.reshape(8, 128).T for g in gl], axis=1).reshape(128, 40).copy()
    sk = np.asarray(sink_a, np.float32)[0]
    sink = np.empty((128, 8), np.float32)
    for p in range(2):
        for a in range(4):
            sink[0:64, 4 * p + a] = sk[8 * p + a]
            sink[64:128, 4 * p + a] = sk[8 * p + 4 + a]
    xt = np.ascontiguousarray(x.reshape(B, S, 8, 128).transpose(0, 3, 2, 1)).reshape(B, 128, 8 * S)
    if "nc" not in _PROG_CACHE:
        _PROG_CACHE["nc"] = build_program()
    nc = _PROG_CACHE["nc"]
    in_maps = []
    for i in range(N_CORES):
        in_maps.append({"xin": xt[SEQ_PER_CORE * i:SEQ_PER_CORE * (i + 1)], "wts": wts, "gains": gains, "sink": sink})
    res = run_bass_kernel_spmd(nc, in_maps, core_ids=list(range(N_CORES)))
    ys = np.concatenate([np.asarray(r["yout"]) for r in res.results], axis=0)
    out = ys.reshape(B, 128, 8, S).transpose(0, 3, 2, 1).reshape(B, S, Dm)
    return np.ascontiguousarray(out.astype(np.float32))
```

```python
import numpy as np
import concourse.bass as bass
import concourse.mybir as mybir
from concourse.bass_utils import run_bass_kernel_spmd

F32 = mybir.dt.float32
BF16 = mybir.dt.bfloat16
I32 = mybir.dt.int32
AF = mybir.ActivationFunctionType
ALU = mybir.AluOpType

T = 2048
KC = 8
NF = 22
NSLOT = 12
FSPL = [(0, 8), (8, 15), (15, 22)]
BIG = 30000.0
EPS = 1e-6
N_CORES = 8
SEQ_PER_CORE = 2
GROUPS = ((128, 1), (512, 4), (2048, 16))


def _unit_lists(layers=(0, 1)):
    ring = []
    if 0 in layers:
        for c in range(8):
            ring.append(("qa", c))
        for p in range(2):
            ring.append(("ka", p))
        for v in range(2):
            ring.append(("va", v))
        for kc in range(8):
            ring.append(("oa", kc))
        for (f0, f1) in FSPL:
            for f in range(f0, f1):
                ring.append(("g", 0, f))
                ring.append(("u", 0, f))
    if 1 in layers:
        items = [(j, g) for j in range(8) for g in range(3)]
        for s in range(3):
            ring.append(("qkvb", 0, 0, s))
        for i, (j, g) in enumerate(items):
            if g == 1 and j >= 2 and j % 2 == 0:
                ring.append(("ob", j - 2))
                ring.append(("ob", j - 1))
            if i + 1 < len(items):
                for s in range(3):
                    ring.append(("qkvb", items[i + 1][0], items[i + 1][1], s))
        ring.append(("ob", 6))
        ring.append(("ob", 7))
        for (f0, f1) in FSPL:
            for f in range(f0, f1):
                ring.append(("g", 1, f))
                ring.append(("u", 1, f))
    wd = []
    for l in range(2):
        for f in range(NF):
            wd.append(("d", l, f))
    return ring, wd


def _pack_weights(w_qkv_a, w_out_a, w_qkv_b, w_out_b, w_gate, w_up, w_down, layers=(0, 1)):
    ring, wd = _unit_lists(layers)
    keys = ring + wd
    out = np.empty((len(keys), 128, 1024), np.float32)

    def proj_unit(W, cols):
        return W[:, cols].reshape(8, 128, 128).transpose(1, 0, 2).reshape(128, 1024)

    qa_rows = {}
    for i, k in enumerate(keys):
        kind = k[0]
        if kind == "qa":
            c = k[1]
            p, a = c // 4, c % 4
            h0, h1 = 8 * p + a, 8 * p + 4 + a
            cols = np.concatenate([np.arange(64 * h0, 64 * h0 + 64), np.arange(64 * h1, 64 * h1 + 64)])
            out[i] = proj_unit(w_qkv_a[0], cols)
        elif kind == "ka":
            p = k[1]
            out[i] = proj_unit(w_qkv_a[0], np.arange(1024 + 128 * p, 1024 + 128 * p + 128))
        elif kind == "va":
            v = k[1]
            out[i] = proj_unit(w_qkv_a[0], np.arange(1280 + 128 * v, 1280 + 128 * v + 128))
        elif kind == "oa":
            c = k[1]
            p, a = c // 4, c % 4
            h0, h1 = 8 * p + a, 8 * p + 4 + a
            rows = np.concatenate([np.arange(64 * h0, 64 * h0 + 64), np.arange(64 * h1, 64 * h1 + 64)])
            out[i] = w_out_a[0][rows, :]
        elif kind == "g":
            _, l, f = k
            out[i] = proj_unit(w_gate[l], np.arange(128 * f, 128 * f + 128))
        elif kind == "u":
            _, l, f = k
            out[i] = proj_unit(w_up[l], np.arange(128 * f, 128 * f + 128))
        elif kind == "qkvb":
            _, j, g, s = k
            base = (3 * g + s) * 1024 + 128 * j
            out[i] = proj_unit(w_qkv_b[0], np.arange(base, base + 128))
        elif kind == "ob":
            kc = k[1]
            out[i] = w_out_b[0][128 * kc:128 * kc + 128, :]
        elif kind == "d":
            _, l, f = k
            out[i] = w_down[l][128 * f:128 * f + 128, :]
        else:
            raise AssertionError(kind)
    return out


class Sem:
    def __init__(self, h):
        self.h = h
        self.cnt = 0


class Tok:
    __slots__ = ("sem", "val")

    def __init__(self, sem, val):
        self.sem = sem
        self.val = val


class Res:
    __slots__ = ("w", "r")

    def __init__(self, fence=None):
        self.w = None
        self.r = list(fence) if fence else []


class Eng:
    def __init__(self, e, sem, is_pe=False):
        self.e = e
        self.sem = sem
        self.is_pe = is_pe
        self.seen = {}
        self.pend = None
        self.last = None

    def wait(self, tok):
        assert tok.val is not None, "unresolved token"
        if self.seen.get(tok.sem, 0) >= tok.val:
            return
        self.e.wait_ge(tok.sem.h, tok.val)
        self.seen[tok.sem] = tok.val


def _add_reader(res, tok):
    if res.r and res.r[-1] is tok:
        return
    if res.r:
        l = res.r[-1]
        if l.sem is tok.sem and l.val is not None and tok.val is not None and tok.val >= l.val:
            res.r[-1] = tok
            return
    res.r.append(tok)


def op(eng, fn, reads=(), writes=(), sig=True):
    deps = []
    for r in reads:
        if r.w is not None:
            deps.append(r.w)
    for w in writes:
        if w.w is not None:
            deps.append(w.w)
        deps.extend(w.r)
    for t in deps:
        if eng.is_pe and t.sem is eng.sem:
            continue
        eng.wait(t)
    ins = fn()
    if sig:
        ins.then_inc(eng.sem.h, 1)
        eng.sem.cnt += 1
        tok = Tok(eng.sem, eng.sem.cnt)
        if eng.pend is not None:
            eng.pend.val = tok.val
            eng.pend = None
    else:
        if eng.pend is None:
            eng.pend = Tok(eng.sem, None)
        tok = eng.pend
    eng.last = tok
    for r in reads:
        _add_reader(r, tok)
    for w in writes:
        w.w = tok
        w.r = []
    return tok


def dma(eng, dsem, out, in_, reads=(), writes=()):
    deps = []
    for r in reads:
        if r.w is not None:
            deps.append(r.w)
    for w in writes:
        if w.w is not None:
            deps.append(w.w)
        deps.extend(w.r)
    for t in deps:
        eng.wait(t)
    eng.e.dma_start(out=out, in_=in_).then_inc(dsem.h, 16)
    dsem.cnt += 16
    tok = Tok(dsem, dsem.cnt)
    for r in reads:
        _add_reader(r, tok)
    for w in writes:
        w.w = tok
        w.r = []
    return tok


def build_program(n_seq=SEQ_PER_CORE, layers=(0, 1), dbg=None):
    dbg = dbg or {}
    ring_keys, wd_keys = _unit_lists(layers)
    n_ring = len(ring_keys)
    n_units = n_ring + len(wd_keys)
    wd_index = {k: n_ring + i for i, k in enumerate(wd_keys)}

    nc = bass.Bass("TRN2", target_bir_lowering=False)
    xin = nc.dram_tensor("xin", [n_seq, 128, 8 * T], F32, kind="ExternalInput").ap()
    wts = nc.dram_tensor("wts", [n_units, 128, 1024], F32, kind="ExternalInput").ap()
    gains_d = nc.dram_tensor("gains", [128, 40], F32, kind="ExternalInput").ap()
    sink_d = nc.dram_tensor("sink", [128, 8], F32, kind="ExternalInput").ap()
    yout = nc.dram_tensor("yout", [n_seq, 128, 8 * T], F32, kind="ExternalOutput").ap()

    from contextlib import ExitStack
    with ExitStack() as es:
        def sb(name, shape, dt):
            return es.enter_context(nc.sbuf_tensor(name, shape, dt))

        def mksem(name):
            return Sem(es.enter_context(nc.semaphore(name)))

        xT = sb("xT", [128, 8, T], F32)
        A = sb("A", [128, 8, T], BF16)
        SCR = sb("SCR", [128, 30720], BF16)
        SWAP = sb("SWAP", [128, 128], F32)
        RD = sb("RD", [128, 512], F32)
        RDXY = sb("RDXY", [128, 2, 2, 256], BF16)
        RING = sb("RING", [128, NSLOT, 1024], BF16)
        IDT = sb("IDT", [128, 24, 128], BF16)
        ONES = sb("ONES", [128, 128], BF16)
        D1 = sb("D1", [128, 256], BF16)
        D0 = sb("D0", [128, 3, 128], BF16)
        GAINS = sb("GAINS", [128, 40], F32)
        ESINK = sb("ESINK", [128, 8], F32)
        EPST = sb("EPST", [128, 1], F32)
        SQ = sb("SQ", [128, 2, 512], BF16)
        RSTD = sb("RSTD", [128, 2, 512], F32)
        PT = sb("PT", [128, 3, 512], BF16)
        SG = sb("SG", [128, 2, 512], BF16)
        PS = [es.enter_context(nc.psum_tensor(f"ps{i}", [128, 512], F32)) for i in range(8)]

        PE = Eng(nc.tensor, mksem("s_pe"), is_pe=True)
        ACT = Eng(nc.scalar, mksem("s_act"))
        DVE = Eng(nc.vector, mksem("s_dve"))
        POOL = Eng(nc.gpsimd, mksem("s_pool"))
        SP = Eng(nc.sync, mksem("s_sp"))
        ENGS = [PE, ACT, DVE, POOL, SP]
        slot_sem = [mksem(f"s_slot{i}") for i in range(NSLOT)]
        wd_sem = [mksem(f"s_wd{i}") for i in range(8)]
        x_sem = [mksem(f"s_x{i}") for i in range(8)]
        y_sem = [mksem(f"s_y{i}") for i in range(2)]
        c_sem = mksem("s_const")

        ps_res = [Res() for _ in range(8)]
        slot_res = [Res() for _ in range(NSLOT)]
        xT_res = [[Res() for _ in range(4)] for _ in range(8)]
        A_res = [[Res() for _ in range(4)] for _ in range(8)]
        sq_res = [Res() for _ in range(2)]
        rstd_res = [Res() for _ in range(2)]
        pt_res = [Res() for _ in range(3)]
        sg_res = [Res() for _ in range(2)]
        rd_res = Res()
        rrh_res = rd_res
        cnt = {"sq": 0, "rstd": 0, "pt": 0, "sg": 0, "y": 0}

        def fence():
            toks = []
            for e in ENGS:
                if e.last is not None:
                    assert e.last.val is not None
                    toks.append(e.last)
            for s_ in wd_sem + y_sem:
                if s_.cnt > 0:
                    toks.append(Tok(s_, s_.cnt))
            return toks

        def scr_f32(off_bf, n_f32):
            return SCR[:, off_bf:off_bf + 2 * n_f32].bitcast(F32)

        cres = Res()
        tok_g = dma(SP, c_sem, GAINS[:], gains_d[:, :], writes=[cres])
        tok_s = dma(SP, c_sem, ESINK[:], sink_d[:, :], writes=[cres])
        tok_s = Tok(c_sem, c_sem.cnt)
        ACT.wait(tok_s)
        es_res = Res()
        op(ACT, lambda: nc.scalar.activation(out=ESINK[:], in_=ESINK[:], func=AF.Exp), writes=[es_res])
        k_res = Res()
        op(DVE, lambda: nc.vector.memset(ONES[:], 1.0), writes=[k_res])
        op(DVE, lambda: nc.vector.memset(EPST[:], EPS), writes=[k_res])
        io_id = SCR[:, 0:256].bitcast(I32)
        io_d1 = SCR[:, 256:768].bitcast(I32)
        io_d0 = SCR[:, 768:1536].bitcast(I32)
        f_a = scr_f32(1536, 384)
        f_b = scr_f32(2304, 384)
        f_c = scr_f32(3072, 384)
        io_res = Res()
        op(POOL, lambda: nc.gpsimd.iota(io_id, pattern=[[1, 128]], base=0, channel_multiplier=-1), writes=[io_res])
        op(POOL, lambda: nc.gpsimd.iota(io_d1, pattern=[[-1, 256]], base=64, channel_multiplier=1), writes=[io_res])
        op(POOL, lambda: nc.gpsimd.iota(io_d0.rearrange("p (a b) -> p a b", a=3), pattern=[[128, 3], [-1, 128]],
                                        base=-128, channel_multiplier=1), writes=[io_res])
        tmp_res = Res()
        op(DVE, lambda: nc.vector.tensor_scalar(out=f_a[:, 0:128], in0=io_id, scalar1=0.0, scalar2=None,
                                                op0=ALU.is_equal), reads=[io_res], writes=[tmp_res])
        for k in range(24):
            m = k - 7
            val = 8.0 * (2.0 ** (-m / 2.0))
            op(DVE, lambda k=k, val=val: nc.vector.tensor_scalar(out=IDT[:, k, :], in0=f_a[:, 0:128], scalar1=val,
                                                                 scalar2=None, op0=ALU.mult),
               reads=[tmp_res], writes=[k_res])

        op(DVE, lambda: nc.vector.tensor_scalar(out=f_b[:, 0:128], in0=io_id, scalar1=64.0, scalar2=None,
                                                op0=ALU.is_equal), reads=[io_res], writes=[tmp_res])
        op(DVE, lambda: nc.vector.tensor_scalar(out=f_c[:, 0:128], in0=io_id, scalar1=-64.0, scalar2=None,
                                                op0=ALU.is_equal), reads=[io_res], writes=[tmp_res])
        op(DVE, lambda: nc.vector.tensor_tensor(out=SWAP[:, :], in0=f_b[:, 0:128], in1=f_c[:, 0:128], op=ALU.add),
           reads=[tmp_res], writes=[k_res])

        def build_dist(io, n, thr, dst):
            op(DVE, lambda: nc.vector.tensor_scalar(out=f_a[:, 0:n], in0=io, scalar1=1.0, scalar2=None,
                                                    op0=ALU.mult), reads=[io_res, tmp_res], writes=[tmp_res])
            op(DVE, lambda: nc.vector.tensor_scalar(out=f_b[:, 0:n], in0=io, scalar1=-1.0, scalar2=None,
                                                    op0=ALU.mult), reads=[io_res, tmp_res], writes=[tmp_res])
            op(DVE, lambda: nc.vector.tensor_tensor(out=f_a[:, 0:n], in0=f_a[:, 0:n], in1=f_b[:, 0:n],
                                                    op=ALU.max), reads=[tmp_res], writes=[tmp_res])
            op(DVE, lambda: nc.vector.tensor_scalar(out=f_b[:, 0:n], in0=f_a[:, 0:n], scalar1=float(thr),
                                                    scalar2=None, op0=ALU.is_gt), reads=[tmp_res], writes=[tmp_res])
            op(DVE, lambda: nc.vector.tensor_scalar(out=f_c[:, 0:n], in0=f_a[:, 0:n], scalar1=-1.0, scalar2=None,
                                                    op0=ALU.mult), reads=[tmp_res], writes=[tmp_res])
            op(DVE, lambda: nc.vector.scalar_tensor_tensor(out=dst, in0=f_b[:, 0:n], scalar=-BIG, in1=f_c[:, 0:n],
                                                           op0=ALU.mult, op1=ALU.add),
               reads=[tmp_res], writes=[k_res])

        build_dist(io_d1, 256, 64, D1[:])
        build_dist(io_d0, 384, 128, D0[:].rearrange("p a b -> p (a b)"))
        for e in (PE, ACT, DVE):
            e.wait(DVE.last)
            e.wait(ACT.last)
            e.wait(Tok(c_sem, c_sem.cnt))

        ws = {"dma": 0, "use": 0, "total": n_seq * n_ring}

        def ws_pump():
            while ws["dma"] < ws["total"] and ws["dma"] < ws["use"] + NSLOT:
                n = ws["dma"]
                s = n % NSLOT
                dma(POOL, slot_sem[s], RING[:, s, :], wts[n % n_ring], writes=[slot_res[s]])
                ws["dma"] += 1

        def ws_take(key):
            n = ws["use"]
            assert ring_keys[n % n_ring] == key, (ring_keys[n % n_ring], key)
            assert ws["dma"] > n
            s = n % NSLOT
            return s

        def ws_release(count=1):
            ws["use"] += count
            ws_pump()

        ws_pump()

        def slot_proj(s, kc):
            return RING[:, s, kc * 128:(kc + 1) * 128]

        pj = {"i": 0, "banks": [6, 7, 0, 1, 2, 3, 4, 5]}

        def pj_bank():
            b = pj["banks"][pj["i"] % len(pj["banks"])]
            pj["i"] += 1
            return b

        def set_pj(banks):
            pj["banks"] = list(banks)
            pj["i"] = 0

        def rmsnorm(n_idx, tb, dst_fn, dst_res_fn, final=False):
            tsl = slice(tb * 512, (tb + 1) * 512)
            bank = pj_bank()
            for c in range(8):
                q = cnt["sq"] % 2
                cnt["sq"] += 1
                op(ACT, lambda c=c, q=q: nc.scalar.activation(out=SQ[:, q, :], in_=xT[:, c, tsl], func=AF.Square),
                   reads=[xT_res[c][tb]], writes=[sq_res[q]])
                op(PE, lambda c=c, q=q: nc.tensor.matmul(PS[bank][:, :], ONES[:, :], SQ[:, q, :], start=(c == 0),
                                                         stop=(c == 7)),
                   reads=[sq_res[q]], writes=[ps_res[bank]], sig=True)
            r = cnt["rstd"] % 2
            cnt["rstd"] += 1
            op(ACT, lambda: nc.scalar.activation(out=RSTD[:, r, :], in_=PS[bank][:, :], func=AF.Ln, bias=EPST[:, 0:1],
                                                 scale=1.0 / 1024.0),
               reads=[ps_res[bank]], writes=[rstd_res[r]])
            op(ACT, lambda: nc.scalar.activation(out=RSTD[:, r, :], in_=RSTD[:, r, :], func=AF.Exp, scale=-0.5),
               reads=[rstd_res[r]], writes=[rstd_res[r]])
            for c in range(8):
                op(DVE, lambda c=c: nc.vector.scalar_tensor_tensor(out=dst_fn(c), in0=xT[:, c, tsl],
                                                                   scalar=GAINS[:, n_idx * 8 + c:n_idx * 8 + c + 1],
                                                                   in1=RSTD[:, r, :], op0=ALU.mult, op1=ALU.mult),
                   reads=[xT_res[c][tb], rstd_res[r]], writes=[dst_res_fn(c)])

        def norm_to_A(n_idx, tb):
            tsl = slice(tb * 512, (tb + 1) * 512)
            rmsnorm(n_idx, tb, lambda c: A[:, c, tsl], lambda c: A_res[c][tb])

        def proj_fm(s, tb, dst, dst_res, last_use, evac_eng=None):
            bank = pj_bank()
            tsl = slice(tb * 512, (tb + 1) * 512)
            for kc in range(8):
                op(PE, lambda kc=kc: nc.tensor.matmul(PS[bank][:, :], slot_proj(s, kc), A[:, kc, tsl],
                                                      start=(kc == 0), stop=(kc == 7)),
                   reads=[slot_res[s], A_res[kc][tb]], writes=[ps_res[bank]], sig=(kc == 7))
            if evac_eng is not None:
                evac_eng(bank)
            else:
                op(DVE, lambda: nc.vector.tensor_copy(out=dst, in_=PS[bank][:, :]), reads=[ps_res[bank]],
                   writes=[dst_res])

        def resid_add(bank, oc, tb):
            tsl = slice(tb * 512, (tb + 1) * 512)
            op(DVE, lambda: nc.vector.tensor_tensor(out=xT[:, oc, tsl], in0=PS[bank][:, :], in1=xT[:, oc, tsl],
                                                    op=ALU.add),
               reads=[ps_res[bank], xT_res[oc][tb]], writes=[xT_res[oc][tb]])

        def ffn(l, scr_fence, pre_normed=False, next_norm=None):
            aT = SCR[:, 0:16384].rearrange("p (f t) -> p f t", f=8)
            WD = SCR[:, 16384:24576].rearrange("p (f c) -> p f c", f=8)
            aT_res = [[Res(scr_fence) for _ in range(4)] for _ in range(8)]
            wd_res = [Res(scr_fence) for _ in range(8)]
            set_pj([6, 7, 0, 1, 2, 3, 4, 5])
            if not pre_normed:
                for tb in range(4):
                    norm_to_A(2 + l, tb)
            for (f0, f1) in FSPL:
                nf = f1 - f0
                for fi in range(nf):
                    dma(POOL, wd_sem[fi], WD[:, fi, :], wts[wd_index[("d", l, f0 + fi)]], writes=[wd_res[fi]])
                for fi in range(nf):
                    f = f0 + fi
                    sg_ = ws_take(("g", l, f))
                    su_ = (ws["use"] + 1) % NSLOT
                    assert ring_keys[(ws["use"] + 1) % n_ring] == ("u", l, f)
                    for tb in range(4):
                        tsl = slice(tb * 512, (tb + 1) * 512)
                        bg = pj_bank()
                        bu = pj_bank()
                        for kc in range(8):
                            op(PE, lambda kc=kc: nc.tensor.matmul(PS[bg][:, :], slot_proj(sg_, kc), A[:, kc, tsl],
                                                                  start=(kc == 0), stop=(kc == 7)),
                               reads=[slot_res[sg_], A_res[kc][tb]], writes=[ps_res[bg]], sig=(kc == 7))
                        for kc in range(8):
                            op(PE, lambda kc=kc: nc.tensor.matmul(PS[bu][:, :], slot_proj(su_, kc), A[:, kc, tsl],
                                                                  start=(kc == 0), stop=(kc == 7)),
                               reads=[slot_res[su_], A_res[kc][tb]], writes=[ps_res[bu]], sig=(kc == 7))
                        q = cnt["sg"] % 2
                        cnt["sg"] += 1
                        op(ACT, lambda q=q: nc.scalar.activation(out=SG[:, q, :], in_=PS[bg][:, :], func=AF.Silu),
                           reads=[ps_res[bg]], writes=[sg_res[q]])
                        op(DVE, lambda q=q: nc.vector.tensor_tensor(out=aT[:, fi, tsl], in0=PS[bu][:, :],
                                                                    in1=SG[:, q, :], op=ALU.mult),
                           reads=[ps_res[bu], sg_res[q]], writes=[aT_res[fi][tb]])
                    ws_release(2)
                for tb in range(4):
                    tsl = slice(tb * 512, (tb + 1) * 512)
                    for oc in range(8):
                        bank = pj_bank()
                        for fi in range(nf):
                            op(PE, lambda fi=fi: nc.tensor.matmul(PS[bank][:, :], WD[:, fi, oc * 128:(oc + 1) * 128],
                                                                  aT[:, fi, tsl], start=(fi == 0), stop=(fi == nf - 1)),
                               reads=[wd_res[fi], aT_res[fi][tb]], writes=[ps_res[bank]], sig=(fi == nf - 1))
                        resid_add(bank, oc, tb)
                    if next_norm is not None and (f0, f1) == FSPL[-1] and tb >= 1:
                        next_norm(tb - 1)
                if next_norm is not None and (f0, f1) == FSPL[-1]:
                    next_norm(3)

        def layer0(scr_fence, next_norm=None):
            QK = SCR[:, 0:20480].rearrange("p (c t) -> p c t", c=10)
            V0 = SCR[:, 20480:24576].rearrange("p (n v) -> p n v", n=16)
            qk_res = [[Res(scr_fence) for _ in range(4)] for _ in range(10)]
            v_res = [Res(scr_fence) for _ in range(16)]
            set_pj([6, 7, 0, 1, 2, 3, 4, 5])
            slots = []
            for i in range(12):
                assert ring_keys[(ws["use"] + i) % n_ring] == (("qa", i) if i < 8 else (("ka", i - 8) if i < 10 else ("va", i - 10)))
                slots.append((ws["use"] + i) % NSLOT)
            for tb in range(4):
                norm_to_A(0, tb)
                tsl = slice(tb * 512, (tb + 1) * 512)
                for oc in range(10):
                    proj_fm(slots[oc], tb, QK[:, oc, tsl], qk_res[oc][tb], False)
                for tt in range(4):
                    n = tb * 4 + tt
                    bank = pj_bank()
                    first = True
                    for vu in range(2):
                        s = slots[10 + vu]
                        for kc in range(8):
                            op(PE, lambda kc=kc, s=s, vu=vu, first=first: nc.tensor.matmul(
                                PS[bank][:, vu * 128:(vu + 1) * 128], A[:, kc, n * 128:(n + 1) * 128], slot_proj(s, kc),
                                start=(kc == 0), stop=(kc == 7), skip_group_check=True),
                               reads=[slot_res[s], A_res[kc][tb]], writes=[ps_res[bank]], sig=(kc == 7))
                            first = False
                    op(DVE, lambda n=n: nc.vector.tensor_copy(out=V0[:, n, :], in_=PS[bank][:, 0:256]),
                       reads=[ps_res[bank]], writes=[v_res[n]])
            ws_release(12)
            set_pj([6, 7])
            s_banks = [0, 1]
            od_banks = [(2, 3), (4, 5)]
            items = []
            for b in range(16):
                for p in range(2):
                    tiles = []
                    for gl in range(2):
                        for c in (b - 1, b, b + 1):
                            if 0 <= c < 16:
                                tiles.append((gl, c))
                    for i, (gl, c) in enumerate(tiles):
                        items.append((b, p, gl, c, i == 0, i == len(tiles) - 1))
            LA = 1
            n_items = len(items)
            grp = {"k": 0}

            def emit_S(i):
                b, p, gl, c, _, _ = items[i]
                bank = s_banks[i % 2]
                rows = slice(gl * 64, gl * 64 + 64)
                typ = c - b + 1
                for a in range(4):
                    h = 8 * p + 4 * gl + a
                    k = (h + 1) + 7
                    op(PE, lambda a=a, k=k: nc.tensor.matmul(PS[bank][:, a * 128:(a + 1) * 128], IDT[:, k, :],
                                                             D0[:, typ, :], start=(a == 0), stop=False,
                                                             skip_group_check=True),
                       writes=[ps_res[bank]], sig=False)
                tb = b // 4
                tbk = c // 4
                op(PE, lambda: nc.tensor.matmul(PS[bank][:, :], QK[rows, 8 + p, c * 128:(c + 1) * 128],
                                                QK[rows, 4 * p:4 * p + 4, b * 128:(b + 1) * 128],
                                                start=False, stop=True, skip_group_check=True),
                   reads=[qk_res[8 + p][tbk]] + [qk_res[4 * p + a][tb] for a in range(4)], writes=[ps_res[bank]],
                   sig=True)

            def emit_rest(i):
                b, p, gl, c, first, last = items[i]
                bank = s_banks[i % 2]
                q = cnt["pt"] % 3
                cnt["pt"] += 1
                op(ACT, lambda: nc.scalar.activation(out=PT[:, q, :], in_=PS[bank][:, :], func=AF.Exp, scale=0.125),
                   reads=[ps_res[bank]], writes=[pt_res[q]])
                if first:
                    grp["k"] += 1
                ob, db = od_banks[grp["k"] % 2]
                rows = slice(gl * 64, gl * 64 + 64)
                cs = [cc for cc in (b - 1, b, b + 1) if 0 <= cc < 16]
                st = (c == cs[0])
                sp = (c == cs[-1])
                g = 2 * p + gl
                op(PE, lambda: nc.tensor.matmul(PS[ob][rows, :], V0[:, c, g * 64:(g + 1) * 64], PT[:, q, :],
                                                start=st, stop=sp, skip_group_check=True),
                   reads=[v_res[c], pt_res[q]], writes=[ps_res[ob]], sig=False)
                op(PE, lambda: nc.tensor.matmul(PS[db][rows, :], ONES[:, 0:64], PT[:, q, :],
                                                start=st, stop=sp, skip_group_check=True),
                   reads=[pt_res[q]], writes=[ps_res[db]], sig=True)
                if last:
                    tb = b // 4
                    for a in range(4):
                        op(DVE, lambda a=a: nc.vector.tensor_scalar(out=RD[:, a * 128:(a + 1) * 128],
                                                                    in0=PS[db][:, a * 128:(a + 1) * 128],
                                                                    scalar1=ESINK[:, 4 * p + a:4 * p + a + 1],
                                                                    scalar2=None, op0=ALU.add),
                           reads=[ps_res[db]], writes=[rd_res])
                    op(DVE, lambda: nc.vector.reciprocal(out=RD[:, :], in_=RD[:, :]), reads=[rd_res], writes=[rd_res])
                    op(DVE, lambda: nc.vector.tensor_tensor(
                        out=A[:, 4 * p:4 * p + 4, b * 128:(b + 1) * 128],
                        in0=PS[ob][:, :].rearrange("p (a q) -> p a q", a=4),
                        in1=RD[:, :].rearrange("p (a q) -> p a q", a=4), op=ALU.mult),
                       reads=[ps_res[ob], rd_res], writes=[A_res[4 * p + a][tb] for a in range(4)])

            for i in range(min(LA, n_items)):
                emit_S(i)
            for i in range(n_items):
                if i + LA < n_items:
                    emit_S(i + LA)
                emit_rest(i)
            set_pj([6, 7, 0, 1])
            oslots = []
            for kc in range(8):
                assert ring_keys[(ws["use"] + kc) % n_ring] == ("oa", kc)
                oslots.append((ws["use"] + kc) % NSLOT)
            for tb in range(4):
                tsl = slice(tb * 512, (tb + 1) * 512)
                for oc in range(8):
                    bank = pj_bank()
                    for kc in range(8):
                        s = oslots[kc]
                        op(PE, lambda kc=kc, s=s: nc.tensor.matmul(PS[bank][:, :], RING[:, s, oc * 128:(oc + 1) * 128],
                                                                   A[:, kc, tsl], start=(kc == 0), stop=(kc == 7)),
                           reads=[slot_res[s], A_res[kc][tb]], writes=[ps_res[bank]], sig=(kc == 7))
                    resid_add(bank, oc, tb)
                if next_norm is not None and tb >= 1:
                    next_norm(tb - 1)
            ws_release(8)
            if next_norm is not None:
                next_norm(3)

        def layer1(scr_fence, pre_normed=False, next_norm=None):
            Bh = SCR[:, 0:4096].rearrange("p (c t) -> p c t", c=2)
            QXY = SCR[:, 4096:12288].rearrange("p (s h t) -> p s h t", s=2, h=2)
            K1 = SCR[:, 12288:16384].rearrange("p (s t) -> p s t", s=2)
            V1 = SCR[:, 16384:22528].rearrange("p (s n v) -> p s n v", s=2, n=16)
            OACC = scr_f32(22528, 2048)
            DACC = scr_f32(26624, 2048)
            rdxy_res = [Res() for _ in range(2)]
            bh_res = [[Res(scr_fence) for _ in range(4)] for _ in range(2)]
            q_res = [Res(scr_fence) for _ in range(2)]
            k_res1 = [Res(scr_fence) for _ in range(2)]
            v_res = [[Res(scr_fence) for _ in range(4)] for _ in range(2)]
            acc_res = [Res(scr_fence) for _ in range(4)]
            acc_all = Res(scr_fence)
            set_pj([6, 7, 0, 1, 2, 3, 4, 5])
            for sl_ in range(2):
                op(POOL, lambda sl_=sl_: nc.gpsimd.memset(V1[:, sl_, :, 64:128], 1.0),
                   writes=[v_res[sl_][n4_] for n4_ in range(4)])
                op(POOL, lambda sl_=sl_: nc.gpsimd.memset(QXY[64:128, sl_, 0, :], 0.0), writes=[q_res[sl_]])
                op(POOL, lambda sl_=sl_: nc.gpsimd.memset(QXY[0:64, sl_, 1, :], 0.0), writes=[q_res[sl_]])
            if not pre_normed:
                for tb in range(4):
                    norm_to_A(1, tb)
            set_pj([6, 7])
            s_banks = [0, 1]
            od_banks = [(2, 3), (4, 5)]
            st = {"s": 0, "od": 0}
            pgi = 0

            def projections(j, g, sl):
                dil = GROUPS[g][1]
                L = T // dil
                sq_ = ws_take(("qkvb", j, g, 0))
                sk_ = (ws["use"] + 1) % NSLOT
                sv_ = (ws["use"] + 2) % NSLOT
                for h_ in range(2):
                    val = 8.0 * (2.0 ** (-((2 * j + h_ + 1) - 4 * g) / 2.0))
                    op(DVE, lambda h_=h_, val=val: nc.vector.tensor_scalar(out=RDXY[:, sl, h_, :], in0=D1[:, :],
                                                                         scalar1=val, scalar2=None, op0=ALU.mult),
                       writes=[rdxy_res[sl]])
                for tb in range(4):
                    tsl = slice(tb * 512, (tb + 1) * 512)

                    def q_evac(bank, tsl=tsl):
                        op(DVE, lambda: nc.vector.tensor_copy(out=QXY[0:64, sl, 0, tsl], in_=PS[bank][0:64, :]),
                           reads=[ps_res[bank]], writes=[q_res[sl]])
                        op(DVE, lambda: nc.vector.tensor_copy(out=QXY[64:128, sl, 1, tsl], in_=PS[bank][64:128, :]),
                           reads=[ps_res[bank]], writes=[q_res[sl]])

                    proj_fm(sq_, tb, None, None, False, evac_eng=q_evac)
                    yield
                    proj_fm(sk_, tb, K1[:, sl, tsl], k_res1[sl], False)
                    yield
                for n4 in range(4):
                    bank = pj_bank()
                    for tt in range(4):
                        n = n4 * 4 + tt
                        pos = n * 128
                        r = pos // L
                        u0 = pos % L
                        t0 = r + dil * u0
                        tsel = slice(t0, t0 + dil * 127 + 1, dil)
                        for kc in range(8):
                            tbs = sorted({(t0 + dil * i) // 512 for i in (0, 127)})
                            rd = [slot_res[sv_]] + [A_res[kc][x] for x in range(tbs[0], tbs[-1] + 1)]
                            op(PE, lambda kc=kc, tt=tt, tsel=tsel: nc.tensor.matmul(
                                PS[bank][:, tt * 128:(tt + 1) * 128], A[:, kc, tsel], slot_proj(sv_, kc),
                                start=(kc == 0), stop=(kc == 7), skip_group_check=True),
                               reads=rd, writes=[ps_res[bank]], sig=(kc == 7))
                        if tt == 3:
                            op(DVE, lambda n4=n4: nc.vector.tensor_copy(
                                out=V1[:, sl, n4 * 4:(n4 + 1) * 4, :].rearrange("p n (h d) -> p n h d", h=3)[:, :, 0:3:2, :],
                                in_=PS[bank][:, :].rearrange("p (n h d) -> p n h d", n=4, h=2)),
                               reads=[ps_res[bank]], writes=[v_res[sl][n4]])
                        yield
                ws_release(3)

            def drain(gen):
                if gen is not None:
                    for _ in gen:
                        pass

            def attention(j, g, sl, filler=None, fill_n=2, prework=None, n_ch=0):
                win, dil = GROUPS[g]
                L = T // dil
                tot_tiles = 22 if dil == 1 else 16
                tctr = {"k": 0}
                hX, hY = 2 * j, 2 * j + 1
                kX = (hX + 1 - 4 * g) + 7
                kY = (hY + 1 - 4 * g) + 7
                for B in range(4):
                    tiles = []
                    p0 = 512 * B
                    for r in range(dil):
                        lo = max(r * L, p0) - r * L
                        hi = min((r + 1) * L, p0 + 512) - r * L
                        if hi <= lo:
                            continue
                        for m in range(L // 128):
                            ulo = max(128 * m - 64, lo, 0)
                            uhi = min(128 * m + 192, hi, L)
                            if uhi > ulo:
                                tiles.append((r, m, ulo, uhi))
                    ob, db = od_banks[st["od"] % 2]
                    st["od"] += 1
                    nt = len(tiles)
                    written = set()

                    def emit_S(i):
                        r, m, ulo, uhi = tiles[i]
                        N = uhi - ulo
                        bank = s_banks[(st["s"] + i) % 2]
                        jlo = ulo - 128 * m + 64
                        k0 = r + dil * 128 * m
                        ksel = slice(k0, k0 + dil * 127 + 1, dil)
                        q0 = r + dil * ulo
                        qsel = slice(q0, q0 + dil * (N - 1) + 1, dil)
                        psv_ = PS[bank][:, :].rearrange("p (h n) -> p h n", h=2)
                        op(PE, lambda: nc.tensor.matmul(psv_[:, :, 0:N], IDT[:, 13, :], RDXY[:, sl, :, jlo:jlo + N],
                                                        start=True, stop=False, skip_group_check=True),
                           reads=[rdxy_res[sl]], writes=[ps_res[bank]], sig=False)
                        op(PE, lambda: nc.tensor.matmul(psv_[:, :, 0:N], K1[:, sl, ksel], QXY[:, sl, :, qsel],
                                                        start=False, stop=True, skip_group_check=True),
                           reads=[k_res1[sl], q_res[sl]], writes=[ps_res[bank]], sig=True)

                    def emit_rest(i):
                        r, m, ulo, uhi = tiles[i]
                        N = uhi - ulo
                        bank = s_banks[(st["s"] + i) % 2]
                        q = cnt["pt"] % 3
                        cnt["pt"] += 1
                        ptv = PT[:, q, :].rearrange("p (h n) -> p h n", h=2)
                        psv = PS[bank][:, :].rearrange("p (h n) -> p h n", h=2)
                        op(ACT, lambda: nc.scalar.activation(out=ptv[:, :, 0:N], in_=psv[:, :, 0:N], func=AF.Exp,
                                                             scale=0.125),
                           reads=[ps_res[bank]], writes=[pt_res[q]])
                        n = (r * L + 128 * m) // 128
                        vr = v_res[sl][n // 4]
                        if L == 128:
                            pieces = [(ulo, uhi)]
                        else:
                            b1 = 128 * m + 64
                            pieces = [(a_, e_) for (a_, e_) in ((ulo, min(uhi, b1)), (max(ulo, b1), uhi)) if e_ > a_]
                        for pi, (a_, e_) in enumerate(pieces):
                            c0 = r * L + a_ - p0
                            n_ = e_ - a_
                            o_ = a_ - ulo
                            fw = c0 not in written
                            written.add(c0)
                            lastp = (pi == len(pieces) - 1)
                            op(PE, lambda: nc.tensor.matmul(PS[ob][:, c0:c0 + n_], V1[:, sl, n, 0:128],
                                                            ptv[:, 0, o_:o_ + n_], start=fw, stop=True,
                                                            skip_group_check=True),
                               reads=[vr, pt_res[q]], writes=[ps_res[ob]], sig=False)
                            op(PE, lambda: nc.tensor.matmul(PS[db][:, c0:c0 + n_], V1[:, sl, n, 64:192],
                                                            ptv[:, 1, o_:o_ + n_], start=fw, stop=True,
                                                            skip_group_check=True),
                               reads=[vr, pt_res[q]], writes=[ps_res[db]], sig=lastp)

                    emit_S(0)
                    for i in range(nt):
                        if i + 1 < nt:
                            emit_S(i + 1)
                        if prework is not None:
                            next(prework, None)
                            next(prework, None)
                        if filler is not None:
                            k_ = tctr["k"]
                            tctr["k"] += 1
                            npull = (n_ch * (k_ + 1)) // tot_tiles - (n_ch * k_) // tot_tiles
                            for _ in range(npull):
                                next(filler, None)
                        emit_rest(i)
                    st["s"] += nt
                    if prework is not None:
                        drain(prework)
                        prework = None
                    for (acc, bank) in ((OACC, ob), (DACC, db)):
                        if dil == 1:
                            dst = acc[:, 512 * B:512 * B + 512]
                            src = PS[bank][:, :]
                        elif dil == 4:
                            dst = acc.rearrange("p (u w) -> p w u", w=4)[:, B, :]
                            src = PS[bank][:, :]
                        else:
                            dst = acc.rearrange("p (u w) -> p w u", w=16)[:, 4 * B:4 * B + 4, :]
                            src = PS[bank][:, :].rearrange("p (w u) -> p w u", w=4)
                        if g == 0:
                            op(DVE, lambda dst=dst, src=src: nc.vector.tensor_copy(out=dst, in_=src),
                               reads=[ps_res[bank]], writes=[acc_all])
                        else:
                            op(DVE, lambda dst=dst, src=src: nc.vector.tensor_tensor(out=dst, in0=src, in1=dst,
                                                                                     op=ALU.add),
                               reads=[ps_res[bank], acc_all], writes=[acc_all])

            def finalize(j):
                jj = j % 2
                for tb in range(4):
                    tsl = slice(tb * 512, (tb + 1) * 512)
                    r = cnt["rstd"] % 2
                    cnt["rstd"] += 1
                    op(ACT, lambda: nc.scalar.activation(out=RSTD[0:64, r, :], in_=DACC[0:64, tsl], func=AF.Ln),
                       reads=[acc_all], writes=[rstd_res[r]])
                    op(ACT, lambda: nc.scalar.activation(out=RSTD[64:128, r, :], in_=OACC[64:128, tsl], func=AF.Ln),
                       reads=[acc_all], writes=[rstd_res[r]])
                    op(ACT, lambda: nc.scalar.activation(out=RSTD[:, r, :], in_=RSTD[:, r, :], func=AF.Exp, scale=-1.0),
                       reads=[rstd_res[r]], writes=[rstd_res[r]])
                    yield
                    bank = pj_bank()
                    op(PE, lambda: nc.tensor.matmul(PS[bank][:, :], SWAP[:, :], RSTD[:, r, :], start=True, stop=True),
                       reads=[rstd_res[r]], writes=[ps_res[bank]], sig=True)
                    op(DVE, lambda: nc.vector.tensor_tensor(out=Bh[0:64, jj, tsl], in0=PS[bank][0:64, :],
                                                            in1=OACC[0:64, tsl], op=ALU.mult),
                       reads=[acc_all, ps_res[bank]], writes=[bh_res[jj][tb]])
                    op(DVE, lambda: nc.vector.tensor_tensor(out=Bh[64:128, jj, tsl], in0=PS[bank][64:128, :],
                                                            in1=DACC[64:128, tsl], op=ALU.mult),
                       reads=[acc_all, ps_res[bank]], writes=[bh_res[jj][tb]])
                    yield

            def outproj(quarter):
                oslots = []
                for kc in range(2):
                    assert ring_keys[(ws["use"] + kc) % n_ring] == ("ob", 2 * quarter + kc), (ring_keys[(ws["use"] + kc) % n_ring], quarter)
                    oslots.append((ws["use"] + kc) % NSLOT)
                ws["use"] += 2
                for tb in range(4):
                    tsl = slice(tb * 512, (tb + 1) * 512)
                    for oc in range(8):
                        bank = pj_bank()
                        for kc in range(2):
                            s = oslots[kc]
                            op(PE, lambda kc=kc, s=s: nc.tensor.matmul(PS[bank][:, :],
                                                                       RING[:, s, oc * 128:(oc + 1) * 128],
                                                                       Bh[:, kc, tsl], start=(kc == 0), stop=(kc == 1)),
                               reads=[slot_res[s], bh_res[kc][tb]], writes=[ps_res[bank]], sig=(kc == 1))
                        resid_add(bank, oc, tb)
                        yield
                ws_pump()

            seq = [(j, g) for j in range(8) for g in range(3)]
            def chain(*gens):
                for g_ in gens:
                    for _ in g_:
                        yield

            drain(projections(seq[0][0], seq[0][1], 0))
            pend_fin = None
            pend_out = None
            for i, (j, g) in enumerate(seq):
                sl = i % 2
                fillers = []
                n_ch = 0
                if pend_out is not None:
                    fillers.append(outproj(pend_out))
                    pend_out = None
                    n_ch += 32
                if i + 1 < len(seq):
                    fillers.append(projections(seq[i + 1][0], seq[i + 1][1], (i + 1) % 2))
                    n_ch += 24
                gen = chain(*fillers)
                pre = None
                if pend_fin is not None:
                    pre = finalize(pend_fin)
                    if pend_fin % 2 == 1:
                        pend_out = pend_fin // 2
                    pend_fin = None
                attention(j, g, sl, gen, prework=pre, n_ch=n_ch)
                drain(gen)
                if g == 2:
                    pend_fin = j
            drain(finalize(7))
            gen = outproj(3)
            for tb in range(4):
                for _ in range(8):
                    next(gen)
                if next_norm is not None and tb >= 1:
                    next_norm(tb - 1)
            drain(gen)
            if next_norm is not None:
                next_norm(3)

        def make_final_norm(seq_i, scr_fence):
            YS = scr_f32(24576, 1024).rearrange("p (s t) -> p s t", s=2)
            ys_res = [Res(scr_fence) for _ in range(2)]
            yv = yout[seq_i].rearrange("p (c t) -> p c t", c=8)

            def final_norm_tb(tb):
                tsl = slice(tb * 512, (tb + 1) * 512)
                bank = pj_bank()
                for c in range(8):
                    q = cnt["sq"] % 2
                    cnt["sq"] += 1
                    op(ACT, lambda c=c, q=q: nc.scalar.activation(out=SQ[:, q, :], in_=xT[:, c, tsl], func=AF.Square),
                       reads=[xT_res[c][tb]], writes=[sq_res[q]])
                    op(PE, lambda c=c, q=q: nc.tensor.matmul(PS[bank][:, :], ONES[:, :], SQ[:, q, :], start=(c == 0),
                                                             stop=(c == 7)),
                       reads=[sq_res[q]], writes=[ps_res[bank]], sig=True)
                r = cnt["rstd"] % 2
                cnt["rstd"] += 1
                op(ACT, lambda: nc.scalar.activation(out=RSTD[:, r, :], in_=PS[bank][:, :], func=AF.Ln,
                                                     bias=EPST[:, 0:1], scale=1.0 / 1024.0),
                   reads=[ps_res[bank]], writes=[rstd_res[r]])
                op(ACT, lambda: nc.scalar.activation(out=RSTD[:, r, :], in_=RSTD[:, r, :], func=AF.Exp, scale=-0.5),
                   reads=[rstd_res[r]], writes=[rstd_res[r]])
                for c in range(8):
                    y = cnt["y"] % 2
                    cnt["y"] += 1
                    op(DVE, lambda c=c, y=y: nc.vector.scalar_tensor_tensor(
                        out=YS[:, y, :], in0=xT[:, c, tsl], scalar=GAINS[:, 32 + c:33 + c], in1=RSTD[:, r, :],
                        op0=ALU.mult, op1=ALU.mult),
                       reads=[xT_res[c][tb], rstd_res[r]], writes=[ys_res[y]])
                    dma(SP, y_sem[y], yv[:, c, tsl], YS[:, y, :], reads=[ys_res[y]])
                if seq_i + 1 < n_seq:
                    xv2 = xin[seq_i + 1].rearrange("p (c t) -> p c t", c=8)
                    dma(SP, x_sem[tb], xT[:, :, tsl], xv2[:, :, tsl], writes=[xT_res[c][tb] for c in range(8)])

            return final_norm_tb

        for si in range(n_seq):
            xv = xin[si].rearrange("p (c t) -> p c t", c=8)
            for tb in range(4):
                if si > 0:
                    break
                tsl = slice(tb * 512, (tb + 1) * 512)
                dma(SP, x_sem[tb], xT[:, :, tsl], xv[:, :, tsl], writes=[xT_res[c][tb] for c in range(8)])
            phases = []
            if 0 in layers:
                phases += ["L0", "F0"]
            if 1 in layers:
                phases += ["L1", "F1"]
            nidx = {"L0": 0, "F0": 2, "L1": 1, "F1": 3}
            pre = False
            for pi, ph in enumerate(phases):
                f_ = fence()
                if pi + 1 < len(phases):
                    nn = (lambda tb, n=nidx[phases[pi + 1]]: norm_to_A(n, tb))
                else:
                    nn = make_final_norm(si, f_)
                if ph == "L0":
                    layer0(f_, next_norm=nn)
                elif ph == "L1":
                    layer1(f_, pre_normed=pre, next_norm=nn)
                else:
                    ffn(int(ph[1]), f_, pre_normed=pre, next_norm=nn)
                pre = True
        for y in range(2):
            SP.wait(Tok(y_sem[y], y_sem[y].cnt))
        for e in (PE, ACT, DVE, POOL):
            pass
    return nc


_PROG_CACHE = {}


def kernel(x, norm_mix, norm_ffn, w_qkv_a, w_out_a, sink_a, w_qkv_b, w_out_b, w_gate, w_up, w_down, norm_final):
    x = np.asarray(x, np.float32)
    B, S, Dm = x.shape
    wts = _pack_weights(*(np.asarray(w, np.float32) for w in (w_qkv_a, w_out_a, w_qkv_b, w_out_b, w_gate, w_up, w_down)))
    gl = [np.asarray(norm_mix, np.float32)[0], np.asarray(norm_mix, np.float32)[1],
          np.asarray(norm_ffn, np.float32)[0], np.asarray(norm_ffn, np.float32)[1],
          np.asarray(norm_final, np.float32)]
    gains = np.stack([g.reshape(8, 128).T for g in gl], axis=1).reshape(128, 40).copy()
    sk = np.asarray(sink_a, np.float32)[0]
    sink = np.empty((128, 8), np.float32)
    for p in range(2):
        for a in range(4):
            sink[0:64, 4 * p + a] = sk[8 * p + a]
            sink[64:128, 4 * p + a] = sk[8 * p + 4 + a]
    xt = np.ascontiguousarray(x.reshape(B, S, 8, 128).transpose(0, 3, 2, 1)).reshape(B, 128, 8 * S)
    if "nc" not in _PROG_CACHE:
        _PROG_CACHE["nc"] = build_program()
    nc = _PROG_CACHE["nc"]
    in_maps = []
    for i in range(N_CORES):
        in_maps.append({"xin": xt[SEQ_PER_CORE * i:SEQ_PER_CORE * (i + 1)], "wts": wts, "gains": gains, "sink": sink})
    res = run_bass_kernel_spmd(nc, in_maps, core_ids=list(range(N_CORES)))
    ys = np.concatenate([np.asarray(r["yout"]) for r in res.results], axis=0)
    out = ys.reshape(B, 128, 8, S).transpose(0, 3, 2, 1).reshape(B, S, Dm)
    return np.ascontiguousarray(out.astype(np.float32))
```
